# Optimizing a Trainium2 kernel written in Bass

```python
import math
import jax
import jax.numpy as jnp
from jax import lax
import numpy as np

D_MODEL = 2048
BATCH = 32
SEQ = 256
DEPTH = 2
DEC_BATCH = 4
DEC_SEQ = 1024
PAST_LEN = 256

GRID_W = 64
NA_HEADS = 8
NA_HEAD_DIM = 128
NA_WIDTH = NA_HEADS * NA_HEAD_DIM
NA_WIN_ROWS = 8
NA_WIN_COLS = 16
NA_Q_BLOCK = 128
HY_WIDTH = D_MODEL // 4
HY_ORDER = 2
HY_BANDS = 16
HY_EMB = 1 + 2 * HY_BANDS
HY_FILTER_WIDTH = 64
GDN_HEADS = 4
GDN_DK = 128
GDN_DV = 128
GDN_WIDTH = GDN_HEADS * GDN_DV
GDN_CHUNK = 64
SHORT_CONV = 3
D_FF = 5632
N_BRANCH = 3
N_IN = 3 * NA_WIDTH + 3 * HY_WIDTH + 3 * GDN_HEADS * GDN_DK + GDN_WIDTH + 4 * GDN_HEADS + N_BRANCH * D_MODEL
NORM_EPS = 1e-6
NEG_INF = -1e30

kernel_name = 'hybrid_diffusion_na_hyena_gdn_step'


def _rmsnorm(x, g):
    xf = x.astype(jnp.float32)
    y = xf * lax.rsqrt(jnp.mean(xf * xf, axis=-1, keepdims=True) + NORM_EPS)
    return (y * g.astype(jnp.float32)).astype(x.dtype)


def _l2norm(x):
    return x * lax.rsqrt(jnp.sum(x * x, axis=-1, keepdims=True) + NORM_EPS)


def _modulate(x, g, shift, scale):
    return _rmsnorm(x, g) * (1 + scale[..., None, :]) + shift[..., None, :]


def _centred_conv(x, w, b=None):
    k = w.shape[0]
    pad = k // 2
    length = x.shape[1]
    xp = jnp.pad(x, ((0, 0), (pad, pad), (0, 0)))
    y = sum(xp[:, i:i + length] * w[i] for i in range(k))
    return y if b is None else y + b


def _context_attention(q, k, v):
    b, l, h, dh = q.shape
    nblk = l // NA_Q_BLOCK
    qb = (q * dh ** -0.5).reshape(b, nblk, NA_Q_BLOCK, h, dh).swapaxes(0, 1)

    def block(q_blk):
        s = jnp.einsum('bqhd,bkhd->bhqk', q_blk, k).astype(jnp.float32)
        p = jax.nn.softmax(s, axis=-1).astype(v.dtype)
        return jnp.einsum('bhqk,bkhd->bqhd', p, v)

    o = lax.map(block, qb)
    return o.swapaxes(0, 1).reshape(b, l, h * dh)


def _neighbourhood_attention(q, k, v, ck, cv, rpb):
    b, t, h, dh = q.shape
    rows = t // GRID_W
    wr = min(NA_WIN_ROWS, rows)
    qg = (q * dh ** -0.5).reshape(b, rows, GRID_W, h, dh)
    kg = k.reshape(b, rows, GRID_W, h, dh)
    vg = v.reshape(b, rows, GRID_W, h, dh)
    col = jnp.arange(GRID_W)
    col_start = jnp.clip(col - NA_WIN_COLS // 2, 0, GRID_W - NA_WIN_COLS)
    col_mask = (col[None, :] >= col_start[:, None]) & (col[None, :] < col_start[:, None] + NA_WIN_COLS)
    rel_col_idx = jnp.clip(col[None, :] - col[:, None] + NA_WIN_COLS - 1, 0, 2 * NA_WIN_COLS - 2)
    rpb_cols = rpb[:, :, rel_col_idx]
    n_win = wr * GRID_W

    def row_block(r):
        start = jnp.clip(r - wr // 2, 0, rows - wr)
        q_r = lax.dynamic_index_in_dim(qg, r, axis=1, keepdims=False)
        k_r = lax.dynamic_slice_in_dim(kg, start, wr, axis=1)
        v_r = lax.dynamic_slice_in_dim(vg, start, wr, axis=1)
        rel_row_idx = start + jnp.arange(wr) - r + NA_WIN_ROWS - 1
        bias = jnp.take(rpb_cols, rel_row_idx, axis=1).transpose(0, 2, 1, 3)
        s_win = jnp.einsum('bqhd,brkhd->bhqrk', q_r, k_r).astype(jnp.float32)
        s_win = jnp.where(col_mask[:, None, :], s_win + bias.astype(jnp.float32), NEG_INF)
        s_ctx = jnp.einsum('bqhd,blhd->bhql', q_r, ck).astype(jnp.float32)
        s = jnp.concatenate([s_win.reshape(b, h, GRID_W, n_win), s_ctx], axis=-1)
        p = jax.nn.softmax(s, axis=-1).astype(v.dtype)
        p_win = p[..., :n_win].reshape(b, h, GRID_W, wr, GRID_W)
        return (jnp.einsum('bhqrk,brkhd->bqhd', p_win, v_r)
                + jnp.einsum('bhql,blhd->bqhd', p[..., n_win:], cv))

    o = lax.map(row_block, jnp.arange(rows))
    return o.transpose(1, 0, 2, 3, 4).reshape(b, t, h * dh)


def _hyena_filters(length, w1, b1, freq, w2, b2, w3, decay):
    f32 = jnp.float32
    t_norm = jnp.linspace(0.0, 1.0, length, dtype=f32)
    t_idx = jnp.arange(length, dtype=f32)
    bands = jnp.linspace(1e-4, HY_BANDS - 1, HY_BANDS, dtype=f32)
    ang = (2.0 * math.pi / length) * t_idx[:, None] * bands[None, :]
    z = jnp.concatenate([t_norm[:, None], jnp.cos(ang), jnp.sin(ang)], axis=-1)
    freq = freq.astype(f32)
    hdn = jnp.sin(freq[0] * (z @ w1.astype(f32) + b1.astype(f32)))
    hdn = jnp.sin(freq[1] * (hdn @ w2.astype(f32) + b2.astype(f32)))
    filt = (hdn @ w3.astype(f32)).reshape(length, HY_ORDER, 2, HY_WIDTH)
    filt = filt * jnp.exp(-t_norm[:, None, None, None] * jnp.abs(decay.astype(f32)))
    fwd, bwd = filt[:, :, 0], filt[:, :, 1]
    return jnp.concatenate([fwd, jnp.zeros_like(fwd[:1]), bwd[:length - 1][::-1]], axis=0)


def _hyena(u, conv_w, conv_b, w1, b1, freq, w2, b2, w3, decay, skip):
    f32 = jnp.float32
    length = u.shape[1]
    n_fft = 2 * length
    uc = _centred_conv(u, conv_w, conv_b)
    v, x1, x2 = jnp.split(uc, 3, axis=-1)
    filt = _hyena_filters(length, w1, b1, freq, w2, b2, w3, decay)
    z = v.astype(f32)
    for o, gate in enumerate((x1, x2)):
        zf = jnp.fft.rfft(z, n=n_fft, axis=1)
        hf = jnp.fft.rfft(filt[:, o], n=n_fft, axis=0)
        conv = jnp.fft.irfft(zf * hf[None], n=n_fft, axis=1)[:, :length]
        z = gate.astype(f32) * (conv + z * skip[o].astype(f32))
    return z.astype(u.dtype)


def _gdn_chunk_scan(q, k, v, g, beta, s0):
    f32 = jnp.float32
    b, t, h, dk = q.shape
    dv = v.shape[-1]
    n = t // GDN_CHUNK

    def chunks(a):
        a = a.astype(f32).reshape((b, n, GDN_CHUNK, h) + a.shape[3:])
        return jnp.moveaxis(a, 3, 1)

    q = chunks(q) * dk ** -0.5
    k = chunks(k)
    v = chunks(v)
    g = chunks(g)
    beta = chunks(beta)
    gc = jnp.cumsum(g, axis=-1)
    idx = jnp.arange(GDN_CHUNK)
    causal = idx[:, None] >= idx[None, :]
    strict = idx[:, None] > idx[None, :]
    diff = gc[..., :, None] - gc[..., None, :]
    decay = jnp.where(causal, jnp.exp(jnp.where(causal, diff, 0.0)), 0.0)
    kb = k * beta[..., None]
    a_mat = (jnp.where(strict, jnp.einsum('bhnid,bhnjd->bhnij', kb, k) * decay, 0.0)
             + jnp.eye(GDN_CHUNK, dtype=f32))
    rhs = jnp.concatenate([v * beta[..., None], kb * jnp.exp(gc)[..., None]], axis=-1)
    sol = lax.linalg.triangular_solve(a_mat, rhs, left_side=True, lower=True, unit_diagonal=True)
    u, w = sol[..., :dv], sol[..., dv:]
    attn = jnp.where(causal, jnp.einsum('bhnid,bhnjd->bhnij', q, k) * decay, 0.0)
    q_dec = q * jnp.exp(gc)[..., None]
    k_dec = k * jnp.exp(gc[..., -1:] - gc)[..., None]
    g_tot = jnp.exp(gc[..., -1])
    xs = tuple(jnp.moveaxis(a, 2, 0) for a in (u, w, attn, q_dec, k_dec, g_tot))

    def step(s, inp):
        u_c, w_c, attn_c, qd_c, kd_c, gt_c = inp
        v_new = u_c - jnp.einsum('bhck,bhkv->bhcv', w_c, s)
        o_c = jnp.einsum('bhck,bhkv->bhcv', qd_c, s) + jnp.einsum('bhij,bhjv->bhiv', attn_c, v_new)
        s = s * gt_c[..., None, None] + jnp.einsum('bhck,bhcv->bhkv', kd_c, v_new)
        return s, o_c

    s_final, o = lax.scan(step, s0.astype(f32), xs)
    o = jnp.transpose(o, (1, 0, 3, 2, 4)).reshape(b, t, h, dv)
    return o, s_final


def _gated_deltanet(qkv, z, b_logit, a_logit, conv_w, a_log, dt_bias, norm_g, s_f0, s_b0):
    f32 = jnp.float32
    bsz, t, _ = qkv.shape
    qkv = jax.nn.silu(_centred_conv(qkv, conv_w))
    q, k, v = jnp.split(qkv, 3, axis=-1)
    q = _l2norm(q.reshape(bsz, t, GDN_HEADS, GDN_DK).astype(f32))
    k = _l2norm(k.reshape(bsz, t, GDN_HEADS, GDN_DK).astype(f32))
    v = v.reshape(bsz, t, GDN_HEADS, GDN_DV)
    beta = jax.nn.sigmoid(b_logit.astype(f32)).reshape(bsz, t, 2, GDN_HEADS)
    g = -jnp.exp(a_log.astype(f32)) * jax.nn.softplus(
        a_logit.astype(f32).reshape(bsz, t, 2, GDN_HEADS) + dt_bias.astype(f32))
    o_f, s_f = _gdn_chunk_scan(q, k, v, g[:, :, 0], beta[:, :, 0], s_f0)
    o_b, s_b = _gdn_chunk_scan(jnp.flip(q, 1), jnp.flip(k, 1), jnp.flip(v, 1),
                               jnp.flip(g[:, :, 1], 1), jnp.flip(beta[:, :, 1], 1), s_b0)
    o = o_f + jnp.flip(o_b, 1)
    o = _rmsnorm(o, norm_g) * jax.nn.silu(z.reshape(bsz, t, GDN_HEADS, GDN_DV).astype(f32))
    return o.reshape(bsz, t, GDN_WIDTH).astype(qkv.dtype), s_f, s_b


def _layer(x, cond, lp, ctx=None):
    b, l, _ = x.shape
    mod = jax.nn.silu(cond) @ lp['w_mod'] + lp['b_mod']
    sh1, sc1, gt1, sh2, sc2, gt2 = jnp.split(mod, 6, axis=-1)
    h = _modulate(x, lp['ln1_g'], sh1, sc1)
    proj = h @ lp['w_in']
    sizes = (NA_WIDTH, NA_WIDTH, NA_WIDTH, 3 * HY_WIDTH, 3 * GDN_HEADS * GDN_DK, GDN_WIDTH,
             2 * GDN_HEADS, 2 * GDN_HEADS, N_BRANCH * D_MODEL)
    qa, ka, va, hy_in, gdn_qkv, gdn_z, gdn_b, gdn_a, gate_logits = jnp.split(
        proj, np.cumsum(sizes)[:-1].tolist(), axis=-1)
    qa = qa.reshape(b, l, NA_HEADS, NA_HEAD_DIM)
    ka = ka.reshape(b, l, NA_HEADS, NA_HEAD_DIM)
    va = va.reshape(b, l, NA_HEADS, NA_HEAD_DIM)
    if ctx is None:
        y_na = _context_attention(qa, ka, va)
        s_f0 = jnp.zeros((b, GDN_HEADS, GDN_DK, GDN_DV), jnp.float32)
        s_b0 = s_f0
    else:
        ctx_k, ctx_v, s_f0, s_b0 = ctx
        y_na = _neighbourhood_attention(qa, ka, va, ctx_k, ctx_v, lp['na_rpb'])
    y_hy = _hyena(hy_in, lp['hy_conv_w'], lp['hy_conv_b'], lp['hy_w1'], lp['hy_b1'], lp['hy_freq'],
                  lp['hy_w2'], lp['hy_b2'], lp['hy_w3'], lp['hy_decay'], lp['hy_skip'])
    y_gdn, s_f, s_b = _gated_deltanet(gdn_qkv, gdn_z, gdn_b, gdn_a, lp['gdn_conv_w'], lp['gdn_a_log'],
                                      lp['gdn_dt_bias'], lp['gdn_norm_g'], s_f0, s_b0)
    gates = jax.nn.sigmoid((gate_logits + lp['b_gate']).astype(jnp.float32)).astype(x.dtype)
    g_na, g_hy, g_gdn = jnp.split(gates, 3, axis=-1)
    merged = g_na * (y_na @ lp['w_pa']) + g_hy * (y_hy @ lp['w_pb']) + g_gdn * (y_gdn @ lp['w_pc'])
    x = x + gt1[..., None, :] * (merged @ lp['w_out'])
    h = _modulate(x, lp['ln2_g'], sh2, sc2)
    up = _centred_conv(h @ lp['ffn_w_up'], lp['ffn_conv_w'], lp['ffn_conv_b'])
    ua, ub = jnp.split(up, 2, axis=-1)
    x = x + gt2[..., None, :] * ((jax.nn.silu(ua) * ub) @ lp['ffn_w_down'])
    return x, ka, va, s_f.astype(x.dtype), s_b.astype(x.dtype)


def setup_inputs(seed: int = 0) -> dict:
    key = jax.random.key(seed)
    keys = iter(jax.random.split(key, 48))
    f32 = jnp.float32
    d = D_MODEL

    def nrm(shape, scale):
        return jax.random.normal(next(keys), shape, f32) * scale

    def unif(shape, lo, hi):
        return jax.random.uniform(next(keys), shape, f32, lo, hi)

    dt = jnp.exp(unif((DEPTH, 2, GDN_HEADS), math.log(1e-3), math.log(1e-1)))
    return {
        'x_prompt': nrm((BATCH, SEQ, d), 1.0),
        'x_sample': nrm((DEC_BATCH, DEC_SEQ, d), 1.0),
        'cache_k': nrm((DEC_BATCH, DEPTH, PAST_LEN, NA_HEADS, NA_HEAD_DIM), 1.0),
        'cache_v': nrm((DEC_BATCH, DEPTH, PAST_LEN, NA_HEADS, NA_HEAD_DIM), 1.0),
        'state_fwd': nrm((DEC_BATCH, DEPTH, GDN_HEADS, GDN_DK, GDN_DV), 0.1),
        'state_bwd': nrm((DEC_BATCH, DEPTH, GDN_HEADS, GDN_DK, GDN_DV), 0.1),
        'c': nrm((DEC_BATCH, d), 1.0),
        'c_ctx': nrm((d,), 1.0),
        'ln1_g': 1.0 + nrm((DEPTH, d), 0.02),
        'ln2_g': 1.0 + nrm((DEPTH, d), 0.02),
        'w_mod': nrm((DEPTH, d, 6 * d), 0.5 * d ** -0.5),
        'b_mod': nrm((DEPTH, 6 * d), 0.01),
        'w_in': nrm((DEPTH, d, N_IN), d ** -0.5),
        'na_rpb': nrm((DEPTH, NA_HEADS, 2 * NA_WIN_ROWS - 1, 2 * NA_WIN_COLS - 1), 0.02),
        'hy_conv_w': nrm((DEPTH, SHORT_CONV, 3 * HY_WIDTH), 0.5),
        'hy_conv_b': nrm((DEPTH, 3 * HY_WIDTH), 0.01),
        'hy_w1': nrm((DEPTH, HY_EMB, HY_FILTER_WIDTH), HY_EMB ** -0.5),
        'hy_b1': nrm((DEPTH, HY_FILTER_WIDTH), 0.1),
        'hy_freq': 1.0 + nrm((DEPTH, 2, HY_FILTER_WIDTH), 0.02),
        'hy_w2': nrm((DEPTH, HY_FILTER_WIDTH, HY_FILTER_WIDTH), HY_FILTER_WIDTH ** -0.5),
        'hy_b2': nrm((DEPTH, HY_FILTER_WIDTH), 0.1),
        'hy_w3': nrm((DEPTH, HY_FILTER_WIDTH, HY_ORDER * 2 * HY_WIDTH), 0.1 * HY_FILTER_WIDTH ** -0.5),
        'hy_decay': unif((DEPTH, HY_ORDER, 2, HY_WIDTH), 3.07, 15.35),
        'hy_skip': nrm((DEPTH, HY_ORDER, HY_WIDTH), 0.5),
        'gdn_conv_w': nrm((DEPTH, SHORT_CONV, 3 * GDN_HEADS * GDN_DK), 0.5),
        'gdn_a_log': jnp.log(unif((DEPTH, 2, GDN_HEADS), 1.0, 16.0)),
        'gdn_dt_bias': dt + jnp.log(-jnp.expm1(-dt)),
        'gdn_norm_g': 1.0 + nrm((DEPTH, GDN_DV), 0.02),
        'w_pa': nrm((DEPTH, NA_WIDTH, d), NA_WIDTH ** -0.5),
        'w_pb': nrm((DEPTH, HY_WIDTH, d), HY_WIDTH ** -0.5),
        'w_pc': nrm((DEPTH, GDN_WIDTH, d), GDN_WIDTH ** -0.5),
        'b_gate': nrm((DEPTH, N_BRANCH * d), 0.01),
        'w_out': nrm((DEPTH, d, d), d ** -0.5),
        'ffn_w_up': nrm((DEPTH, d, 2 * D_FF), d ** -0.5),
        'ffn_conv_w': nrm((DEPTH, SHORT_CONV, 2 * D_FF), SHORT_CONV ** -0.5),
        'ffn_conv_b': nrm((DEPTH, 2 * D_FF), 0.01),
        'ffn_w_down': nrm((DEPTH, D_FF, d), D_FF ** -0.5),
        'final_g': 1.0 + nrm((d,), 0.02),
    }


def reference(x_prompt, x_sample, cache_k, cache_v, state_fwd, state_bwd, c, c_ctx,
              ln1_g, ln2_g, w_mod, b_mod, w_in, na_rpb, hy_conv_w, hy_conv_b, hy_w1, hy_b1, hy_freq,
              hy_w2, hy_b2, hy_w3, hy_decay, hy_skip, gdn_conv_w, gdn_a_log, gdn_dt_bias, gdn_norm_g,
              w_pa, w_pb, w_pc, b_gate, w_out, ffn_w_up, ffn_conv_w, ffn_conv_b, ffn_w_down, final_g):
    y_p = x_prompt
    y_s = x_sample
    ks, vs, sfs, sbs = [], [], [], []
    for i in range(DEPTH):
        lp = {
            'ln1_g': ln1_g[i], 'ln2_g': ln2_g[i], 'w_mod': w_mod[i], 'b_mod': b_mod[i], 'w_in': w_in[i],
            'na_rpb': na_rpb[i], 'hy_conv_w': hy_conv_w[i], 'hy_conv_b': hy_conv_b[i],
            'hy_w1': hy_w1[i], 'hy_b1': hy_b1[i], 'hy_freq': hy_freq[i], 'hy_w2': hy_w2[i],
            'hy_b2': hy_b2[i], 'hy_w3': hy_w3[i], 'hy_decay': hy_decay[i], 'hy_skip': hy_skip[i],
            'gdn_conv_w': gdn_conv_w[i], 'gdn_a_log': gdn_a_log[i], 'gdn_dt_bias': gdn_dt_bias[i],
            'gdn_norm_g': gdn_norm_g[i], 'w_pa': w_pa[i], 'w_pb': w_pb[i], 'w_pc': w_pc[i],
            'b_gate': b_gate[i], 'w_out': w_out[i], 'ffn_w_up': ffn_w_up[i], 'ffn_conv_w': ffn_conv_w[i],
            'ffn_conv_b': ffn_conv_b[i], 'ffn_w_down': ffn_w_down[i],
        }
        y_p, k_l, v_l, sf_l, sb_l = _layer(y_p, c_ctx, lp)
        ks.append(k_l)
        vs.append(v_l)
        sfs.append(sf_l)
        sbs.append(sb_l)
        y_s = _layer(y_s, c, lp, (cache_k[:, i], cache_v[:, i], state_fwd[:, i], state_bwd[:, i]))[0]
    y_prompt = _rmsnorm(y_p, final_g)
    y_sample = _rmsnorm(y_s, final_g)
    new_cache_k = jnp.stack(ks, axis=1)
    new_cache_v = jnp.stack(vs, axis=1)
    new_state_fwd = jnp.stack(sfs, axis=1)
    new_state_bwd = jnp.stack(sbs, axis=1)
    return (y_prompt, y_sample, new_cache_k, new_cache_v, new_state_fwd, new_state_bwd)
```

```python
import math
import numpy as np
import ml_dtypes
import concourse.bass as bass
import concourse.mybir as mybir
from concourse.bass_utils import run_bass_kernel_spmd

F32 = mybir.dt.float32
BF16 = mybir.dt.bfloat16
AF = mybir.ActivationFunctionType
ALU = mybir.AluOpType
AX = mybir.AxisListType


class Buf:
    __slots__ = ("name", "w", "r")

    def __init__(self, name):
        self.name = name
        self.w = {}
        self.r = {}


class _Eng:
    def __init__(self, key, sem):
        self.key = key
        self.sem = sem
        self.cnt = 0
        self.ops = []
        self.waited = {}


class Sched:
    ENG = ("pe", "act", "dve", "pool", "sp")

    def __init__(self, nc):
        self.nc = nc
        self.eng = {k: _Eng(k, nc.alloc_semaphore(name="s_" + k)) for k in self.ENG}
        self.dsem = {}

    def dma_sem(self, name):
        if name not in self.dsem:
            self.dsem[name] = [self.nc.alloc_semaphore(name="d_" + name), 0]
        return name

    def _wait(self, e, sid, sem, val):
        if e.waited.get(sid, 0) >= val:
            return
        e.waited[sid] = val
        e.ops.append(lambda h, sem=sem, val=val: h.wait_ge(sem, val))

    def op(self, ekey, fn, r=(), w=(), dma=None):
        e = self.eng[ekey]
        deps = {}

        def add(d, raw):
            for sid, (sem, val) in d.items():
                if sid == e.key and dma is None:
                    if e.key == "pe" or not raw:
                        continue
                if val > deps.get(sid, (None, 0))[1]:
                    deps[sid] = (sem, val)

        for b in r:
            add(b.w, True)
        for b in w:
            add(b.w, False)
            add(b.r, False)
        for sid, (sem, val) in deps.items():
            self._wait(e, sid, sem, val)
        if dma is None:
            e.cnt += 1
            tok = (e.key, e.sem, e.cnt)
            sem, inc = e.sem, 1
        else:
            ds = self.dsem[dma]
            ds[1] += 16
            tok = ("d_" + dma, ds[0], ds[1])
            sem, inc = ds[0], 16
        e.ops.append(lambda h, fn=fn, sem=sem, inc=inc: fn(h).then_inc(sem, inc))
        for b in r:
            b.r[tok[0]] = (tok[1], tok[2])
        for b in w:
            b.w = {tok[0]: (tok[1], tok[2])}
            b.r = {}
        return tok

    def barrier(self):
        for e in self.eng.values():
            for o in self.eng.values():
                if o is not e and o.cnt > 0:
                    self._wait(e, o.key, o.sem, o.cnt)
            for name, (sem, val) in self.dsem.items():
                if val > 0:
                    self._wait(e, "d_" + name, sem, val)

    def final_wait(self, ekey="sp"):
        e = self.eng[ekey]
        for o in self.eng.values():
            if o is not e and o.cnt > 0:
                self._wait(e, o.key, o.sem, o.cnt)
        for name, (sem, val) in self.dsem.items():
            if val > 0:
                self._wait(e, "d_" + name, sem, val)

    def emit(self):
        nc = self.nc
        with nc.Block() as block:
            @block.tensor
            def _(h):
                for f in self.eng["pe"].ops:
                    f(h)

            @block.scalar
            def _(h):
                for f in self.eng["act"].ops:
                    f(h)

            @block.vector
            def _(h):
                for f in self.eng["dve"].ops:
                    f(h)

            @block.gpsimd
            def _(h):
                for f in self.eng["pool"].ops:
                    f(h)

            @block.sync
            def _(h):
                for f in self.eng["sp"].ops:
                    f(h)


D = 2048
T = 1024
KC = 16
NLAYER = 2
DH = 128
D_FF = 5632
N_IN = 12816
EPS = 1e-6
TQ, TK, TV, THY, TGQ, TGZ, TBA, TGATE = 0, 8, 16, 24, 36, 48, 52, 53
NWIN = 101
V_BMOD, V_LN1, V_LN2, V_BGATE, V_HCW, V_HCB, V_HSKIP, V_GCW, V_GNG, V_FCW, V_FCB, V_FING, NV = (
    0, 96, 112, 128, 176, 212, 224, 232, 268, 272, 536, 624, 640)
KS_TOK = [0, 0, 0, 128, 256, 384, 384, 384]
NSLOT = 4
BFNP = ml_dtypes.bfloat16


def _tile_w(w):
    K, N = w.shape
    return np.ascontiguousarray(w.reshape(K // 128, 128, N // 128, 128).transpose(2, 1, 0, 3)).reshape(N // 128, 128, K)


def _fm(v):
    return np.ascontiguousarray(v.reshape(-1, 128).T)


def _dft_consts(L):
    N = 2 * L
    k = np.arange(L, dtype=np.float64)
    m = np.arange(L, dtype=np.float64)
    w = 2 * np.pi * (k + 0.5) / N
    ang = np.outer(m, w)
    Cm, Sm = np.cos(ang), np.sin(ang)
    ang1 = np.outer(m + 1, w)
    Cm1, Sm1 = np.cos(ang1), np.sin(ang1)
    Cm1[L - 1] = 0
    Sm1[L - 1] = 0
    n = L // 128

    def tf(M):
        return M.reshape(n, 128, n, 128).transpose(2, 1, 0, 3)

    F4 = np.stack([tf(Cm), tf(Cm1), tf(-Sm), tf(Sm1)], axis=2)
    F2 = np.stack([tf(Cm), tf(Sm)], axis=2)
    CT, ST = Cm.T / L, Sm.T / L
    I2 = np.stack([CT.reshape(n, 128, L), ST.reshape(n, 128, L)], axis=2)
    c = lambda a: np.ascontiguousarray(a.astype(np.float32).astype(BFNP))
    return c(F4), c(F2), c(I2)


def _zfeat(L):
    f32 = np.float32
    t_norm = np.linspace(0.0, 1.0, L, dtype=f32)
    t_idx = np.arange(L, dtype=f32)
    bands = np.linspace(1e-4, 15.0, 16, dtype=f32)
    ang = (f32(2.0 * math.pi / L) * t_idx[:, None] * bands[None, :]).astype(f32)
    z = np.concatenate([t_norm[:, None], np.cos(ang), np.sin(ang)], axis=-1).astype(f32)
    tn = np.ascontiguousarray((-t_norm).reshape(L // 128, 128).T)
    return np.ascontiguousarray(z.T), tn


def _gdn_masks():
    p = np.arange(64)[:, None]
    f = np.arange(64)[None, :]
    ms = [(p <= f), (p >= f), (p > f), (p < f)]
    out = np.stack([np.tile(m.astype(np.float32)[:, None, :], (1, 8, 1)) for m in ms], axis=0)
    return np.ascontiguousarray(out)


def _bias_mask(rpb):
    out = np.full((8, 8, 128, 640), -1e30, np.float32)
    p = np.arange(128)
    cq = p % 64
    cs = np.clip(cq - 8, 0, 48)
    for qt in range(8):
        r = 2 * qt + p // 64
        st = np.clip(r - 4, 0, 8)
        ktok = KS_TOK[qt] + np.arange(640)
        kr = ktok // 64
        ck = ktok % 64
        ok = ((kr[None, :] >= st[:, None]) & (kr[None, :] < st[:, None] + 8)
              & (ck[None, :] >= cs[:, None]) & (ck[None, :] < cs[:, None] + 16))
        ri = np.clip(kr[None, :] - r[:, None] + 7, 0, 14)
        ci = np.clip(ck[None, :] - cq[:, None] + 15, 0, 30)
        for h in range(8):
            vals = rpb[h][ri, ci]
            out[h, qt] = np.where(ok, vals, np.float32(-1e30))
    return out


def prep_shared(inp):
    sh = {}
    L2 = NLAYER
    w_in = inp["w_in"]
    win_t = np.zeros((L2, NWIN, 128, D), np.float32)
    wmod_t = np.zeros((L2, 96, 128, D), np.float32)
    wp_t = np.zeros((L2, 16, 128, D), np.float32)
    wout_t = np.zeros((L2, 16, 128, D), np.float32)
    wup_t = np.zeros((L2, 88, 128, D), np.float32)
    wdn_t = np.zeros((L2, 4, 16, 128, 11 * 128), np.float32)
    vecs = np.zeros((128, L2, NV), np.float32)
    for l in range(L2):
        w = w_in[l]
        win_t[l, 0:52] = _tile_w(w[:, 0:6656])
        ba = np.zeros((D, 128), np.float32)
        ba[:, 0:8] = w[:, 6656:6664]
        ba[:, 32:40] = w[:, 6664:6672]
        win_t[l, 52] = _tile_w(ba)[0]
        win_t[l, 53:101] = _tile_w(w[:, 6672:12816])
        wmod_t[l] = _tile_w(inp["w_mod"][l])
        wp_t[l] = _tile_w(np.concatenate([inp["w_pa"][l], inp["w_pb"][l], inp["w_pc"][l]], axis=0))
        wout_t[l] = _tile_w(inp["w_out"][l])
        wup_t[l] = _tile_w(inp["ffn_w_up"][l])
        for q in range(4):
            wdn_t[l, q] = _tile_w(inp["ffn_w_down"][l][q * 1408:(q + 1) * 1408])
        v = vecs[:, l]
        v[:, V_BMOD:V_BMOD + 96] = _fm(inp["b_mod"][l])
        v[:, V_LN1:V_LN1 + 16] = _fm(inp["ln1_g"][l])
        v[:, V_LN2:V_LN2 + 16] = _fm(inp["ln2_g"][l])
        v[:, V_BGATE:V_BGATE + 48] = _fm(inp["b_gate"][l])
        for tap in range(3):
            v[:, V_HCW + tap * 12:V_HCW + tap * 12 + 12] = _fm(inp["hy_conv_w"][l][tap])
            v[:, V_GCW + tap * 12:V_GCW + tap * 12 + 12] = _fm(inp["gdn_conv_w"][l][tap])
            v[:, V_FCW + tap * 88:V_FCW + tap * 88 + 88] = _fm(inp["ffn_conv_w"][l][tap])
        v[:, V_HCB:V_HCB + 12] = _fm(inp["hy_conv_b"][l])
        v[:, V_HSKIP:V_HSKIP + 8] = _fm(inp["hy_skip"][l].reshape(-1))
        v[:, V_GNG:V_GNG + 1] = _fm(inp["gdn_norm_g"][l])
        v[:, V_FCB:V_FCB + 88] = _fm(inp["ffn_conv_b"][l])
        v[:, V_FING:V_FING + 16] = _fm(inp["final_g"])
    sh.update(win_t=win_t, wmod_t=wmod_t, wp_t=wp_t, wout_t=wout_t, wup_t=wup_t, wdn_t=wdn_t, vecs=vecs)
    sh["bm"] = np.stack([_bias_mask(inp["na_rpb"][l]) for l in range(L2)], axis=0)
    sh["hw1"] = np.ascontiguousarray(inp["hy_w1"])
    sh["hw2"] = np.ascontiguousarray(inp["hy_w2"])
    sh["hw3"] = np.ascontiguousarray(inp["hy_w3"])
    hcol = np.zeros((L2, 64, 4), np.float32)
    hcol[:, :, 0] = inp["hy_b1"]
    hcol[:, :, 1] = inp["hy_b2"]
    hcol[:, :, 2] = inp["hy_freq"][:, 0]
    hcol[:, :, 3] = inp["hy_freq"][:, 1]
    sh["hcol"] = hcol
    sh["hdec"] = np.ascontiguousarray(inp["hy_decay"].reshape(L2, 2048))
    gcol = np.zeros((L2, 40, 2), np.float32)
    gcol[:, 32:40, 0] = inp["gdn_a_log"].reshape(L2, 8)
    gcol[:, 32:40, 1] = inp["gdn_dt_bias"].reshape(L2, 8)
    sh["gcol"] = gcol
    for L in (256, 1024):
        F4, F2, I2 = _dft_consts(L)
        zfT, tn = _zfeat(L)
        sh[f"F4_{L}"], sh[f"F2_{L}"], sh[f"I2_{L}"], sh[f"zfT_{L}"], sh[f"tn_{L}"] = F4, F2, I2, zfT, tn
    sh["ident"] = np.eye(128, dtype=np.float32)
    sh["gmask"] = _gdn_masks()
    return sh


def prep_core(inp, core):
    pc = {}
    xp = inp["x_prompt"][4 * core:4 * core + 4].reshape(T, D)
    sb = core % 4
    xs = inp["x_sample"][sb]
    pc["xT"] = np.ascontiguousarray(np.stack([xp.T, xs.T], axis=0))
    cv = np.stack([inp["c_ctx"], inp["c"][sb]], axis=-1)
    pc["cvec"] = np.ascontiguousarray(cv.reshape(16, 128, 2).transpose(1, 0, 2))
    pc["ckT"] = np.ascontiguousarray(inp["cache_k"][sb].transpose(0, 2, 3, 1))
    pc["cv"] = np.ascontiguousarray(inp["cache_v"][sb].reshape(NLAYER, 256, 1024))
    pc["sf0"] = np.ascontiguousarray(inp["state_fwd"][sb])
    pc["sb0"] = np.ascontiguousarray(inp["state_bwd"][sb])
    return pc


class Region:
    def __init__(self, tensor, ncols):
        self.t = tensor
        self.ncols = ncols
        self.off = 0

    def reset(self):
        self.off = 0

    def alloc(self, shape, dtype=F32, parts=128):
        n = 1
        for s in shape:
            n *= s
        nb = n * (4 if dtype == F32 else 2)
        nb = (nb + 31) // 32 * 32
        c0 = self.off // 4
        c1 = (self.off + nb) // 4
        assert c1 <= self.ncols, ("region overflow", self.off, nb, self.ncols * 4)
        self.off += nb
        ap = self.t[0:parts, c0:c1]
        if dtype != F32:
            ap = ap.bitcast(dtype)
        ap = ap[:, 0:n]
        if len(shape) == 2:
            ap = ap.rearrange("p (a b) -> p a b", a=shape[0])
        elif len(shape) == 3:
            ap = ap.rearrange("p (a b c) -> p a b c", a=shape[0], b=shape[1])
        return ap


def build(cfg=None):
    cfg = cfg or {}
    layers = cfg.get("layers", [0, 1])
    groups = cfg.get("groups", [0, 1])
    phases = cfg.get("phases", ["attn", "hy", "gdn", "merge", "ffn"])
    taps = cfg.get("taps", [])
    PL = cfg.get("pl", "pool")
    PLG = cfg.get("pl_gdn", "dve")
    nc = bass.Bass("TRN2", target_bir_lowering=False)
    S = Sched(nc)
    Dd = {}

    def din(name, shape, dt=F32):
        Dd[name] = nc.dram_tensor(name, list(shape), dt, kind="ExternalInput").ap()
        return Dd[name]

    def dout(name, shape, dt=F32):
        Dd[name] = nc.dram_tensor(name, list(shape), dt, kind="ExternalOutput").ap()
        return Dd[name]

    def dscr(name, shape, dt=F32):
        Dd[name] = nc.dram_tensor(name, list(shape), dt, kind="Internal").ap()
        return Dd[name]

    xT_d = din("xT", [2, D, T])
    cvec_d = din("cvec", [128, 16, 2])
    ckT_d = din("ckT", [NLAYER, 8, 128, 256])
    cv_d = din("cv", [NLAYER, 256, 1024])
    sf0_d = din("sf0", [NLAYER, 4, 128, 128])
    sb0_d = din("sb0", [NLAYER, 4, 128, 128])
    win_d = din("win_t", [NLAYER, NWIN, 128, D])
    wmod_d = din("wmod_t", [NLAYER, 96, 128, D])
    wp_d = din("wp_t", [NLAYER, 16, 128, D])
    wout_d = din("wout_t", [NLAYER, 16, 128, D])
    wup_d = din("wup_t", [NLAYER, 88, 128, D])
    wdn_d = din("wdn_t", [NLAYER, 4, 16, 128, 11 * 128])
    vecs_d = din("vecs", [128, NLAYER, NV])
    bm_d = din("bm", [NLAYER, 8, 8, 128, 640])
    hw1_d = din("hw1", [NLAYER, 33, 64])
    hw2_d = din("hw2", [NLAYER, 64, 64])
    hw3_d = din("hw3", [NLAYER, 64, 2048])
    hcol_d = din("hcol", [NLAYER, 64, 4])
    hdec_d = din("hdec", [NLAYER, 2048])
    gcol_d = din("gcol", [NLAYER, 40, 2])
    F4_d, F2_d, I2_d, zfT_d, tn_d = {}, {}, {}, {}, {}
    for L in (256, 1024):
        n = L // 128
        F4_d[L] = din(f"F4_{L}", [n, 128, 4, n, 128], BF16)
        F2_d[L] = din(f"F2_{L}", [n, 128, 2, n, 128], BF16)
        I2_d[L] = din(f"I2_{L}", [n, 128, 2, L], BF16)
        zfT_d[L] = din(f"zfT_{L}", [33, L])
        tn_d[L] = din(f"tn_{L}", [128, n])
    ident_d = din("ident", [128, 128])
    gmask_d = din("gmask", [4, 64, 8, 64])
    yT_d = dout("yT", [2, D, T])
    kT_o = dout("kT_out", [NLAYER, 1024, T])
    vT_o = dout("vT_out", [NLAYER, 1024, T])
    sf_o = dout("sf_out", [NLAYER, 4, 4, 128, 128])
    sb_o = dout("sb_out", [NLAYER, 4, 4, 128, 128])
    xscr = dscr("xscr", [D, T])
    hspec = dscr("hspec", [8, 128, 2, 1024])

    big = nc.alloc_sbuf_tensor("big", [128, 16384], F32)
    xT = big[:, :].rearrange("p (k t) -> p k t", k=16)
    hT = nc.alloc_sbuf_tensor("hT", [128, 16, T], BF16)
    ymix = nc.alloc_sbuf_tensor("ymix", [128, 16, T], BF16)
    wring = [nc.alloc_sbuf_tensor(f"wr{i}", [128, 16, 128], BF16) for i in range(NSLOT)]
    vecs = nc.alloc_sbuf_tensor("vecs_s", [128, NLAYER, NV], F32)
    modv = nc.alloc_sbuf_tensor("modv", [128, NLAYER, 96, 2], F32)
    modA = nc.alloc_sbuf_tensor("modA", [128, NLAYER, 2, 2, 16], F32)
    cvec = nc.alloc_sbuf_tensor("cvec_s", [128, 16, 2], F32)
    csil = nc.alloc_sbuf_tensor("csil", [128, 16, 2], BF16)
    ident = nc.alloc_sbuf_tensor("ident_s", [128, 128], F32)
    ones_bf = nc.alloc_sbuf_tensor("ones_bf", [128, 128], BF16)
    ones32 = nc.alloc_sbuf_tensor("ones32", [128, 128], F32)
    gmask = nc.alloc_sbuf_tensor("gmask_s", [64, 4, 8, 64], F32)
    EXC = 11264
    ex_t = nc.alloc_sbuf_tensor("ex", [128, EXC], F32)
    RB = Region(big, 16384)
    RX = Region(ex_t, EXC)

    def AA(shape, dtype=F32, parts=128):
        n = 1
        for s_ in shape:
            n *= s_
        nb = (n * (4 if dtype == F32 else 2) + 31) // 32 * 32
        reg = RB if RB.off + nb <= RB.ncols * 4 else RX
        return reg.alloc(shape, dtype, parts)

    PS = [nc.alloc_psum_tensor(f"ps{i}", [128, 512], F32) for i in range(8)]
    PSB = [Buf(f"ps{i}") for i in range(8)]
    st = {"ps": 0, "w": 0, "wi": 0}

    Bx, Bh, By, Bvec, Bmod, Bcs, Bconst, Bxscr, Bhspec = (Buf(n) for n in
                                                           ("x", "h", "ymix", "vecs", "mod", "csil", "const", "xscr", "hspec"))
    wB = [Buf(f"w{i}") for i in range(NSLOT)]
    for i in range(NSLOT):
        S.dma_sem(f"w{i}")

    def psn():
        i = st["ps"] % 8
        st["ps"] += 1
        return i

    def mm(out, lhsT, rhs, start, stop, r, w):
        S.op("pe", lambda h: h.matmul(out, lhsT, rhs, start=start, stop=stop), r=r, w=w)

    def tr(out, in_, idn, r, w):
        S.op("pe", lambda h: h.transpose(out, in_, idn), r=r + [Bconst], w=w)

    def act(out, in_, func, r, w, bias=None, scale=None):
        kw = {}
        if bias is not None:
            kw["bias"] = bias
        if scale is not None:
            kw["scale"] = scale
        S.op("act", lambda h: h.activation(out=out, in_=in_, func=func, **kw), r=r, w=w)

    def tt(eng, out, in0, in1, op, r, w):
        S.op(eng, lambda h: h.tensor_tensor(out=out, in0=in0, in1=in1, op=op), r=r, w=w)

    def ts(eng, out, in0, s1, s2, op0, op1, r, w):
        if op1 is None:
            S.op(eng, lambda h: h.tensor_scalar(out=out, in0=in0, scalar1=s1, scalar2=None, op0=op0), r=r, w=w)
        else:
            S.op(eng, lambda h: h.tensor_scalar(out=out, in0=in0, scalar1=s1, scalar2=s2, op0=op0, op1=op1), r=r, w=w)

    def stt(eng, out, in0, scalar, in1, op0, op1, r, w):
        S.op(eng, lambda h: h.scalar_tensor_tensor(out=out, in0=in0, scalar=scalar, in1=in1, op0=op0, op1=op1), r=r, w=w)

    def cp(eng, out, in_, r, w):
        if eng == "act":
            act(out, in_, AF.Copy, r, w)
        else:
            S.op(eng, lambda h: h.tensor_copy(out=out, in_=in_), r=r, w=w)

    def red(eng, out, in_, op, r, w):
        S.op(eng, lambda h: h.tensor_reduce(out=out, in_=in_, axis=AX.X, op=op), r=r, w=w)

    def recip(eng, out, in_, r, w):
        S.op(eng, lambda h: h.reciprocal(out=out, in_=in_), r=r, w=w)

    def memset(eng, ap, val, w):
        S.op(eng, lambda h: h.memset(ap, val), w=w)

    def dma(eng, out, in_, r, w, sem):
        S.dma_sem(sem)
        S.op(eng, lambda h: h.dma_start(out=out, in_=in_), r=r, w=w, dma=sem)

    wplan = cfg.get("_wplan")
    wkeys = []
    LOOKAHEAD = NSLOT - 1

    def _wsrc(key):
        ap = Dd[key[0]]
        for ix in key[1:]:
            ap = ap[ix]
        return ap

    def _wissue(i, key, kct):
        t = wring[i % NSLOT]
        dma("pool", t[:, 0:kct, :], _wsrc(key).rearrange("p (k n) -> p k n", n=128), [], [wB[i % NSLOT]], f"w{i % NSLOT}")

    def wget(key, kct=16):
        i = st["w"]
        st["w"] += 1
        wkeys.append((key, kct))
        if wplan is None:
            _wissue(i, key, kct)
        else:
            assert wplan[i] == (key, kct), (i, wplan[i], key, kct)
            while st["wi"] <= min(i + LOOKAHEAD, len(wplan) - 1):
                k2, c2 = wplan[st["wi"]]
                _wissue(st["wi"], k2, c2)
                st["wi"] += 1
        return wring[i % NSLOT], wB[i % NSLOT]

    def proj(wt, wb, rhs_t, rhs_b, kcs=range(16)):
        kcs = list(kcs)
        pis = []
        for tb in range(2):
            pi = psn()
            for n_, kc in enumerate(kcs):
                mm(PS[pi][:, :], wt[:, kc, :], rhs_t[:, kc, tb * 512:(tb + 1) * 512], n_ == 0, n_ == len(kcs) - 1,
                   [wb, rhs_b], [PSB[pi]])
            pis.append(pi)
        return pis

    def tap(name, src_ap, shape, dt, rb):
        if name in taps:
            d = dout("tap_" + name, shape, dt)
            dma("sp", d, src_ap, rb, [], "tap")

    dma("sp", vecs[:, :, :], vecs_d, [], [Bvec], "c0")
    dma("sp", cvec[:, :, :], cvec_d, [], [Bcs], "c1")
    dma("sp", ident[:, :], ident_d, [], [Bconst], "c2")
    dma("sp", gmask[:, :, :, :], gmask_d.rearrange("m p e f -> p m e f"), [], [Bconst], "c2")
    memset("dve", ones_bf[:, :], 1.0, [Bconst])
    memset("dve", ones32[:, :], 1.0, [Bconst])
    act(csil[:, :, :], cvec[:, :, :], AF.Silu, [Bcs], [Bcs])
    if cfg.get("zero_ymix"):
        memset(PL, ymix[:, :, :], 0.0, [By])

    for l in layers:
        pi = psn()
        for j in range(96):
            wt, wb = wget(("wmod_t", l, j))
            for kc in range(16):
                mm(PS[pi][:, 2 * j:2 * j + 2], wt[:, kc, :], csil[:, kc, :], kc == 0, kc == 15, [wb, Bcs], [PSB[pi]])
        psv = PS[pi][:, 0:192].rearrange("p (j g) -> p j g", g=2)
        for g in range(2):
            tt("dve", modv[:, l, :, g], psv[:, :, g], vecs[:, l, V_BMOD:V_BMOD + 96], ALU.add, [PSB[pi], Bvec], [Bmod])
        for g in range(2):
            stt("dve", modA[:, l, 0, g, :], modv[:, l, 16:32, g], 1.0, vecs[:, l, V_LN1:V_LN1 + 16], ALU.add, ALU.mult,
                [Bmod, Bvec], [Bmod])
            stt("dve", modA[:, l, 1, g, :], modv[:, l, 64:80, g], 1.0, vecs[:, l, V_LN2:V_LN2 + 16], ALU.add, ALU.mult,
                [Bmod, Bvec], [Bmod])

    def phase_norm(l, g, which):
        RX.reset()
        sq = [RX.alloc([512], BF16) for _ in range(2)]
        Bsq = [Buf("sq0"), Buf("sq1")]
        tmp = [RX.alloc([512]) for _ in range(2)]
        Btmp = [Buf("tmp0"), Buf("tmp1")]
        rs = RX.alloc([512])
        Brs = Buf("rs")
        shift0 = 0 if which == 0 else 48
        for tb in range(2):
            sl = slice(tb * 512, (tb + 1) * 512)
            pi = psn()
            for kc in range(16):
                act(sq[kc % 2], xT[:, kc, sl], AF.Square, [Bx], [Bsq[kc % 2]])
                mm(PS[pi][:, :], ones_bf[:, :], sq[kc % 2], kc == 0, kc == 15, [Bsq[kc % 2], Bconst], [PSB[pi]])
            act(rs, PS[pi][:, :], AF.Sqrt, [PSB[pi]], [Brs], bias=EPS, scale=1.0 / D)
            recip("dve", rs, rs, [Brs], [Brs])
            for kc in range(16):
                tt("dve", tmp[kc % 2], xT[:, kc, sl], rs, ALU.mult, [Bx, Brs], [Btmp[kc % 2]])
                act(hT[:, kc, sl], tmp[kc % 2], AF.Identity, [Btmp[kc % 2], Bmod], [Bh],
                    bias=modv[:, l, shift0 + kc:shift0 + kc + 1, g], scale=modA[:, l, which, g, kc:kc + 1])
        S.barrier()

    def phase_final(g):
        RX.reset()
        sq = [RX.alloc([512], BF16) for _ in range(2)]
        Bsq = [Buf("fsq0"), Buf("fsq1")]
        tmp = [RX.alloc([512]) for _ in range(2)]
        Btmp = [Buf("ftmp0"), Buf("ftmp1")]
        o32 = [RX.alloc([512]) for _ in range(2)]
        Bo = [Buf("fo0"), Buf("fo1")]
        rs = RX.alloc([512])
        Brs = Buf("frs")
        for tb in range(2):
            sl = slice(tb * 512, (tb + 1) * 512)
            pi = psn()
            for kc in range(16):
                act(sq[kc % 2], xT[:, kc, sl], AF.Square, [Bx], [Bsq[kc % 2]])
                mm(PS[pi][:, :], ones_bf[:, :], sq[kc % 2], kc == 0, kc == 15, [Bsq[kc % 2], Bconst], [PSB[pi]])
            act(rs, PS[pi][:, :], AF.Sqrt, [PSB[pi]], [Brs], bias=EPS, scale=1.0 / D)
            recip("dve", rs, rs, [Brs], [Brs])
            for kc in range(16):
                tt("dve", tmp[kc % 2], xT[:, kc, sl], rs, ALU.mult, [Bx, Brs], [Btmp[kc % 2]])
                act(o32[kc % 2], tmp[kc % 2], AF.Identity, [Btmp[kc % 2], Bvec], [Bo[kc % 2]],
                    scale=vecs[:, 0, V_FING + kc:V_FING + kc + 1])
                dma("sp", yT_d[g, kc * 128:(kc + 1) * 128, sl], o32[kc % 2], [Bo[kc % 2]], [], f"yo{kc % 2}")
        S.barrier()

    def phase_attn(l, g):
        RB.reset()
        RX.reset()
        qT = RB.alloc([T], BF16)
        kTb = RB.alloc([T], BF16)
        k32 = RB.alloc([T])
        v32 = RB.alloc([T])
        vtok = RB.alloc([8, 128], BF16)
        BqT, BkTb, Bk32, Bv32, Bvtok = (Buf(n) for n in ("qT", "kTb", "k32", "v32", "vtok"))
        NW = 3
        if g == 0:
            P32 = [RB.alloc([256]) for _ in range(NW)]
            PT = [RB.alloc([2, 128], BF16) for _ in range(NW)]
        else:
            P32 = [RB.alloc([896]) for _ in range(NW)]
            PT = [RB.alloc([7, 128], BF16) for _ in range(NW)]
            bmt = [RB.alloc([640]) for _ in range(2)]
            Bbm = [Buf("bm0"), Buf("bm1")]
            ckTb = RB.alloc([256], BF16)
            cvb = RB.alloc([2, 128], BF16)
            Bck, Bcvb = Buf("ckTb"), Buf("cvb")
        sm = [RB.alloc([4]) for _ in range(NW)]
        BP = [Buf(f"P{i}") for i in range(NW)]
        BPT = [Buf(f"PT{i}") for i in range(NW)]
        Bsm = [Buf(f"sm{i}") for i in range(NW)]
        inst = 0
        for h in range(cfg.get("nheads", 8)):
            if cfg.get("attn_stop", 9) <= 0:
                continue
            wt, wb = wget(("win_t", l, TQ + h))
            pis = proj(wt, wb, hT, Bh)
            for tb in range(2):
                act(qT[:, tb * 512:(tb + 1) * 512], PS[pis[tb]][:, :], AF.Identity, [PSB[pis[tb]]], [BqT], scale=DH ** -0.5)
            if cfg.get("parts", 9) <= 1:
                continue
            wt, wb = wget(("win_t", l, TK + h))
            pis = proj(wt, wb, hT, Bh)
            for tb in range(2):
                sl = slice(tb * 512, (tb + 1) * 512)
                if g == 0:
                    act(k32[:, sl], PS[pis[tb]][:, :], AF.Copy, [PSB[pis[tb]]], [Bk32])
                    cp("dve", kTb[:, sl], k32[:, sl], [Bk32], [BkTb])
                else:
                    cp("dve", kTb[:, sl], PS[pis[tb]][:, :], [PSB[pis[tb]]], [BkTb])
            if g == 0 and not cfg.get("no_kvout"):
                dma("sp", kT_o[l, h * 128:(h + 1) * 128, :], k32, [Bk32], [], "ko")
            if cfg.get("parts", 9) <= 2:
                continue
            wt, wb = wget(("win_t", l, TV + h))
            pis = proj(wt, wb, hT, Bh)
            for tb in range(2):
                act(v32[:, tb * 512:(tb + 1) * 512], PS[pis[tb]][:, :], AF.Copy, [PSB[pis[tb]]], [Bv32])
            if g == 0 and not cfg.get("no_kvout"):
                dma("sp", vT_o[l, h * 128:(h + 1) * 128, :], v32, [Bv32], [], "vo")
            if cfg.get("attn_stop", 9) <= 1:
                continue
            for half in range(2):
                pi = psn()
                for j in range(4):
                    t_ = half * 4 + j
                    tr(PS[pi][:, j * 128:(j + 1) * 128], v32[:, t_ * 128:(t_ + 1) * 128], ident[:, :], [Bv32], [PSB[pi]])
                cp("dve", vtok[:, half * 4:(half + 1) * 4, :], PS[pi][:, :].rearrange("p (a b) -> p a b", a=4),
                   [PSB[pi]], [Bvtok])
            if g == 1:
                dma("pool", ckTb, ckT_d[l, h], [], [Bck], "ck")
                dma("pool", cvb, cv_d[l, :, h * 128:(h + 1) * 128].rearrange("(c p) d -> p c d", p=128), [], [Bcvb], "cvb")
            if cfg.get("attn_stop", 9) <= 2:
                continue
            nq = 8
            for qi in range(nq):
                w_ = inst % NW
                inst += 1
                q0 = qi * 128
                if g == 0:
                    b = qi // 2
                    nk = 256
                    pi = psn()
                    mm(PS[pi][:, 0:256], qT[:, q0:q0 + 128], kTb[:, b * 256:(b + 1) * 256], True, True,
                       [BqT, BkTb], [PSB[pi]])
                    src = PS[pi][:, 0:256]
                    srcb = [PSB[pi]]
                else:
                    nk = 896
                    ks = KS_TOK[qi]
                    bi = (h * 8 + qi) % 2
                    dma("sp", bmt[bi], bm_d[l, h, qi], [], [Bbm[bi]], f"bm{bi}")
                    pa = psn()
                    mm(PS[pa][:, :], qT[:, q0:q0 + 128], kTb[:, ks:ks + 512], True, True, [BqT, BkTb], [PSB[pa]])
                    pb = psn()
                    mm(PS[pb][:, 0:128], qT[:, q0:q0 + 128], kTb[:, ks + 512:ks + 640], True, True, [BqT, BkTb], [PSB[pb]])
                    mm(PS[pb][:, 128:384], qT[:, q0:q0 + 128], ckTb, True, True, [BqT, Bck], [PSB[pb]])
                    tt("dve", P32[w_][:, 0:512], PS[pa][:, :], bmt[bi][:, 0:512], ALU.add, [PSB[pa], Bbm[bi]], [BP[w_]])
                    tt("dve", P32[w_][:, 512:640], PS[pb][:, 0:128], bmt[bi][:, 512:640], ALU.add, [PSB[pb], Bbm[bi]], [BP[w_]])
                    cp("dve", P32[w_][:, 640:896], PS[pb][:, 128:384], [PSB[pb]], [BP[w_]])
                    src = P32[w_]
                    srcb = [BP[w_]]
                red("dve", sm[w_][:, 0:1], src, ALU.max, srcb, [Bsm[w_]])
                ts("dve", sm[w_][:, 1:2], sm[w_][:, 0:1], -1.0, None, ALU.mult, None, [Bsm[w_]], [Bsm[w_]])
                act(P32[w_], src, AF.Exp, srcb + [Bsm[w_]], [BP[w_]], bias=sm[w_][:, 1:2])
                if cfg.get("attn_stop", 9) <= 3:
                    continue
                red("dve", sm[w_][:, 2:3], P32[w_], ALU.add, [BP[w_]], [Bsm[w_]])
                recip("dve", sm[w_][:, 3:4], sm[w_][:, 2:3], [Bsm[w_]], [Bsm[w_]])
                ts(PL, P32[w_], P32[w_], sm[w_][:, 3:4], None, ALU.mult, None, [BP[w_], Bsm[w_]], [BP[w_]])
                if cfg.get("attn_stop", 9) <= 4:
                    continue
                nchunk = nk // 128
                c = 0
                while c < nchunk:
                    n_ = min(4, nchunk - c)
                    pi2 = psn()
                    for j in range(n_):
                        tr(PS[pi2][:, j * 128:(j + 1) * 128], P32[w_][:, (c + j) * 128:(c + j + 1) * 128], ident[:, :],
                           [BP[w_]], [PSB[pi2]])
                    act(PT[w_][:, c:c + n_, :], PS[pi2][:, 0:n_ * 128].rearrange("p (a b) -> p a b", a=n_), AF.Copy,
                        [PSB[pi2]], [BPT[w_]])
                    c += n_
                pi3 = psn()
                if g == 0:
                    for c in range(2):
                        mm(PS[pi3][:, 0:128], vtok[:, b * 2 + c, :], PT[w_][:, c, :], c == 0, c == 1,
                           [Bvtok, BPT[w_]], [PSB[pi3]])
                else:
                    for c in range(5):
                        mm(PS[pi3][:, 0:128], vtok[:, ks // 128 + c, :], PT[w_][:, c, :], c == 0, False,
                           [Bvtok, BPT[w_]], [PSB[pi3]])
                    for c in range(2):
                        mm(PS[pi3][:, 0:128], cvb[:, c, :], PT[w_][:, 5 + c, :], False, c == 1,
                           [Bcvb, BPT[w_]], [PSB[pi3]])
                cp("dve", ymix[:, h, q0:q0 + 128], PS[pi3][:, 0:128], [PSB[pi3]], [By])
        S.barrier()

    PI = math.pi

    def phase_hyprep(l, L):
        RB.reset()
        RX.reset()
        n = L // 128
        nblk = max(1, L // 512)
        bw = min(L, 512)
        zfT = RB.alloc([L], parts=33)
        w1 = RB.alloc([64], parts=33)
        w2 = RB.alloc([64], parts=64)
        w3 = RB.alloc([2048], parts=64)
        hcol = RB.alloc([4], parts=64)
        fb = RB.alloc([2], parts=64)
        tn = RB.alloc([n])
        h1T = RB.alloc([L], parts=64)
        h2T = RB.alloc([L], parts=64)
        a_ = RB.alloc([bw], parts=64)
        m1 = RB.alloc([bw], parts=64)
        m2 = RB.alloc([bw], parts=64)
        absdec = RB.alloc([2048])
        env = [RB.alloc([512]) for _ in range(2)]
        F4 = [RB.alloc([4, n, 128], BF16) for _ in range(2)]
        hsp = [RX.alloc([2, 2, 256]) for _ in range(2)]
        filt = ymix[:, :, :].rearrange("p a b -> p (a b)")[:, 0:n * 2048].rearrange("p (m c) -> p m c", m=n)
        Bc, Bh1, Bh2, Ba, Bm1, Bm2, Bad = (Buf(x) for x in ("hyc", "h1T", "h2T", "a_", "m1", "m2", "absdec"))
        Benv = [Buf("env0"), Buf("env1")]
        BF4 = [Buf("F40"), Buf("F41")]
        Bhsp = [Buf("hsp0"), Buf("hsp1")]
        dma("sp", zfT, zfT_d[L], [], [Bc], "hy0")
        dma("sp", w1, hw1_d[l], [], [Bc], "hy0")
        dma("sp", w2, hw2_d[l], [], [Bc], "hy0")
        dma("sp", w3, hw3_d[l], [], [Bc], "hy0")
        dma("sp", hcol, hcol_d[l], [], [Bc], "hy0")
        dma("sp", tn, tn_d[L], [], [Bc], "hy0")
        dma("sp", absdec, hdec_d[l].partition_broadcast(128), [], [Bad], "hy1")
        act(absdec, absdec, AF.Abs, [Bad], [Bad])
        tt("dve", fb[:, 0:1], hcol[:, 0:1], hcol[:, 2:3], ALU.mult, [Bc], [Bc])
        tt("dve", fb[:, 1:2], hcol[:, 1:2], hcol[:, 3:4], ALU.mult, [Bc], [Bc])

        def sin_layer(dst, Bdst, lhsT, K, src, Bsrc, fcol, bcol):
            for blk in range(nblk):
                sl = slice(blk * bw, (blk + 1) * bw)
                pi = psn()
                mm(PS[pi][0:64, 0:bw], lhsT, src[0:K, sl], True, True, [Bc] + Bsrc, [PSB[pi]])
                ts("dve", a_, PS[pi][0:64, 0:bw], hcol[:, fcol:fcol + 1], fb[:, bcol:bcol + 1], ALU.mult, ALU.add,
                   [PSB[pi], Bc], [Ba])
                ts("dve", m1, a_, -PI, 2 * PI, ALU.is_lt, ALU.mult, [Ba], [Bm1])
                ts(PL, m2, a_, PI, -2 * PI, ALU.is_gt, ALU.mult, [Ba], [Bm2])
                tt("dve", a_, a_, m1, ALU.add, [Ba, Bm1], [Ba])
                tt("dve", a_, a_, m2, ALU.add, [Ba, Bm2], [Ba])
                act(dst[:, sl], a_, AF.Sin, [Ba], [Bdst])

        sin_layer(h1T, Bh1, w1[0:33, :], 33, zfT, [], 2, 0)
        sin_layer(h2T, Bh2, w2[0:64, :], 64, h1T, [Bh1], 3, 1)
        k_ = 0
        for mc in range(n):
            for nb in range(4):
                e_ = k_ % 2
                k_ += 1
                pi = psn()
                mm(PS[pi][:, :], h2T[:, mc * 128:(mc + 1) * 128], w3[:, nb * 512:(nb + 1) * 512], True, True,
                   [Bh2, Bc], [PSB[pi]])
                act(env[e_], absdec[:, nb * 512:(nb + 1) * 512], AF.Exp, [Bad, Bc], [Benv[e_]], scale=tn[:, mc:mc + 1])
                tt("dve", filt[:, mc, nb * 512:(nb + 1) * 512], PS[pi][:, :], env[e_], ALU.mult,
                   [PSB[pi], Benv[e_]], [By])
        for kf in range(n):
            f_ = kf % 2
            dma("sp", F4[f_], F4_d[L][kf], [], [BF4[f_]], f"F4{f_}")
            for o in range(2):
                e_ = (kf * 2 + o) % 2
                pr = psn()
                pim = psn()
                for (pp, ia, ib) in ((pr, 0, 1), (pim, 2, 3)):
                    for mc in range(n):
                        mm(PS[pp][:, :], F4[f_][:, ia, mc, :], filt[:, mc, o * 1024:o * 1024 + 512], mc == 0, False,
                           [BF4[f_], By], [PSB[pp]])
                    for mc in range(n):
                        mm(PS[pp][:, :], F4[f_][:, ib, mc, :], filt[:, mc, o * 1024 + 512:o * 1024 + 1024], False, mc == n - 1,
                           [BF4[f_], By], [PSB[pp]])
                act(hsp[e_][:, :, 0, :], PS[pr][:, :].rearrange("p (a b) -> p a b", a=2), AF.Copy, [PSB[pr]], [Bhsp[e_]])
                cp("dve", hsp[e_][:, :, 1, :], PS[pim][:, :].rearrange("p (a b) -> p a b", a=2), [PSB[pim]], [Bhsp[e_]])
                dma("sp", hspec[kf, :, o, :], hsp[e_].rearrange("p a b c -> p (a b c)"), [Bhsp[e_]], [Bhspec], f"hso{e_}")
        S.barrier()

    def conv3(u3, Bu, nseq, Ls, wcols, bias_col, dst, Bdst, t1, t2, Bt1, Bt2, final_eng="dve", final_func=None):
        w0, w1_, w2_ = wcols
        if bias_col is not None:
            act(t1, u3[:, :, 0:Ls], AF.Identity, [Bu, Bvec], [Bt1], bias=bias_col, scale=w0)
        else:
            act(t1, u3[:, :, 0:Ls], AF.Identity, [Bu, Bvec], [Bt1], scale=w0)
        stt("dve", t2, u3[:, :, 1:Ls + 1], w1_, t1, ALU.mult, ALU.add, [Bu, Bvec, Bt1], [Bt2])
        if final_func is None:
            stt(final_eng, dst, u3[:, :, 2:Ls + 2], w2_, t2, ALU.mult, ALU.add, [Bu, Bvec, Bt2], [Bdst])
        else:
            stt(final_eng, t1, u3[:, :, 2:Ls + 2], w2_, t2, ALU.mult, ALU.add, [Bu, Bvec, Bt2], [Bt1])
            act(dst, t1, final_func, [Bt1], [Bdst])

    def evac_pad(u3, Bu, pis, nseq, Ls):
        for tb in range(2):
            if nseq == 4:
                act(u3[:, 2 * tb:2 * tb + 2, 1:Ls + 1], PS[pis[tb]][:, :].rearrange("p (a b) -> p a b", a=2), AF.Copy,
                    [PSB[pis[tb]]], [Bu])
            else:
                act(u3[:, 0:1, 1 + tb * 512:1 + (tb + 1) * 512], PS[pis[tb]][:, :].rearrange("p (a b) -> p a b", a=1),
                    AF.Copy, [PSB[pis[tb]]], [Bu])

    def phase_hyena(l, g):
        RB.reset()
        RX.reset()
        nseq, Ls = (4, 256) if g == 0 else (1, 1024)
        n = Ls // 128
        upad = AA([nseq * (Ls + 2)])
        u3 = upad.rearrange("p (s l) -> p s l", s=nseq)
        Bu = Buf("upad")
        memset(PL, upad, 0.0, [Bu])
        t1 = AA([nseq, Ls])
        t2 = AA([nseq, Ls])
        Bt1, Bt2 = Buf("t1"), Buf("t2")
        z = AA([2, T])
        x1 = AA([2, T], BF16)
        x2 = AA([2, T], BF16)
        zt = AA([8, 256], BF16)
        Yr = AA([8, 256], BF16)
        Wv = AA([8, 256], BF16)
        zcs = [AA([512]) for _ in range(2)]
        HH = [AA([512]) for _ in range(2)]
        HH2 = [AA([512]) for _ in range(2)]
        m12 = [AA([512]) for _ in range(2)]
        m34 = [AA([512]) for _ in range(2)]
        F2 = [AA([2, n, 128], BF16) for _ in range(2)]
        I2 = [AA([2, Ls], BF16) for _ in range(2)]
        tmpo = [AA([512]) for _ in range(2)]
        Bz, Bx1, Bx2, Bzt, BYr, BW = (Buf(x) for x in ("z", "x1", "x2", "zt", "Yr", "W"))
        Bzcs = [Buf("zcs0"), Buf("zcs1")]
        BHH = [Buf("HH0"), Buf("HH1")]
        BHH2 = [Buf("HH20"), Buf("HH21")]
        Bm12 = [Buf("m120"), Buf("m121")]
        Bm34 = [Buf("m340"), Buf("m341")]
        BF2 = [Buf("F20"), Buf("F21")]
        BI2 = [Buf("I20"), Buf("I21")]
        Btmpo = [Buf("tmpo0"), Buf("tmpo1")]
        for hf in range(2):
            for part in range(3):
                for cc in range(2):
                    ch = part * 4 + 2 * hf + cc
                    wt, wb = wget(("win_t", l, THY + ch))
                    pis = proj(wt, wb, hT, Bh)
                    evac_pad(u3, Bu, pis, nseq, Ls)
                    wc = [vecs[:, l, V_HCW + tap * 12 + ch:V_HCW + tap * 12 + ch + 1] for tap in range(3)]
                    bc = vecs[:, l, V_HCB + ch:V_HCB + ch + 1]
                    if part == 0:
                        dst, Bd = z[:, cc, :], Bz
                    elif part == 1:
                        dst, Bd = x1[:, cc, :], Bx1
                    else:
                        dst, Bd = x2[:, cc, :], Bx2
                    conv3(u3, Bu, nseq, Ls, wc, bc, dst.rearrange("p (s l) -> p s l", s=nseq), Bd, t1, t2, Bt1, Bt2)
            for o in range(2):
                for t2_ in range(0, 8, 2):
                    pi = psn()
                    for dt_ in range(2):
                        for cc in range(2):
                            tr(PS[pi][:, dt_ * 256 + cc * 128:dt_ * 256 + (cc + 1) * 128],
                               z[:, cc, (t2_ + dt_) * 128:(t2_ + dt_ + 1) * 128], ident[:, :], [Bz], [PSB[pi]])
                    act(zt[:, t2_:t2_ + 2, :], PS[pi][:, :].rearrange("p (a b) -> p a b", a=2), AF.Copy, [PSB[pi]], [Bzt])
                k_ = 0
                for kf in range(n):
                    f_ = kf % 2
                    dma("sp", F2[f_], F2_d[Ls][kf], [], [BF2[f_]], f"F2{f_}")
                    hsl = hspec[kf, :, o, :]
                    dma("sp", HH[f_], hsl[:, hf * 512:(hf + 1) * 512], [Bhspec], [BHH[f_]], f"HH{f_}")
                    dma("sp", HH2[f_][:, 0:256], hsl[:, hf * 512 + 256:hf * 512 + 512], [Bhspec], [BHH2[f_]], f"HHb{f_}")
                    dma("sp", HH2[f_][:, 256:512], hsl[:, hf * 512:hf * 512 + 256], [Bhspec], [BHH2[f_]], f"HHb{f_}")
                    for s in range(nseq):
                        e_ = k_ % 2
                        k_ += 1
                        pi = psn()
                        for half, im in ((0, 0), (1, 1)):
                            for mc in range(n):
                                mm(PS[pi][:, half * 256:(half + 1) * 256], F2[f_][:, im, mc, :], zt[:, s * n + mc, :],
                                   mc == 0, mc == n - 1, [BF2[f_], Bzt], [PSB[pi]])
                        act(zcs[e_], PS[pi][:, :], AF.Copy, [PSB[pi]], [Bzcs[e_]])
                        tt("dve", m12[e_], zcs[e_], HH[f_], ALU.mult, [Bzcs[e_], BHH[f_]], [Bm12[e_]])
                        tt(PL, m34[e_], zcs[e_], HH2[f_], ALU.mult, [Bzcs[e_], BHH2[f_]], [Bm34[e_]])
                        tt("dve", Yr[:, s * n + kf, :], m12[e_][:, 0:256], m12[e_][:, 256:512], ALU.add, [Bm12[e_]], [BYr])
                        tt(PL, Wv[:, s * n + kf, :], m34[e_][:, 256:512], m34[e_][:, 0:256], ALU.subtract, [Bm34[e_]], [BW])
                acc = [psn() for _ in range(4)]
                if g == 1:
                    for kf in range(n):
                        f_ = kf % 2
                        dma("sp", I2[f_], I2_d[Ls][kf], [], [BI2[f_]], f"I2{f_}")
                        for cc in range(2):
                            for j in range(2):
                                pi = acc[cc * 2 + j]
                                for im, src_, Bs in ((0, Yr, BYr), (1, Wv, BW)):
                                    mm(PS[pi][:, :], src_[:, kf, cc * 128:(cc + 1) * 128], I2[f_][:, im, j * 512:(j + 1) * 512],
                                       kf == 0 and im == 0, kf == n - 1 and im == 1, [Bs, BI2[f_]], [PSB[pi]])
                else:
                    for kf in range(n):
                        dma("sp", I2[kf], I2_d[Ls][kf], [], [BI2[kf]], f"I2{kf}")
                    for cc in range(2):
                        for j in range(2):
                            pi = acc[cc * 2 + j]
                            for s2 in range(2):
                                s = j * 2 + s2
                                for kf in range(n):
                                    for im, src_, Bs in ((0, Yr, BYr), (1, Wv, BW)):
                                        mm(PS[pi][:, s2 * 256:(s2 + 1) * 256], src_[:, s * n + kf, cc * 128:(cc + 1) * 128],
                                           I2[kf][:, im, :], kf == 0 and im == 0, kf == n - 1 and im == 1,
                                           [Bs, BI2[kf]], [PSB[pi]])
                k_ = 0
                for cc in range(2):
                    for j in range(2):
                        pi = acc[cc * 2 + j]
                        e_ = k_ % 2
                        k_ += 1
                        sl = slice(j * 512, (j + 1) * 512)
                        sk = vecs[:, l, V_HSKIP + o * 4 + 2 * hf + cc:V_HSKIP + o * 4 + 2 * hf + cc + 1]
                        stt("dve", tmpo[e_], z[:, cc, sl], sk, PS[pi][:, :], ALU.mult, ALU.add, [Bz, Bvec, PSB[pi]], [Btmpo[e_]])
                        if o == 0:
                            tt(PL, z[:, cc, sl], tmpo[e_], x1[:, cc, sl], ALU.mult, [Btmpo[e_], Bx1], [Bz])
                        else:
                            tt(PL, ymix[:, 8 + 2 * hf + cc, sl], tmpo[e_], x2[:, cc, sl], ALU.mult, [Btmpo[e_], Bx2], [By])
        S.barrier()

    def phase_gdn(l, g):
        RB.reset()
        RX.reset()

        def A(shape, dtype=F32, parts=128):
            n = 1
            for s_ in shape:
                n *= s_
            nb = (n * (4 if dtype == F32 else 2) + 31) // 32 * 32
            reg = RB if RB.off + nb <= RB.ncols * 4 else RX
            return reg.alloc(shape, dtype, parts)

        nseq, Ls = (4, 256) if g == 0 else (1, 1024)
        cps = Ls // 64
        upad = A([nseq * (Ls + 2)])
        u3 = upad.rearrange("p (s l) -> p s l", s=nseq)
        Bu = Buf("gupad")
        memset(PLG, upad, 0.0, [Bu])
        t1f = A([T])
        t2f = A([T])
        t1 = t1f.rearrange("p (s l) -> p s l", s=nseq)
        t2 = t2f.rearrange("p (s l) -> p s l", s=nseq)
        Bt1, Bt2 = Buf("gt1"), Buf("gt2")
        bg = A([T], parts=40)
        gcl = A([2], parts=40)
        nA = A([1], parts=40)
        bgt = A([16, 40], parts=64)
        gc = A([16, 8], parts=64)
        egc = A([16, 8], parts=64)
        kdf = A([16, 8], parts=64)
        begc = A([16, 8], parts=64)
        gtot = A([16, 8])
        Bbg, Bgs = Buf("bg"), Buf("gsmall")
        memset(PLG, bg, 0.0, [Bbg])
        dma("sp", gcl, gcol_d[l], [], [Bgs], "g0")
        act(nA[32:40, :], gcl[32:40, 0:1], AF.Exp, [Bgs], [Bgs])
        ts("dve", nA[32:40, :], nA[32:40, :], -1.0, None, ALU.mult, None, [Bgs], [Bgs])
        wt, wb = wget(("win_t", l, TBA))
        pis = proj(wt, wb, hT, Bh)
        e1 = t1f
        for tb in range(2):
            sl = slice(tb * 512, (tb + 1) * 512)
            pi = pis[tb]
            act(bg[0:8, sl], PS[pi][0:8, :], AF.Sigmoid, [PSB[pi]], [Bbg])
            act(e1[32:40, sl], PS[pi][32:40, :], AF.Exp, [PSB[pi], Bgs], [Bt1], bias=gcl[32:40, 1:2])
            act(e1[32:40, sl], e1[32:40, sl], AF.Ln, [Bt1], [Bt1], bias=1.0)
            ts("dve", bg[32:40, sl], e1[32:40, sl], nA[32:40, 0:1], None, ALU.mult, None, [Bt1, Bgs], [Bbg])
        for half in range(2):
            pi = psn()
            for j in range(8):
                ci = half * 8 + j
                tr(PS[pi][0:64, j * 40:(j + 1) * 40], bg[0:40, ci * 64:(ci + 1) * 64], ident[0:40, 0:40], [Bbg], [PSB[pi]])
            cp("dve", bgt[:, half * 8:(half + 1) * 8, :], PS[pi][0:64, 0:320].rearrange("p (a b) -> p a b", a=8),
               [PSB[pi]], [Bgs])
        g8 = A([16, 8], parts=64)
        gF4 = A([16, 4], parts=64)
        gB4 = A([16, 4], parts=64)
        cp("dve", g8, bgt[:, :, 32:40], [Bgs], [Bgs])
        cp("dve", gF4, bgt[:, :, 32:36], [Bgs], [Bgs])
        cp("dve", gB4, bgt[:, :, 36:40], [Bgs], [Bgs])
        pi = psn()
        mm(PS[pi][0:64, 0:64], gmask[:, 0, 0, :], gF4.rearrange("p a b -> p (a b)"), True, True, [Bconst, Bgs], [PSB[pi]])
        mm(PS[pi][0:64, 64:128], gmask[:, 1, 0, :], gB4.rearrange("p a b -> p (a b)"), True, True, [Bconst, Bgs], [PSB[pi]])
        cp("dve", gc[:, :, 0:4], PS[pi][0:64, 0:64].rearrange("p (a b) -> p a b", b=4), [PSB[pi]], [Bgs])
        cp("dve", gc[:, :, 4:8], PS[pi][0:64, 64:128].rearrange("p (a b) -> p a b", b=4), [PSB[pi]], [Bgs])
        pi2 = psn()
        ptot = PS[pi2][:, 0:128].rearrange("p (a b) -> p a b", b=8)
        mm(PS[pi2][:, 0:128], ones32[0:64, :], g8.rearrange("p a b -> p (a b)"), True, True, [Bconst, Bgs], [PSB[pi2]])
        act(egc, gc, AF.Exp, [Bgs], [Bgs])
        cp("dve", gtot, ptot, [PSB[pi2]], [Bgs])
        tt("dve", kdf, gtot[0:64], gc, ALU.subtract, [Bgs], [Bgs])
        act(kdf, kdf, AF.Exp, [Bgs], [Bgs])
        act(gtot, gtot, AF.Exp, [Bgs], [Bgs])
        tt("dve", begc, bgt[:, :, 0:8], egc, ALU.mult, [Bgs], [Bgs])

        QT = A([T])
        KT = A([T])
        VT = A([T])
        szT = A([T], BF16)
        sqb = A([512], BF16)
        rsb = A([512])
        Ktok = A([16, 128], BF16, parts=64)
        Vtok = A([16, 128], BF16, parts=64)
        oacc = A([16, 128], parts=64)
        rso = A([16], parts=64)
        nst = 2 * nseq
        Sst = [A([128]) for _ in range(nst)]
        BS = [Buf(f"S{i}") for i in range(nst)]
        BQT, BKT, BVT, Bsz, Bsq, Brs, BKt, BVt, Bo, Brso = (Buf(x) for x in (
            "QT", "KT", "VT", "szT", "gsq", "grs", "Ktok", "Vtok", "oacc", "rso"))
        tx = A([8, 64], parts=64)
        Nn = [A([8, 64], parts=64) for _ in range(2)]
        NTt = [A([8, 64], parts=64) for _ in range(2)]
        Btx = Buf("tx")
        BN = [Buf("N0"), Buf("N1")]
        BNT = [Buf("NT0"), Buf("NT1")]
        Xs = [A([8, 256], parts=64) for _ in range(2)]
        wTs = [A([8, 64]) for _ in range(2)]
        qdTs = [A([8, 64]) for _ in range(2)]
        kds = [A([8, 128], parts=64) for _ in range(2)]
        Ats = [A([8, 64], parts=64) for _ in range(2)]
        BXs = [Buf("X0"), Buf("X1")]
        BwTs = [Buf("wT0"), Buf("wT1")]
        Bqds = [Buf("qd0"), Buf("qd1")]
        Bkds = [Buf("kd0"), Buf("kd1")]
        BAts = [Buf("At0"), Buf("At1")]
        vnew = [A([128], parts=64) for _ in range(4)]
        Bvn = [Buf(f"vn{i}") for i in range(4)]
        i64 = ident[0:64, 0:64]
        vn_i = [0]
        GST = cfg.get("gdn_stop", 9)

        def gen_solve(si_, hd, d, ci0):
            col = d * 4 + hd
            spi = [0]

            def psn():
                i = spi[0] % 6
                spi[0] += 1
                return i

            Mc = gmask[:, d, :, :]
            Ms = gmask[:, 2 + d, :, :]
            X, BX, wT, BwT, qdT, Bqd, kd, Bkd, At, BAt = (Xs[si_], BXs[si_], wTs[si_], BwTs[si_], qdTs[si_], Bqds[si_],
                                                          kds[si_], Bkds[si_], Ats[si_], BAts[si_])
            cs8 = slice(ci0, ci0 + 8)
            tok8 = slice(ci0 * 64, (ci0 + 8) * 64)
            gb = bgt[:, cs8, 32 + col:33 + col].to_broadcast([64, 8, 64])
            tt(PLG, tx, Mc, gb, ALU.mult, [Bconst, Bgs], [Btx])
            pR = psn()
            for e in range(8):
                mm(PS[pR][:, e * 64:(e + 1) * 64], ones32[0:64, :], tx[:, e, :], True, True, [Bconst, Btx], [PSB[pR]])
            pG = psn()
            pQ = psn()
            for e in range(8):
                ck = slice((ci0 + e) * 64, (ci0 + e + 1) * 64)
                mm(PS[pG][0:64, e * 64:(e + 1) * 64], KT[:, ck], KT[:, ck], True, True, [BKT], [PSB[pG]])
                mm(PS[pQ][0:64, e * 64:(e + 1) * 64], KT[:, ck], QT[:, ck], True, True, [BKT, BQT], [PSB[pQ]])
            tt("dve", X[:, :, 0:128], Vtok[:, cs8, :], bgt[:, cs8, col:col + 1].to_broadcast([64, 8, 128]),
               ALU.mult, [BVt, Bgs], [BX])
            tt("dve", X[:, :, 128:256], Ktok[:, cs8, :], begc[:, cs8, col:col + 1].to_broadcast([64, 8, 128]),
               ALU.mult, [BKt, Bgs], [BX])
            tt("dve", kd, Ktok[:, cs8, :], kdf[:, cs8, col:col + 1].to_broadcast([64, 8, 128]), ALU.mult,
               [BKt, Bgs], [Bkd])
            yield
            pRv = PS[pR][:, :].rearrange("p (a b) -> p a b", a=8)
            tt("dve", tx, pRv[0:64], gc[:, cs8, col:col + 1].to_broadcast([64, 8, 64]), ALU.subtract,
               [PSB[pR], Bgs], [Btx])
            cp("dve", qdT, pRv, [PSB[pR]], [Bqd])
            ts("dve", Nn[0], tx, 0.0, None, ALU.max, None, [Btx], [BN[0]])
            ts("dve", At, tx, 0.0, None, ALU.min, None, [Btx], [BAt])
            act(Nn[0], Nn[0], AF.Exp, [BN[0]], [BN[0]], scale=-1.0)
            act(At, At, AF.Exp, [BAt], [BAt])
            act(qdT, qdT, AF.Exp, [Bqd], [Bqd])
            yield
            pGv = PS[pG][0:64, :].rearrange("p (a b) -> p a b", a=8)
            pQv = PS[pQ][0:64, :].rearrange("p (a b) -> p a b", a=8)
            tt("dve", Nn[0], Nn[0], pGv, ALU.mult, [BN[0], PSB[pG]], [BN[0]])
            tt("dve", Nn[0], Nn[0], Ms, ALU.mult, [BN[0], Bconst], [BN[0]])
            tt("dve", Nn[0], Nn[0], bgt[:, cs8, col:col + 1].to_broadcast([64, 8, 64]), ALU.mult,
               [BN[0], Bgs], [BN[0]])
            tt("dve", At, At, pQv, ALU.mult, [BAt, PSB[pQ]], [BAt])
            tt("dve", At, At, Mc, ALU.mult, [BAt, Bconst], [BAt])
            tt("dve", qdT, QT[:, tok8].rearrange("p (a b) -> p a b", a=8), qdT, ALU.mult, [BQT, Bqd], [Bqd])
            pT = psn()
            for e in range(8):
                tr(PS[pT][0:64, e * 64:(e + 1) * 64], Nn[0][:, e, :], i64, [BN[0]], [PSB[pT]])
            yield
            cp("act", NTt[0], PS[pT][0:64, :].rearrange("p (a b) -> p a b", a=8), [PSB[pT]], [BNT[0]])

            def apply(NTp, BNTp, op):
                banks = [psn() for _ in range(4)]
                for e in range(8):
                    pb = banks[e // 2]
                    mm(PS[pb][0:64, (e % 2) * 256:(e % 2 + 1) * 256], NTp[:, e, :], X[:, e, :], True, True,
                       [BNTp, BX], [PSB[pb]])
                return banks

            def apply_fin(banks, op):
                for b_ in range(4):
                    pb = banks[b_]
                    tt("dve", X[:, 2 * b_:2 * b_ + 2, :], X[:, 2 * b_:2 * b_ + 2, :],
                       PS[pb][0:64, :].rearrange("p (a b) -> p a b", a=2), op, [BX, PSB[pb]], [BX])

            banks = apply(NTt[0], BNT[0], ALU.subtract)
            cur = 0
            op_prev = ALU.subtract
            for lev in range(5):
                nxt = 1 - cur
                pN2 = psn()
                for e in range(8):
                    mm(PS[pN2][0:64, e * 64:(e + 1) * 64], Nn[cur][:, e, :], NTt[cur][:, e, :], True, True,
                       [BN[cur], BNT[cur]], [PSB[pN2]])
                pN1 = None
                if lev < 4:
                    pN1 = psn()
                    for e in range(8):
                        mm(PS[pN1][0:64, e * 64:(e + 1) * 64], NTt[cur][:, e, :], Nn[cur][:, e, :], True, True,
                           [BN[cur], BNT[cur]], [PSB[pN1]])
                yield
                apply_fin(banks, op_prev)
                if pN1 is not None:
                    cp("act", Nn[nxt], PS[pN1][0:64, :].rearrange("p (a b) -> p a b", a=8), [PSB[pN1]], [BN[nxt]])
                cp("act", NTt[nxt], PS[pN2][0:64, :].rearrange("p (a b) -> p a b", a=8), [PSB[pN2]], [BNT[nxt]])
                banks = apply(NTt[nxt], BNT[nxt], ALU.add)
                op_prev = ALU.add
                cur = nxt
            yield
            apply_fin(banks, op_prev)
            pW = psn()
            for e in range(8):
                tr(PS[pW][:, e * 64:(e + 1) * 64], X[:, e, 128:256], i64, [BX], [PSB[pW]])
            yield
            cp("act", wT, PS[pW][:, :].rearrange("p (a b) -> p a b", a=8), [PSB[pW]], [BwT])

        def gen_scan(si_, hd, d, ci0, first):
            col = d * 4 + hd
            X, BX, wT, BwT, qdT, Bqd, kd, Bkd, At, BAt = (Xs[si_], BXs[si_], wTs[si_], BwTs[si_], qdTs[si_], Bqds[si_],
                                                          kds[si_], Bkds[si_], Ats[si_], BAts[si_])
            if g == 1:
                chains = [(0, list(range(8)) if d == 0 else list(range(7, -1, -1)))]
            else:
                chains = []
                for s2 in range(2):
                    es = list(range(s2 * 4, s2 * 4 + 4))
                    chains.append((ci0 // 4 + s2, es if d == 0 else es[::-1]))
            nstep = len(chains[0][1])
            pA, pB = 6, 7
            for step in range(nstep):
                vis = []
                for c_, (s, es) in enumerate(chains):
                    e = es[step]
                    si = d * nseq + s
                    mm(PS[pA][0:64, c_ * 256:c_ * 256 + 128], wT[:, e, :], Sst[si], True, True, [BwT, BS[si]], [PSB[pA]])
                yield
                for c_, (s, es) in enumerate(chains):
                    e = es[step]
                    si = d * nseq + s
                    vi = vn_i[0] % 4
                    vn_i[0] += 1
                    vis.append(vi)
                    tt("dve", vnew[vi], X[:, e, 0:128], PS[pA][0:64, c_ * 256:c_ * 256 + 128], ALU.subtract,
                       [BX, PSB[pA]], [Bvn[vi]])
                for c_, (s, es) in enumerate(chains):
                    e = es[step]
                    si = d * nseq + s
                    vi = vis[c_]
                    po = PS[pA][0:64, c_ * 256 + 128:c_ * 256 + 256]
                    mm(po, qdT[:, e, :], Sst[si], True, False, [Bqd, BS[si]], [PSB[pA]])
                    mm(po, At[:, e, :], vnew[vi], False, True, [BAt, Bvn[vi]], [PSB[pA]])
                    mm(PS[pB][:, c_ * 128:(c_ + 1) * 128], kd[:, e, :], vnew[vi], True, True, [Bkd, Bvn[vi]], [PSB[pB]])
                yield
                for c_, (s, es) in enumerate(chains):
                    e = es[step]
                    ci = ci0 + e
                    si = d * nseq + s
                    po = PS[pA][0:64, c_ * 256 + 128:c_ * 256 + 256]
                    if first:
                        cp("dve", oacc[:, ci, :], po, [PSB[pA]], [Bo])
                    else:
                        tt("dve", oacc[:, ci, :], oacc[:, ci, :], po, ALU.add, [Bo, PSB[pA]], [Bo])
                    stt("dve", Sst[si], Sst[si], gtot[:, ci, col:col + 1], PS[pB][:, c_ * 128:(c_ + 1) * 128], ALU.mult, ALU.add,
                        [BS[si], Bgs, PSB[pB]], [BS[si]])

        def drive(*gens):
            gens = [g_ for g_ in gens if g_ is not None]
            while gens:
                for g_ in list(gens):
                    try:
                        next(g_)
                    except StopIteration:
                        gens.remove(g_)

        for hd in range(4):
            if GST <= 1:
                continue
            for part, dst, Bd in ((0, QT, BQT), (1, KT, BKT), (2, VT, BVT)):
                ch = part * 4 + hd
                wt, wb = wget(("win_t", l, TGQ + ch))
                pis = proj(wt, wb, hT, Bh)
                evac_pad(u3, Bu, pis, nseq, Ls)
                wc = [vecs[:, l, V_GCW + tap * 12 + ch:V_GCW + tap * 12 + ch + 1] for tap in range(3)]
                conv3(u3, Bu, nseq, Ls, wc, None, dst.rearrange("p (s l) -> p s l", s=nseq), Bd, t1, t2, Bt1, Bt2,
                      final_func=AF.Silu)
                if part < 2:
                    for tb in range(2):
                        sl = slice(tb * 512, (tb + 1) * 512)
                        act(sqb, dst[:, sl], AF.Square, [Bd], [Bsq])
                        pi = psn()
                        mm(PS[pi][:, :], ones_bf[:, :], sqb, True, True, [Bsq, Bconst], [PSB[pi]])
                        act(rsb, PS[pi][:, :], AF.Sqrt, [PSB[pi]], [Brs], bias=EPS)
                        recip("dve", rsb, rsb, [Brs], [Brs])
                        stt("dve", dst[:, sl], dst[:, sl], (DH ** -0.5 if part == 0 else 1.0), rsb, ALU.mult, ALU.mult,
                            [Bd, Brs], [Bd])
            wt, wb = wget(("win_t", l, TGZ + hd))
            pis = proj(wt, wb, hT, Bh)
            for tb in range(2):
                act(szT[:, tb * 512:(tb + 1) * 512], PS[pis[tb]][:, :], AF.Silu, [PSB[pis[tb]]], [Bsz])
            for src_, Bs_, dst_, Bd_ in ((KT, BKT, Ktok, BKt), (VT, BVT, Vtok, BVt)):
                for q4 in range(4):
                    pi = psn()
                    for j in range(4):
                        ci = q4 * 4 + j
                        tr(PS[pi][0:64, j * 128:(j + 1) * 128], src_[:, ci * 64:(ci + 1) * 64], ident[:, :], [Bs_], [PSB[pi]])
                    cp("act", dst_[:, q4 * 4:(q4 + 1) * 4, :], PS[pi][0:64, :].rearrange("p (a b) -> p a b", a=4),
                       [PSB[pi]], [Bd_])
            for d in range(2):
                for s in range(nseq):
                    si = d * nseq + s
                    if g == 0:
                        memset(PLG, Sst[si], 0.0, [BS[si]])
                    else:
                        dma("sp", Sst[si], (sf0_d if d == 0 else sb0_d)[l, hd], [], [BS[si]], f"st{si}")
            if GST <= 2:
                continue
            subs = []
            for sbi in range(2):
                subs.append((0, sbi * 8))
                subs.append((1, (1 - sbi) * 8))
            prev = None
            for i_, (d, ci0) in enumerate(subs):
                drive(gen_solve(i_ % 2, hd, d, ci0), prev)
                prev = gen_scan(i_ % 2, hd, d, ci0, i_ < 2)
            drive(prev)
            if g == 0:
                for d in range(2):
                    for s in range(nseq):
                        si = d * nseq + s
                        dma("sp", (sf_o if d == 0 else sb_o)[l, s, hd], Sst[si], [BS[si]], [], f"so{si}")
            for half in range(2):
                dst_sq = (t1f if half == 0 else t2f)[0:64, :].rearrange("p (a b) -> p a b", a=8)
                tt(PLG, dst_sq, oacc[:, half * 8:(half + 1) * 8, :], oacc[:, half * 8:(half + 1) * 8, :], ALU.mult,
                   [Bo], [Bt1 if half == 0 else Bt2])
                red("dve", rso[:, half * 8:(half + 1) * 8], dst_sq, ALU.add, [Bt1 if half == 0 else Bt2], [Brso])
            act(rso, rso, AF.Sqrt, [Brso], [Brso], bias=EPS, scale=1.0 / 128)
            recip("dve", rso, rso, [Brso], [Brso])
            tt("dve", oacc, oacc, rso.unsqueeze(2).to_broadcast([64, 16, 128]), ALU.mult, [Bo, Brso], [Bo])
            for half in range(2):
                pi = psn()
                for j in range(8):
                    ci = half * 8 + j
                    tr(PS[pi][:, j * 64:(j + 1) * 64], oacc[:, ci, :], i64, [Bo], [PSB[pi]])
                sl = slice(half * 512, (half + 1) * 512)
                stt("dve", ymix[:, 12 + hd, sl], PS[pi][:, :], vecs[:, l, V_GNG:V_GNG + 1], szT[:, sl], ALU.mult, ALU.mult,
                    [PSB[pi], Bvec, Bsz], [By])
        S.barrier()

    def phase_merge(l, g):
        RX.reset()
        merged = RX.alloc([16, T], BF16)
        Bmg = Buf("merged")
        sig = [RX.alloc([512]) for _ in range(2)]
        Bsig = [Buf("sig0"), Buf("sig1")]
        accm = [RX.alloc([512]) for _ in range(2)]
        Bacc = [Buf("acc0"), Buf("acc1")]
        tmpm = [RX.alloc([512]) for _ in range(2)]
        Btm = [Buf("mt0"), Buf("mt1")]
        kranges = [range(0, 8), range(8, 12), range(12, 16)]
        for n_ in range(16):
            wtp, wbp = wget(("wp_t", l, n_))
            pbs = [proj(wtp, wbp, ymix, By, kranges[i]) for i in range(3)]
            for i in range(3):
                wtg, wbg = wget(("win_t", l, TGATE + i * 16 + n_))
                pg = proj(wtg, wbg, hT, Bh)
                pb = pbs[i]
                bcol = vecs[:, l, V_BGATE + i * 16 + n_:V_BGATE + i * 16 + n_ + 1]
                for tb in range(2):
                    sl = slice(tb * 512, (tb + 1) * 512)
                    act(sig[tb], PS[pg[tb]][:, :], AF.Sigmoid, [PSB[pg[tb]], Bvec], [Bsig[tb]], bias=bcol)
                    if i == 0:
                        tt("dve", accm[tb], PS[pb[tb]][:, :], sig[tb], ALU.mult, [PSB[pb[tb]], Bsig[tb]], [Bacc[tb]])
                    else:
                        tt("dve", tmpm[tb], PS[pb[tb]][:, :], sig[tb], ALU.mult, [PSB[pb[tb]], Bsig[tb]], [Btm[tb]])
                        if i == 1:
                            tt(PL, accm[tb], accm[tb], tmpm[tb], ALU.add, [Bacc[tb], Btm[tb]], [Bacc[tb]])
                        else:
                            tt(PL, merged[:, n_, sl], accm[tb], tmpm[tb], ALU.add, [Bacc[tb], Btm[tb]], [Bmg])
        tap(f"merged_{l}_{g}", merged, [128, 16, T], BF16, [Bmg])
        dma("sp", xT, xscr.rearrange("(k p) t -> p k t", p=128), [Bxscr], [Bx], "xld")
        for m in range(16):
            wt, wb = wget(("wout_t", l, m))
            pis = proj(wt, wb, merged, Bmg)
            for tb in range(2):
                sl = slice(tb * 512, (tb + 1) * 512)
                stt("dve", xT[:, m, sl], PS[pis[tb]][:, :], modv[:, l, 32 + m:33 + m, g], xT[:, m, sl], ALU.mult, ALU.add,
                    [PSB[pis[tb]], Bmod, Bx], [Bx])
        S.barrier()

    def phase_ffn(l, g):
        RX.reset()
        nseq, Ls = (4, 256) if g == 0 else (1, 1024)
        upad = [RX.alloc([nseq * (Ls + 2)]) for _ in range(2)]
        u3 = [u.rearrange("p (s l) -> p s l", s=nseq) for u in upad]
        Bu = [Buf("fu0"), Buf("fu1")]
        for i in range(2):
            memset(PL, upad[i], 0.0, [Bu[i]])
        t1 = [RX.alloc([nseq, Ls]) for _ in range(2)]
        t2 = [RX.alloc([nseq, Ls]) for _ in range(2)]
        Bt1 = [Buf("ft10"), Buf("ft11")]
        Bt2 = [Buf("ft20"), Buf("ft21")]
        ca = RX.alloc([T])
        cb = RX.alloc([T])
        Bca, Bcb = Buf("ca"), Buf("cb")
        gbuf = ymix
        for qd in range(4):
            for jj in range(11):
                j = qd * 11 + jj
                for ab in range(2):
                    ch = ab * 44 + j
                    wt, wb = wget(("wup_t", l, ch))
                    pis = proj(wt, wb, hT, Bh)
                    evac_pad(u3[ab], Bu[ab], pis, nseq, Ls)
                    wc = [vecs[:, l, V_FCW + tap_ * 88 + ch:V_FCW + tap_ * 88 + ch + 1] for tap_ in range(3)]
                    bc = vecs[:, l, V_FCB + ch:V_FCB + ch + 1]
                    if ab == 0:
                        conv3(u3[0], Bu[0], nseq, Ls, wc, bc, ca.rearrange("p (s l) -> p s l", s=nseq), Bca,
                              t1[0], t2[0], Bt1[0], Bt2[0], final_func=AF.Silu)
                    else:
                        conv3(u3[1], Bu[1], nseq, Ls, wc, bc, cb.rearrange("p (s l) -> p s l", s=nseq), Bcb,
                              t1[1], t2[1], Bt1[1], Bt2[1])
                tt(PL, gbuf[:, jj, :], ca, cb, ALU.mult, [Bca, Bcb], [By])
            for m in range(16):
                wt, wb = wget(("wdn_t", l, qd, m), kct=11)
                pis = proj(wt, wb, gbuf, By, range(11))
                for tb in range(2):
                    sl = slice(tb * 512, (tb + 1) * 512)
                    stt("dve", xT[:, m, sl], PS[pis[tb]][:, :], modv[:, l, 80 + m:81 + m, g], xT[:, m, sl], ALU.mult, ALU.add,
                        [PSB[pis[tb]], Bmod, Bx], [Bx])
        S.barrier()

    for g in groups:
        dma("sp", xT, xT_d[g].rearrange("(k p) t -> p k t", p=128), [], [Bx], "xin")
        for l in layers:
            phase_norm(l, g, 0)
            tap(f"h_{l}_{g}", hT[:, :, :], [128, 16, T], BF16, [Bh])
            dma("sp", xscr.rearrange("(k p) t -> p k t", p=128), xT, [Bx], [Bxscr], "xsp")
            S.barrier()
            if "hy" in phases:
                phase_hyprep(l, 256 if g == 0 else 1024)
            if "attn" in phases:
                phase_attn(l, g)
            if "hy" in phases:
                phase_hyena(l, g)
            if "gdn" in phases:
                phase_gdn(l, g)
            tap(f"ymix_{l}_{g}", ymix[:, :, :], [128, 16, T], BF16, [By])
            if "merge" in phases:
                phase_merge(l, g)
                tap(f"x1_{l}_{g}", xT, [128, 16, T], F32, [Bx])
            if "ffn" in phases:
                phase_norm(l, g, 1)
                phase_ffn(l, g)
                tap(f"x2_{l}_{g}", xT, [128, 16, T], F32, [Bx])
        if cfg.get("final", True):
            phase_final(g)
    if cfg.get("_collect"):
        return wkeys
    S.final_wait("sp")
    S.emit()
    return nc


def build2(cfg=None):
    cfg = dict(cfg or {})
    c1 = dict(cfg)
    c1["_collect"] = True
    plan = build(c1)
    cfg["_wplan"] = plan
    return build(cfg)


_SHARED_KEYS = None


def kernel(**inputs):
    inp = {k: np.asarray(v) for k, v in inputs.items()}
    sh = prep_shared(inp)
    nc = build2()
    in_maps = []
    for core in range(8):
        m = dict(sh)
        m.update(prep_core(inp, core))
        in_maps.append(m)
    res = run_bass_kernel_spmd(nc, in_maps, core_ids=list(range(8))).results
    B, SEQ = 32, 256
    y_p = np.zeros((B, SEQ, D), np.float32)
    y_s = np.zeros((4, 1024, D), np.float32)
    nk = np.zeros((B, NLAYER, SEQ, 8, 128), np.float32)
    nv = np.zeros((B, NLAYER, SEQ, 8, 128), np.float32)
    nsf = np.zeros((B, NLAYER, 4, 128, 128), np.float32)
    nsb = np.zeros((B, NLAYER, 4, 128, 128), np.float32)
    for core in range(8):
        r = res[core]
        yT = np.asarray(r["yT"])
        y_p[4 * core:4 * core + 4] = yT[0].T.reshape(4, SEQ, D)
        if core < 4:
            y_s[core] = yT[1].T
        kT = np.asarray(r["kT_out"])
        vT = np.asarray(r["vT_out"])
        nk[4 * core:4 * core + 4] = kT.transpose(2, 0, 1).reshape(4, SEQ, NLAYER, 8, 128).transpose(0, 2, 1, 3, 4)
        nv[4 * core:4 * core + 4] = vT.transpose(2, 0, 1).reshape(4, SEQ, NLAYER, 8, 128).transpose(0, 2, 1, 3, 4)
        nsf[4 * core:4 * core + 4] = np.asarray(r["sf_out"]).transpose(1, 0, 2, 3, 4)
        nsb[4 * core:4 * core + 4] = np.asarray(r["sb_out"]).transpose(1, 0, 2, 3, 4)
    return (y_p, y_s, nk, nv, nsf, nsb)
```

```python
import math
import numpy as np
import ml_dtypes
import concourse.bass as bass
import concourse.mybir as mybir
from concourse.bass_utils import run_bass_kernel_spmd

F32 = mybir.dt.float32
BF16 = mybir.dt.bfloat16
AF = mybir.ActivationFunctionType
ALU = mybir.AluOpType
AX = mybir.AxisListType


class Buf:
    __slots__ = ("name", "w", "r")

    def __init__(self, name):
        self.name = name
        self.w = {}
        self.r = {}


class _Eng:
    def __init__(self, key, sem):
        self.key = key
        self.sem = sem
        self.cnt = 0
        self.ops = []
        self.waited = {}


class Sched:
    ENG = ("pe", "act", "dve", "pool", "sp")

    def __init__(self, nc):
        self.nc = nc
        self.eng = {k: _Eng(k, nc.alloc_semaphore(name="s_" + k)) for k in self.ENG}
        self.dsem = {}

    def dma_sem(self, name):
        if name not in self.dsem:
            self.dsem[name] = [self.nc.alloc_semaphore(name="d_" + name), 0]
        return name

    def _wait(self, e, sid, sem, val):
        if e.waited.get(sid, 0) >= val:
            return
        e.waited[sid] = val
        e.ops.append(lambda h, sem=sem, val=val: h.wait_ge(sem, val))

    def op(self, ekey, fn, r=(), w=(), dma=None):
        e = self.eng[ekey]
        deps = {}

        def add(d, raw):
            for sid, (sem, val) in d.items():
                if sid == e.key and dma is None:
                    if e.key == "pe" or not raw:
                        continue
                if val > deps.get(sid, (None, 0))[1]:
                    deps[sid] = (sem, val)

        for b in r:
            add(b.w, True)
        for b in w:
            add(b.w, False)
            add(b.r, False)
        for sid, (sem, val) in deps.items():
            self._wait(e, sid, sem, val)
        if dma is None:
            e.cnt += 1
            tok = (e.key, e.sem, e.cnt)
            sem, inc = e.sem, 1
        else:
            ds = self.dsem[dma]
            ds[1] += 16
            tok = ("d_" + dma, ds[0], ds[1])
            sem, inc = ds[0], 16
        e.ops.append(lambda h, fn=fn, sem=sem, inc=inc: fn(h).then_inc(sem, inc))
        for b in r:
            b.r[tok[0]] = (tok[1], tok[2])
        for b in w:
            b.w = {tok[0]: (tok[1], tok[2])}
            b.r = {}
        return tok

    def barrier(self):
        for e in self.eng.values():
            for o in self.eng.values():
                if o is not e and o.cnt > 0:
                    self._wait(e, o.key, o.sem, o.cnt)
            for name, (sem, val) in self.dsem.items():
                if val > 0:
                    self._wait(e, "d_" + name, sem, val)

    def final_wait(self, ekey="sp"):
        e = self.eng[ekey]
        for o in self.eng.values():
            if o is not e and o.cnt > 0:
                self._wait(e, o.key, o.sem, o.cnt)
        for name, (sem, val) in self.dsem.items():
            if val > 0:
                self._wait(e, "d_" + name, sem, val)

    def emit(self):
        nc = self.nc
        with nc.Block() as block:
            @block.tensor
            def _(h):
                for f in self.eng["pe"].ops:
                    f(h)

            @block.scalar
            def _(h):
                for f in self.eng["act"].ops:
                    f(h)

            @block.vector
            def _(h):
                for f in self.eng["dve"].ops:
                    f(h)

            @block.gpsimd
            def _(h):
                for f in self.eng["pool"].ops:
                    f(h)

            @block.sync
            def _(h):
                for f in self.eng["sp"].ops:
                    f(h)


D = 2048
T = 1024
KC = 16
NLAYER = 2
DH = 128
D_FF = 5632
N_IN = 12816
EPS = 1e-6
TQ, TK, TV, THY, TGQ, TGZ, TBA, TGATE = 0, 8, 16, 24, 36, 48, 52, 53
NWIN = 101
V_BMOD, V_LN1, V_LN2, V_BGATE, V_HCW, V_HCB, V_HSKIP, V_GCW, V_GNG, V_FCW, V_FCB, V_FING, NV = (
    0, 96, 112, 128, 176, 212, 224, 232, 268, 272, 536, 624, 640)
KS_TOK = [0, 0, 0, 128, 256, 384, 384, 384]
NSLOT = 4
BFNP = ml_dtypes.bfloat16


def _tile_w(w):
    K, N = w.shape
    return np.ascontiguousarray(w.reshape(K // 128, 128, N // 128, 128).transpose(2, 1, 0, 3)).reshape(N // 128, 128, K)


def _fm(v):
    return np.ascontiguousarray(v.reshape(-1, 128).T)


def _dft_consts(L):
    N = 2 * L
    k = np.arange(L, dtype=np.float64)
    m = np.arange(L, dtype=np.float64)
    w = 2 * np.pi * (k + 0.5) / N
    ang = np.outer(m, w)
    Cm, Sm = np.cos(ang), np.sin(ang)
    ang1 = np.outer(m + 1, w)
    Cm1, Sm1 = np.cos(ang1), np.sin(ang1)
    Cm1[L - 1] = 0
    Sm1[L - 1] = 0
    n = L // 128

    def tf(M):
        return M.reshape(n, 128, n, 128).transpose(2, 1, 0, 3)

    F4 = np.stack([tf(Cm), tf(Cm1), tf(-Sm), tf(Sm1)], axis=2)
    F2 = np.stack([tf(Cm), tf(Sm)], axis=2)
    CT, ST = Cm.T / L, Sm.T / L
    I2 = np.stack([CT.reshape(n, 128, L), ST.reshape(n, 128, L)], axis=2)
    c = lambda a: np.ascontiguousarray(a.astype(np.float32).astype(BFNP))
    return c(F4), c(F2), c(I2)


def _zfeat(L):
    f32 = np.float32
    t_norm = np.linspace(0.0, 1.0, L, dtype=f32)
    t_idx = np.arange(L, dtype=f32)
    bands = np.linspace(1e-4, 15.0, 16, dtype=f32)
    ang = (f32(2.0 * math.pi / L) * t_idx[:, None] * bands[None, :]).astype(f32)
    z = np.concatenate([t_norm[:, None], np.cos(ang), np.sin(ang)], axis=-1).astype(f32)
    tn = np.ascontiguousarray((-t_norm).reshape(L // 128, 128).T)
    return np.ascontiguousarray(z.T), tn


def _gdn_masks():
    p = np.arange(64)[:, None]
    f = np.arange(64)[None, :]
    ms = [(p <= f), (p >= f), (p > f), (p < f)]
    out = np.stack([np.tile(m.astype(np.float32)[:, None, :], (1, 8, 1)) for m in ms], axis=0)
    return np.ascontiguousarray(out)


def _bias_mask(rpb):
    out = np.full((8, 8, 128, 640), -1e30, np.float32)
    p = np.arange(128)
    cq = p % 64
    cs = np.clip(cq - 8, 0, 48)
    for qt in range(8):
        r = 2 * qt + p // 64
        st = np.clip(r - 4, 0, 8)
        ktok = KS_TOK[qt] + np.arange(640)
        kr = ktok // 64
        ck = ktok % 64
        ok = ((kr[None, :] >= st[:, None]) & (kr[None, :] < st[:, None] + 8)
              & (ck[None, :] >= cs[:, None]) & (ck[None, :] < cs[:, None] + 16))
        ri = np.clip(kr[None, :] - r[:, None] + 7, 0, 14)
        ci = np.clip(ck[None, :] - cq[:, None] + 15, 0, 30)
        for h in range(8):
            vals = rpb[h][ri, ci]
            out[h, qt] = np.where(ok, vals, np.float32(-1e30))
    return out


def prep_shared(inp):
    sh = {}
    L2 = NLAYER
    w_in = inp["w_in"]
    win_t = np.zeros((L2, NWIN, 128, D), np.float32)
    wmod_t = np.zeros((L2, 96, 128, D), np.float32)
    wp_t = np.zeros((L2, 16, 128, D), np.float32)
    wout_t = np.zeros((L2, 16, 128, D), np.float32)
    wup_t = np.zeros((L2, 88, 128, D), np.float32)
    wdn_t = np.zeros((L2, 4, 16, 128, 11 * 128), np.float32)
    vecs = np.zeros((128, L2, NV), np.float32)
    for l in range(L2):
        w = w_in[l]
        win_t[l, 0:52] = _tile_w(w[:, 0:6656])
        ba = np.zeros((D, 128), np.float32)
        ba[:, 0:8] = w[:, 6656:6664]
        ba[:, 32:40] = w[:, 6664:6672]
        win_t[l, 52] = _tile_w(ba)[0]
        win_t[l, 53:101] = _tile_w(w[:, 6672:12816])
        wmod_t[l] = _tile_w(inp["w_mod"][l])
        wp_t[l] = _tile_w(np.concatenate([inp["w_pa"][l], inp["w_pb"][l], inp["w_pc"][l]], axis=0))
        wout_t[l] = _tile_w(inp["w_out"][l])
        wup_t[l] = _tile_w(inp["ffn_w_up"][l])
        for q in range(4):
            wdn_t[l, q] = _tile_w(inp["ffn_w_down"][l][q * 1408:(q + 1) * 1408])
        v = vecs[:, l]
        v[:, V_BMOD:V_BMOD + 96] = _fm(inp["b_mod"][l])
        v[:, V_LN1:V_LN1 + 16] = _fm(inp["ln1_g"][l])
        v[:, V_LN2:V_LN2 + 16] = _fm(inp["ln2_g"][l])
        v[:, V_BGATE:V_BGATE + 48] = _fm(inp["b_gate"][l])
        for tap in range(3):
            v[:, V_HCW + tap * 12:V_HCW + tap * 12 + 12] = _fm(inp["hy_conv_w"][l][tap])
            v[:, V_GCW + tap * 12:V_GCW + tap * 12 + 12] = _fm(inp["gdn_conv_w"][l][tap])
            v[:, V_FCW + tap * 88:V_FCW + tap * 88 + 88] = _fm(inp["ffn_conv_w"][l][tap])
        v[:, V_HCB:V_HCB + 12] = _fm(inp["hy_conv_b"][l])
        v[:, V_HSKIP:V_HSKIP + 8] = _fm(inp["hy_skip"][l].reshape(-1))
        v[:, V_GNG:V_GNG + 1] = _fm(inp["gdn_norm_g"][l])
        v[:, V_FCB:V_FCB + 88] = _fm(inp["ffn_conv_b"][l])
        v[:, V_FING:V_FING + 16] = _fm(inp["final_g"])
    sh.update(win_t=win_t, wmod_t=wmod_t, wp_t=wp_t, wout_t=wout_t, wup_t=wup_t, wdn_t=wdn_t, vecs=vecs)
    sh["bm"] = np.stack([_bias_mask(inp["na_rpb"][l]) for l in range(L2)], axis=0)
    sh["hw1"] = np.ascontiguousarray(inp["hy_w1"])
    sh["hw2"] = np.ascontiguousarray(inp["hy_w2"])
    sh["hw3"] = np.ascontiguousarray(inp["hy_w3"])
    hcol = np.zeros((L2, 64, 4), np.float32)
    hcol[:, :, 0] = inp["hy_b1"]
    hcol[:, :, 1] = inp["hy_b2"]
    hcol[:, :, 2] = inp["hy_freq"][:, 0]
    hcol[:, :, 3] = inp["hy_freq"][:, 1]
    sh["hcol"] = hcol
    sh["hdec"] = np.ascontiguousarray(inp["hy_decay"].reshape(L2, 2048))
    gcol = np.zeros((L2, 40, 2), np.float32)
    gcol[:, 32:40, 0] = inp["gdn_a_log"].reshape(L2, 8)
    gcol[:, 32:40, 1] = inp["gdn_dt_bias"].reshape(L2, 8)
    sh["gcol"] = gcol
    for L in (256, 1024):
        F4, F2, I2 = _dft_consts(L)
        zfT, tn = _zfeat(L)
        sh[f"F4_{L}"], sh[f"F2_{L}"], sh[f"I2_{L}"], sh[f"zfT_{L}"], sh[f"tn_{L}"] = F4, F2, I2, zfT, tn
    sh["ident"] = np.eye(128, dtype=np.float32)
    sh["gmask"] = _gdn_masks()
    return sh


def prep_core(inp, core):
    pc = {}
    xp = inp["x_prompt"][4 * core:4 * core + 4].reshape(T, D)
    sb = core % 4
    xs = inp["x_sample"][sb]
    pc["xT"] = np.ascontiguousarray(np.stack([xp.T, xs.T], axis=0))
    cv = np.stack([inp["c_ctx"], inp["c"][sb]], axis=-1)
    pc["cvec"] = np.ascontiguousarray(cv.reshape(16, 128, 2).transpose(1, 0, 2))
    pc["ckT"] = np.ascontiguousarray(inp["cache_k"][sb].transpose(0, 2, 3, 1))
    pc["cv"] = np.ascontiguousarray(inp["cache_v"][sb].reshape(NLAYER, 256, 1024))
    pc["sf0"] = np.ascontiguousarray(inp["state_fwd"][sb])
    pc["sb0"] = np.ascontiguousarray(inp["state_bwd"][sb])
    return pc


class Region:
    def __init__(self, tensor, ncols):
        self.t = tensor
        self.ncols = ncols
        self.off = 0

    def reset(self):
        self.off = 0

    def alloc(self, shape, dtype=F32, parts=128):
        n = 1
        for s in shape:
            n *= s
        nb = n * (4 if dtype == F32 else 2)
        nb = (nb + 31) // 32 * 32
        c0 = self.off // 4
        c1 = (self.off + nb) // 4
        assert c1 <= self.ncols, ("region overflow", self.off, nb, self.ncols * 4)
        self.off += nb
        ap = self.t[0:parts, c0:c1]
        if dtype != F32:
            ap = ap.bitcast(dtype)
        ap = ap[:, 0:n]
        if len(shape) == 2:
            ap = ap.rearrange("p (a b) -> p a b", a=shape[0])
        elif len(shape) == 3:
            ap = ap.rearrange("p (a b c) -> p a b c", a=shape[0], b=shape[1])
        return ap


def build(cfg=None):
    cfg = cfg or {}
    layers = cfg.get("layers", [0, 1])
    groups = cfg.get("groups", [0, 1])
    phases = cfg.get("phases", ["attn", "hy", "gdn", "merge", "ffn"])
    taps = cfg.get("taps", [])
    PL = cfg.get("pl", "pool")
    PLG = cfg.get("pl_gdn", "dve")
    nc = bass.Bass("TRN2", target_bir_lowering=False)
    S = Sched(nc)
    Dd = {}

    def din(name, shape, dt=F32):
        Dd[name] = nc.dram_tensor(name, list(shape), dt, kind="ExternalInput").ap()
        return Dd[name]

    def dout(name, shape, dt=F32):
        Dd[name] = nc.dram_tensor(name, list(shape), dt, kind="ExternalOutput").ap()
        return Dd[name]

    def dscr(name, shape, dt=F32):
        Dd[name] = nc.dram_tensor(name, list(shape), dt, kind="Internal").ap()
        return Dd[name]

    xT_d = din("xT", [2, D, T])
    cvec_d = din("cvec", [128, 16, 2])
    ckT_d = din("ckT", [NLAYER, 8, 128, 256])
    cv_d = din("cv", [NLAYER, 256, 1024])
    sf0_d = din("sf0", [NLAYER, 4, 128, 128])
    sb0_d = din("sb0", [NLAYER, 4, 128, 128])
    win_d = din("win_t", [NLAYER, NWIN, 128, D])
    wmod_d = din("wmod_t", [NLAYER, 96, 128, D])
    wp_d = din("wp_t", [NLAYER, 16, 128, D])
    wout_d = din("wout_t", [NLAYER, 16, 128, D])
    wup_d = din("wup_t", [NLAYER, 88, 128, D])
    wdn_d = din("wdn_t", [NLAYER, 4, 16, 128, 11 * 128])
    vecs_d = din("vecs", [128, NLAYER, NV])
    bm_d = din("bm", [NLAYER, 8, 8, 128, 640])
    hw1_d = din("hw1", [NLAYER, 33, 64])
    hw2_d = din("hw2", [NLAYER, 64, 64])
    hw3_d = din("hw3", [NLAYER, 64, 2048])
    hcol_d = din("hcol", [NLAYER, 64, 4])
    hdec_d = din("hdec", [NLAYER, 2048])
    gcol_d = din("gcol", [NLAYER, 40, 2])
    F4_d, F2_d, I2_d, zfT_d, tn_d = {}, {}, {}, {}, {}
    for L in (256, 1024):
        n = L // 128
        F4_d[L] = din(f"F4_{L}", [n, 128, 4, n, 128], BF16)
        F2_d[L] = din(f"F2_{L}", [n, 128, 2, n, 128], BF16)
        I2_d[L] = din(f"I2_{L}", [n, 128, 2, L], BF16)
        zfT_d[L] = din(f"zfT_{L}", [33, L])
        tn_d[L] = din(f"tn_{L}", [128, n])
    ident_d = din("ident", [128, 128])
    gmask_d = din("gmask", [4, 64, 8, 64])
    yT_d = dout("yT", [2, D, T])
    kT_o = dout("kT_out", [NLAYER, 1024, T])
    vT_o = dout("vT_out", [NLAYER, 1024, T])
    sf_o = dout("sf_out", [NLAYER, 4, 4, 128, 128])
    sb_o = dout("sb_out", [NLAYER, 4, 4, 128, 128])
    xscr = dscr("xscr", [D, T])
    hspec = dscr("hspec", [8, 128, 2, 1024])

    big = nc.alloc_sbuf_tensor("big", [128, 16384], F32)
    xT = big[:, :].rearrange("p (k t) -> p k t", k=16)
    hT = nc.alloc_sbuf_tensor("hT", [128, 16, T], BF16)
    ymix = nc.alloc_sbuf_tensor("ymix", [128, 16, T], BF16)
    wring = [nc.alloc_sbuf_tensor(f"wr{i}", [128, 16, 128], BF16) for i in range(NSLOT)]
    vecs = nc.alloc_sbuf_tensor("vecs_s", [128, NLAYER, NV], F32)
    modv = nc.alloc_sbuf_tensor("modv", [128, NLAYER, 96, 2], F32)
    modA = nc.alloc_sbuf_tensor("modA", [128, NLAYER, 2, 2, 16], F32)
    cvec = nc.alloc_sbuf_tensor("cvec_s", [128, 16, 2], F32)
    csil = nc.alloc_sbuf_tensor("csil", [128, 16, 2], BF16)
    ident = nc.alloc_sbuf_tensor("ident_s", [128, 128], F32)
    ones_bf = nc.alloc_sbuf_tensor("ones_bf", [128, 128], BF16)
    ones32 = nc.alloc_sbuf_tensor("ones32", [128, 128], F32)
    gmask = nc.alloc_sbuf_tensor("gmask_s", [64, 4, 8, 64], F32)
    EXC = 11264
    ex_t = nc.alloc_sbuf_tensor("ex", [128, EXC], F32)
    RB = Region(big, 16384)
    RX = Region(ex_t, EXC)

    def AA(shape, dtype=F32, parts=128):
        n = 1
        for s_ in shape:
            n *= s_
        nb = (n * (4 if dtype == F32 else 2) + 31) // 32 * 32
        reg = RB if RB.off + nb <= RB.ncols * 4 else RX
        return reg.alloc(shape, dtype, parts)

    PS = [nc.alloc_psum_tensor(f"ps{i}", [128, 512], F32) for i in range(8)]
    PSB = [Buf(f"ps{i}") for i in range(8)]
    st = {"ps": 0, "w": 0, "wi": 0}

    Bx, Bh, By, Bvec, Bmod, Bcs, Bconst, Bxscr, Bhspec = (Buf(n) for n in
                                                           ("x", "h", "ymix", "vecs", "mod", "csil", "const", "xscr", "hspec"))
    wB = [Buf(f"w{i}") for i in range(NSLOT)]
    for i in range(NSLOT):
        S.dma_sem(f"w{i}")

    def psn():
        i = st["ps"] % 8
        st["ps"] += 1
        return i

    def mm(out, lhsT, rhs, start, stop, r, w):
        S.op("pe", lambda h: h.matmul(out, lhsT, rhs, start=start, stop=stop), r=r, w=w)

    def tr(out, in_, idn, r, w):
        S.op("pe", lambda h: h.transpose(out, in_, idn), r=r + [Bconst], w=w)

    def act(out, in_, func, r, w, bias=None, scale=None):
        kw = {}
        if bias is not None:
            kw["bias"] = bias
        if scale is not None:
            kw["scale"] = scale
        S.op("act", lambda h: h.activation(out=out, in_=in_, func=func, **kw), r=r, w=w)

    def tt(eng, out, in0, in1, op, r, w):
        S.op(eng, lambda h: h.tensor_tensor(out=out, in0=in0, in1=in1, op=op), r=r, w=w)

    def ts(eng, out, in0, s1, s2, op0, op1, r, w):
        if op1 is None:
            S.op(eng, lambda h: h.tensor_scalar(out=out, in0=in0, scalar1=s1, scalar2=None, op0=op0), r=r, w=w)
        else:
            S.op(eng, lambda h: h.tensor_scalar(out=out, in0=in0, scalar1=s1, scalar2=s2, op0=op0, op1=op1), r=r, w=w)

    def stt(eng, out, in0, scalar, in1, op0, op1, r, w):
        S.op(eng, lambda h: h.scalar_tensor_tensor(out=out, in0=in0, scalar=scalar, in1=in1, op0=op0, op1=op1), r=r, w=w)

    def cp(eng, out, in_, r, w):
        if eng == "act":
            act(out, in_, AF.Copy, r, w)
        else:
            S.op(eng, lambda h: h.tensor_copy(out=out, in_=in_), r=r, w=w)

    def red(eng, out, in_, op, r, w):
        S.op(eng, lambda h: h.tensor_reduce(out=out, in_=in_, axis=AX.X, op=op), r=r, w=w)

    def recip(eng, out, in_, r, w):
        S.op(eng, lambda h: h.reciprocal(out=out, in_=in_), r=r, w=w)

    def memset(eng, ap, val, w):
        S.op(eng, lambda h: h.memset(ap, val), w=w)

    def dma(eng, out, in_, r, w, sem):
        S.dma_sem(sem)
        S.op(eng, lambda h: h.dma_start(out=out, in_=in_), r=r, w=w, dma=sem)

    wplan = cfg.get("_wplan")
    wkeys = []
    LOOKAHEAD = NSLOT - 1

    def _wsrc(key):
        ap = Dd[key[0]]
        for ix in key[1:]:
            ap = ap[ix]
        return ap

    def _wissue(i, key, kct):
        t = wring[i % NSLOT]
        dma("pool", t[:, 0:kct, :], _wsrc(key).rearrange("p (k n) -> p k n", n=128), [], [wB[i % NSLOT]], f"w{i % NSLOT}")

    def wget(key, kct=16):
        i = st["w"]
        st["w"] += 1
        wkeys.append((key, kct))
        if wplan is None:
            _wissue(i, key, kct)
        else:
            assert wplan[i] == (key, kct), (i, wplan[i], key, kct)
            while st["wi"] <= min(i + LOOKAHEAD, len(wplan) - 1):
                k2, c2 = wplan[st["wi"]]
                _wissue(st["wi"], k2, c2)
                st["wi"] += 1
        return wring[i % NSLOT], wB[i % NSLOT]

    def proj(wt, wb, rhs_t, rhs_b, kcs=range(16)):
        kcs = list(kcs)
        pis = []
        for tb in range(2):
            pi = psn()
            for n_, kc in enumerate(kcs):
                mm(PS[pi][:, :], wt[:, kc, :], rhs_t[:, kc, tb * 512:(tb + 1) * 512], n_ == 0, n_ == len(kcs) - 1,
                   [wb, rhs_b], [PSB[pi]])
            pis.append(pi)
        return pis

    def tap(name, src_ap, shape, dt, rb):
        if name in taps:
            d = dout("tap_" + name, shape, dt)
            dma("sp", d, src_ap, rb, [], "tap")

    dma("sp", vecs[:, :, :], vecs_d, [], [Bvec], "c0")
    dma("sp", cvec[:, :, :], cvec_d, [], [Bcs], "c1")
    dma("sp", ident[:, :], ident_d, [], [Bconst], "c2")
    dma("sp", gmask[:, :, :, :], gmask_d.rearrange("m p e f -> p m e f"), [], [Bconst], "c2")
    memset("dve", ones_bf[:, :], 1.0, [Bconst])
    memset("dve", ones32[:, :], 1.0, [Bconst])
    act(csil[:, :, :], cvec[:, :, :], AF.Silu, [Bcs], [Bcs])
    if cfg.get("zero_ymix"):
        memset(PL, ymix[:, :, :], 0.0, [By])

    for l in layers:
        pi = psn()
        for j in range(96):
            wt, wb = wget(("wmod_t", l, j))
            for kc in range(16):
                mm(PS[pi][:, 2 * j:2 * j + 2], wt[:, kc, :], csil[:, kc, :], kc == 0, kc == 15, [wb, Bcs], [PSB[pi]])
        psv = PS[pi][:, 0:192].rearrange("p (j g) -> p j g", g=2)
        for g in range(2):
            tt("dve", modv[:, l, :, g], psv[:, :, g], vecs[:, l, V_BMOD:V_BMOD + 96], ALU.add, [PSB[pi], Bvec], [Bmod])
        for g in range(2):
            stt("dve", modA[:, l, 0, g, :], modv[:, l, 16:32, g], 1.0, vecs[:, l, V_LN1:V_LN1 + 16], ALU.add, ALU.mult,
                [Bmod, Bvec], [Bmod])
            stt("dve", modA[:, l, 1, g, :], modv[:, l, 64:80, g], 1.0, vecs[:, l, V_LN2:V_LN2 + 16], ALU.add, ALU.mult,
                [Bmod, Bvec], [Bmod])

    def phase_norm(l, g, which):
        RX.reset()
        sq = [RX.alloc([512], BF16) for _ in range(2)]
        Bsq = [Buf("sq0"), Buf("sq1")]
        tmp = [RX.alloc([512]) for _ in range(2)]
        Btmp = [Buf("tmp0"), Buf("tmp1")]
        rs = RX.alloc([512])
        Brs = Buf("rs")
        shift0 = 0 if which == 0 else 48
        for tb in range(2):
            sl = slice(tb * 512, (tb + 1) * 512)
            pi = psn()
            for kc in range(16):
                act(sq[kc % 2], xT[:, kc, sl], AF.Square, [Bx], [Bsq[kc % 2]])
                mm(PS[pi][:, :], ones_bf[:, :], sq[kc % 2], kc == 0, kc == 15, [Bsq[kc % 2], Bconst], [PSB[pi]])
            act(rs, PS[pi][:, :], AF.Sqrt, [PSB[pi]], [Brs], bias=EPS, scale=1.0 / D)
            recip("dve", rs, rs, [Brs], [Brs])
            for kc in range(16):
                tt("dve", tmp[kc % 2], xT[:, kc, sl], rs, ALU.mult, [Bx, Brs], [Btmp[kc % 2]])
                act(hT[:, kc, sl], tmp[kc % 2], AF.Identity, [Btmp[kc % 2], Bmod], [Bh],
                    bias=modv[:, l, shift0 + kc:shift0 + kc + 1, g], scale=modA[:, l, which, g, kc:kc + 1])
        S.barrier()

    def phase_final(g):
        RX.reset()
        sq = [RX.alloc([512], BF16) for _ in range(2)]
        Bsq = [Buf("fsq0"), Buf("fsq1")]
        tmp = [RX.alloc([512]) for _ in range(2)]
        Btmp = [Buf("ftmp0"), Buf("ftmp1")]
        o32 = [RX.alloc([512]) for _ in range(2)]
        Bo = [Buf("fo0"), Buf("fo1")]
        rs = RX.alloc([512])
        Brs = Buf("frs")
        for tb in range(2):
            sl = slice(tb * 512, (tb + 1) * 512)
            pi = psn()
            for kc in range(16):
                act(sq[kc % 2], xT[:, kc, sl], AF.Square, [Bx], [Bsq[kc % 2]])
                mm(PS[pi][:, :], ones_bf[:, :], sq[kc % 2], kc == 0, kc == 15, [Bsq[kc % 2], Bconst], [PSB[pi]])
            act(rs, PS[pi][:, :], AF.Sqrt, [PSB[pi]], [Brs], bias=EPS, scale=1.0 / D)
            recip("dve", rs, rs, [Brs], [Brs])
            for kc in range(16):
                tt("dve", tmp[kc % 2], xT[:, kc, sl], rs, ALU.mult, [Bx, Brs], [Btmp[kc % 2]])
                act(o32[kc % 2], tmp[kc % 2], AF.Identity, [Btmp[kc % 2], Bvec], [Bo[kc % 2]],
                    scale=vecs[:, 0, V_FING + kc:V_FING + kc + 1])
                dma("sp", yT_d[g, kc * 128:(kc + 1) * 128, sl], o32[kc % 2], [Bo[kc % 2]], [], f"yo{kc % 2}")
        S.barrier()

    def phase_attn(l, g):
        RB.reset()
        RX.reset()
        qT = RB.alloc([T], BF16)
        kTb = RB.alloc([T], BF16)
        k32 = RB.alloc([T])
        v32 = RB.alloc([T])
        vtok = RB.alloc([8, 128], BF16)
        BqT, BkTb, Bk32, Bv32, Bvtok = (Buf(n) for n in ("qT", "kTb", "k32", "v32", "vtok"))
        NW = 3
        if g == 0:
            P32 = [RB.alloc([256]) for _ in range(NW)]
            PT = [RB.alloc([2, 128], BF16) for _ in range(NW)]
        else:
            P32 = [RB.alloc([896]) for _ in range(NW)]
            PT = [RB.alloc([7, 128], BF16) for _ in range(NW)]
            bmt = [RB.alloc([640]) for _ in range(2)]
            Bbm = [Buf("bm0"), Buf("bm1")]
            ckTb = RB.alloc([256], BF16)
            cvb = RB.alloc([2, 128], BF16)
            Bck, Bcvb = Buf("ckTb"), Buf("cvb")
        sm = [RB.alloc([4]) for _ in range(NW)]
        BP = [Buf(f"P{i}") for i in range(NW)]
        BPT = [Buf(f"PT{i}") for i in range(NW)]
        Bsm = [Buf(f"sm{i}") for i in range(NW)]
        inst = 0
        for h in range(cfg.get("nheads", 8)):
            if cfg.get("attn_stop", 9) <= 0:
                continue
            wt, wb = wget(("win_t", l, TQ + h))
            pis = proj(wt, wb, hT, Bh)
            for tb in range(2):
                act(qT[:, tb * 512:(tb + 1) * 512], PS[pis[tb]][:, :], AF.Identity, [PSB[pis[tb]]], [BqT], scale=DH ** -0.5)
            if cfg.get("parts", 9) <= 1:
                continue
            wt, wb = wget(("win_t", l, TK + h))
            pis = proj(wt, wb, hT, Bh)
            for tb in range(2):
                sl = slice(tb * 512, (tb + 1) * 512)
                if g == 0:
                    act(k32[:, sl], PS[pis[tb]][:, :], AF.Copy, [PSB[pis[tb]]], [Bk32])
                    cp("dve", kTb[:, sl], k32[:, sl], [Bk32], [BkTb])
                else:
                    cp("dve", kTb[:, sl], PS[pis[tb]][:, :], [PSB[pis[tb]]], [BkTb])
            if g == 0 and not cfg.get("no_kvout"):
                dma("sp", kT_o[l, h * 128:(h + 1) * 128, :], k32, [Bk32], [], "ko")
            if cfg.get("parts", 9) <= 2:
                continue
            wt, wb = wget(("win_t", l, TV + h))
            pis = proj(wt, wb, hT, Bh)
            for tb in range(2):
                act(v32[:, tb * 512:(tb + 1) * 512], PS[pis[tb]][:, :], AF.Copy, [PSB[pis[tb]]], [Bv32])
            if g == 0 and not cfg.get("no_kvout"):
                dma("sp", vT_o[l, h * 128:(h + 1) * 128, :], v32, [Bv32], [], "vo")
            if cfg.get("attn_stop", 9) <= 1:
                continue
            for half in range(2):
                pi = psn()
                for j in range(4):
                    t_ = half * 4 + j
                    tr(PS[pi][:, j * 128:(j + 1) * 128], v32[:, t_ * 128:(t_ + 1) * 128], ident[:, :], [Bv32], [PSB[pi]])
                cp("dve", vtok[:, half * 4:(half + 1) * 4, :], PS[pi][:, :].rearrange("p (a b) -> p a b", a=4),
                   [PSB[pi]], [Bvtok])
            if g == 1:
                dma("pool", ckTb, ckT_d[l, h], [], [Bck], "ck")
                dma("pool", cvb, cv_d[l, :, h * 128:(h + 1) * 128].rearrange("(c p) d -> p c d", p=128), [], [Bcvb], "cvb")
            if cfg.get("attn_stop", 9) <= 2:
                continue
            nq = 8
            for qi in range(nq):
                w_ = inst % NW
                inst += 1
                q0 = qi * 128
                if g == 0:
                    b = qi // 2
                    nk = 256
                    pi = psn()
                    mm(PS[pi][:, 0:256], qT[:, q0:q0 + 128], kTb[:, b * 256:(b + 1) * 256], True, True,
                       [BqT, BkTb], [PSB[pi]])
                    src = PS[pi][:, 0:256]
                    srcb = [PSB[pi]]
                else:
                    nk = 896
                    ks = KS_TOK[qi]
                    bi = (h * 8 + qi) % 2
                    dma("sp", bmt[bi], bm_d[l, h, qi], [], [Bbm[bi]], f"bm{bi}")
                    pa = psn()
                    mm(PS[pa][:, :], qT[:, q0:q0 + 128], kTb[:, ks:ks + 512], True, True, [BqT, BkTb], [PSB[pa]])
                    pb = psn()
                    mm(PS[pb][:, 0:128], qT[:, q0:q0 + 128], kTb[:, ks + 512:ks + 640], True, True, [BqT, BkTb], [PSB[pb]])
                    mm(PS[pb][:, 128:384], qT[:, q0:q0 + 128], ckTb, True, True, [BqT, Bck], [PSB[pb]])
                    tt("dve", P32[w_][:, 0:512], PS[pa][:, :], bmt[bi][:, 0:512], ALU.add, [PSB[pa], Bbm[bi]], [BP[w_]])
                    tt("dve", P32[w_][:, 512:640], PS[pb][:, 0:128], bmt[bi][:, 512:640], ALU.add, [PSB[pb], Bbm[bi]], [BP[w_]])
                    cp("dve", P32[w_][:, 640:896], PS[pb][:, 128:384], [PSB[pb]], [BP[w_]])
                    src = P32[w_]
                    srcb = [BP[w_]]
                red("dve", sm[w_][:, 0:1], src, ALU.max, srcb, [Bsm[w_]])
                ts("dve", sm[w_][:, 1:2], sm[w_][:, 0:1], -1.0, None, ALU.mult, None, [Bsm[w_]], [Bsm[w_]])
                act(P32[w_], src, AF.Exp, srcb + [Bsm[w_]], [BP[w_]], bias=sm[w_][:, 1:2])
                if cfg.get("attn_stop", 9) <= 3:
                    continue
                red("dve", sm[w_][:, 2:3], P32[w_], ALU.add, [BP[w_]], [Bsm[w_]])
                recip("dve", sm[w_][:, 3:4], sm[w_][:, 2:3], [Bsm[w_]], [Bsm[w_]])
                ts(PL, P32[w_], P32[w_], sm[w_][:, 3:4], None, ALU.mult, None, [BP[w_], Bsm[w_]], [BP[w_]])
                if cfg.get("attn_stop", 9) <= 4:
                    continue
                nchunk = nk // 128
                c = 0
                while c < nchunk:
                    n_ = min(4, nchunk - c)
                    pi2 = psn()
                    for j in range(n_):
                        tr(PS[pi2][:, j * 128:(j + 1) * 128], P32[w_][:, (c + j) * 128:(c + j + 1) * 128], ident[:, :],
                           [BP[w_]], [PSB[pi2]])
                    act(PT[w_][:, c:c + n_, :], PS[pi2][:, 0:n_ * 128].rearrange("p (a b) -> p a b", a=n_), AF.Copy,
                        [PSB[pi2]], [BPT[w_]])
                    c += n_
                pi3 = psn()
                if g == 0:
                    for c in range(2):
                        mm(PS[pi3][:, 0:128], vtok[:, b * 2 + c, :], PT[w_][:, c, :], c == 0, c == 1,
                           [Bvtok, BPT[w_]], [PSB[pi3]])
                else:
                    for c in range(5):
                        mm(PS[pi3][:, 0:128], vtok[:, ks // 128 + c, :], PT[w_][:, c, :], c == 0, False,
                           [Bvtok, BPT[w_]], [PSB[pi3]])
                    for c in range(2):
                        mm(PS[pi3][:, 0:128], cvb[:, c, :], PT[w_][:, 5 + c, :], False, c == 1,
                           [Bcvb, BPT[w_]], [PSB[pi3]])
                cp("dve", ymix[:, h, q0:q0 + 128], PS[pi3][:, 0:128], [PSB[pi3]], [By])
        S.barrier()

    PI = math.pi

    def phase_hyprep(l, L):
        RB.reset()
        RX.reset()
        n = L // 128
        nblk = max(1, L // 512)
        bw = min(L, 512)
        zfT = RB.alloc([L], parts=33)
        w1 = RB.alloc([64], parts=33)
        w2 = RB.alloc([64], parts=64)
        w3 = RB.alloc([2048], parts=64)
        hcol = RB.alloc([4], parts=64)
        fb = RB.alloc([2], parts=64)
        tn = RB.alloc([n])
        h1T = RB.alloc([L], parts=64)
        h2T = RB.alloc([L], parts=64)
        a_ = RB.alloc([bw], parts=64)
        m1 = RB.alloc([bw], parts=64)
        m2 = RB.alloc([bw], parts=64)
        absdec = RB.alloc([2048])
        env = [RB.alloc([512]) for _ in range(2)]
        F4 = [RB.alloc([4, n, 128], BF16) for _ in range(2)]
        hsp = [RX.alloc([2, 2, 256]) for _ in range(2)]
        filt = ymix[:, :, :].rearrange("p a b -> p (a b)")[:, 0:n * 2048].rearrange("p (m c) -> p m c", m=n)
        Bc, Bh1, Bh2, Ba, Bm1, Bm2, Bad = (Buf(x) for x in ("hyc", "h1T", "h2T", "a_", "m1", "m2", "absdec"))
        Benv = [Buf("env0"), Buf("env1")]
        BF4 = [Buf("F40"), Buf("F41")]
        Bhsp = [Buf("hsp0"), Buf("hsp1")]
        dma("sp", zfT, zfT_d[L], [], [Bc], "hy0")
        dma("sp", w1, hw1_d[l], [], [Bc], "hy0")
        dma("sp", w2, hw2_d[l], [], [Bc], "hy0")
        dma("sp", w3, hw3_d[l], [], [Bc], "hy0")
        dma("sp", hcol, hcol_d[l], [], [Bc], "hy0")
        dma("sp", tn, tn_d[L], [], [Bc], "hy0")
        dma("sp", absdec, hdec_d[l].partition_broadcast(128), [], [Bad], "hy1")
        act(absdec, absdec, AF.Abs, [Bad], [Bad])
        tt("dve", fb[:, 0:1], hcol[:, 0:1], hcol[:, 2:3], ALU.mult, [Bc], [Bc])
        tt("dve", fb[:, 1:2], hcol[:, 1:2], hcol[:, 3:4], ALU.mult, [Bc], [Bc])

        def sin_layer(dst, Bdst, lhsT, K, src, Bsrc, fcol, bcol):
            for blk in range(nblk):
                sl = slice(blk * bw, (blk + 1) * bw)
                pi = psn()
                mm(PS[pi][0:64, 0:bw], lhsT, src[0:K, sl], True, True, [Bc] + Bsrc, [PSB[pi]])
                ts("dve", a_, PS[pi][0:64, 0:bw], hcol[:, fcol:fcol + 1], fb[:, bcol:bcol + 1], ALU.mult, ALU.add,
                   [PSB[pi], Bc], [Ba])
                ts("dve", m1, a_, -PI, 2 * PI, ALU.is_lt, ALU.mult, [Ba], [Bm1])
                ts(PL, m2, a_, PI, -2 * PI, ALU.is_gt, ALU.mult, [Ba], [Bm2])
                tt("dve", a_, a_, m1, ALU.add, [Ba, Bm1], [Ba])
                tt("dve", a_, a_, m2, ALU.add, [Ba, Bm2], [Ba])
                act(dst[:, sl], a_, AF.Sin, [Ba], [Bdst])

        sin_layer(h1T, Bh1, w1[0:33, :], 33, zfT, [], 2, 0)
        sin_layer(h2T, Bh2, w2[0:64, :], 64, h1T, [Bh1], 3, 1)
        k_ = 0
        for mc in range(n):
            for nb in range(4):
                e_ = k_ % 2
                k_ += 1
                pi = psn()
                mm(PS[pi][:, :], h2T[:, mc * 128:(mc + 1) * 128], w3[:, nb * 512:(nb + 1) * 512], True, True,
                   [Bh2, Bc], [PSB[pi]])
                act(env[e_], absdec[:, nb * 512:(nb + 1) * 512], AF.Exp, [Bad, Bc], [Benv[e_]], scale=tn[:, mc:mc + 1])
                tt("dve", filt[:, mc, nb * 512:(nb + 1) * 512], PS[pi][:, :], env[e_], ALU.mult,
                   [PSB[pi], Benv[e_]], [By])
        for kf in range(n):
            f_ = kf % 2
            dma("sp", F4[f_], F4_d[L][kf], [], [BF4[f_]], f"F4{f_}")
            for o in range(2):
                e_ = (kf * 2 + o) % 2
                pr = psn()
                pim = psn()
                for (pp, ia, ib) in ((pr, 0, 1), (pim, 2, 3)):
                    for mc in range(n):
                        mm(PS[pp][:, :], F4[f_][:, ia, mc, :], filt[:, mc, o * 1024:o * 1024 + 512], mc == 0, False,
                           [BF4[f_], By], [PSB[pp]])
                    for mc in range(n):
                        mm(PS[pp][:, :], F4[f_][:, ib, mc, :], filt[:, mc, o * 1024 + 512:o * 1024 + 1024], False, mc == n - 1,
                           [BF4[f_], By], [PSB[pp]])
                act(hsp[e_][:, :, 0, :], PS[pr][:, :].rearrange("p (a b) -> p a b", a=2), AF.Copy, [PSB[pr]], [Bhsp[e_]])
                cp("dve", hsp[e_][:, :, 1, :], PS[pim][:, :].rearrange("p (a b) -> p a b", a=2), [PSB[pim]], [Bhsp[e_]])
                dma("sp", hspec[kf, :, o, :], hsp[e_].rearrange("p a b c -> p (a b c)"), [Bhsp[e_]], [Bhspec], f"hso{e_}")
        S.barrier()

    def conv3(u3, Bu, nseq, Ls, wcols, bias_col, dst, Bdst, t1, t2, Bt1, Bt2, final_eng="dve", final_func=None):
        w0, w1_, w2_ = wcols
        if bias_col is not None:
            act(t1, u3[:, :, 0:Ls], AF.Identity, [Bu, Bvec], [Bt1], bias=bias_col, scale=w0)
        else:
            act(t1, u3[:, :, 0:Ls], AF.Identity, [Bu, Bvec], [Bt1], scale=w0)
        stt("dve", t2, u3[:, :, 1:Ls + 1], w1_, t1, ALU.mult, ALU.add, [Bu, Bvec, Bt1], [Bt2])
        if final_func is None:
            stt(final_eng, dst, u3[:, :, 2:Ls + 2], w2_, t2, ALU.mult, ALU.add, [Bu, Bvec, Bt2], [Bdst])
        else:
            stt(final_eng, t1, u3[:, :, 2:Ls + 2], w2_, t2, ALU.mult, ALU.add, [Bu, Bvec, Bt2], [Bt1])
            act(dst, t1, final_func, [Bt1], [Bdst])

    def evac_pad(u3, Bu, pis, nseq, Ls):
        for tb in range(2):
            if nseq == 4:
                act(u3[:, 2 * tb:2 * tb + 2, 1:Ls + 1], PS[pis[tb]][:, :].rearrange("p (a b) -> p a b", a=2), AF.Copy,
                    [PSB[pis[tb]]], [Bu])
            else:
                act(u3[:, 0:1, 1 + tb * 512:1 + (tb + 1) * 512], PS[pis[tb]][:, :].rearrange("p (a b) -> p a b", a=1),
                    AF.Copy, [PSB[pis[tb]]], [Bu])

    def phase_hyena(l, g):
        RB.reset()
        RX.reset()
        nseq, Ls = (4, 256) if g == 0 else (1, 1024)
        n = Ls // 128
        upad = AA([nseq * (Ls + 2)])
        u3 = upad.rearrange("p (s l) -> p s l", s=nseq)
        Bu = Buf("upad")
        memset(PL, upad, 0.0, [Bu])
        t1 = AA([nseq, Ls])
        t2 = AA([nseq, Ls])
        Bt1, Bt2 = Buf("t1"), Buf("t2")
        z = AA([2, T])
        x1 = AA([2, T], BF16)
        x2 = AA([2, T], BF16)
        zt = AA([8, 256], BF16)
        Yr = AA([8, 256], BF16)
        Wv = AA([8, 256], BF16)
        zcs = [AA([512]) for _ in range(2)]
        HH = [AA([512]) for _ in range(2)]
        HH2 = [AA([512]) for _ in range(2)]
        m12 = [AA([512]) for _ in range(2)]
        m34 = [AA([512]) for _ in range(2)]
        F2 = [AA([2, n, 128], BF16) for _ in range(2)]
        I2 = [AA([2, Ls], BF16) for _ in range(2)]
        tmpo = [AA([512]) for _ in range(2)]
        Bz, Bx1, Bx2, Bzt, BYr, BW = (Buf(x) for x in ("z", "x1", "x2", "zt", "Yr", "W"))
        Bzcs = [Buf("zcs0"), Buf("zcs1")]
        BHH = [Buf("HH0"), Buf("HH1")]
        BHH2 = [Buf("HH20"), Buf("HH21")]
        Bm12 = [Buf("m120"), Buf("m121")]
        Bm34 = [Buf("m340"), Buf("m341")]
        BF2 = [Buf("F20"), Buf("F21")]
        BI2 = [Buf("I20"), Buf("I21")]
        Btmpo = [Buf("tmpo0"), Buf("tmpo1")]
        for hf in range(2):
            for part in range(3):
                for cc in range(2):
                    ch = part * 4 + 2 * hf + cc
                    wt, wb = wget(("win_t", l, THY + ch))
                    pis = proj(wt, wb, hT, Bh)
                    evac_pad(u3, Bu, pis, nseq, Ls)
                    wc = [vecs[:, l, V_HCW + tap * 12 + ch:V_HCW + tap * 12 + ch + 1] for tap in range(3)]
                    bc = vecs[:, l, V_HCB + ch:V_HCB + ch + 1]
                    if part == 0:
                        dst, Bd = z[:, cc, :], Bz
                    elif part == 1:
                        dst, Bd = x1[:, cc, :], Bx1
                    else:
                        dst, Bd = x2[:, cc, :], Bx2
                    conv3(u3, Bu, nseq, Ls, wc, bc, dst.rearrange("p (s l) -> p s l", s=nseq), Bd, t1, t2, Bt1, Bt2)
            for o in range(2):
                for t2_ in range(0, 8, 2):
                    pi = psn()
                    for dt_ in range(2):
                        for cc in range(2):
                            tr(PS[pi][:, dt_ * 256 + cc * 128:dt_ * 256 + (cc + 1) * 128],
                               z[:, cc, (t2_ + dt_) * 128:(t2_ + dt_ + 1) * 128], ident[:, :], [Bz], [PSB[pi]])
                    act(zt[:, t2_:t2_ + 2, :], PS[pi][:, :].rearrange("p (a b) -> p a b", a=2), AF.Copy, [PSB[pi]], [Bzt])
                k_ = 0
                for kf in range(n):
                    f_ = kf % 2
                    dma("sp", F2[f_], F2_d[Ls][kf], [], [BF2[f_]], f"F2{f_}")
                    hsl = hspec[kf, :, o, :]
                    dma("sp", HH[f_], hsl[:, hf * 512:(hf + 1) * 512], [Bhspec], [BHH[f_]], f"HH{f_}")
                    dma("sp", HH2[f_][:, 0:256], hsl[:, hf * 512 + 256:hf * 512 + 512], [Bhspec], [BHH2[f_]], f"HHb{f_}")
                    dma("sp", HH2[f_][:, 256:512], hsl[:, hf * 512:hf * 512 + 256], [Bhspec], [BHH2[f_]], f"HHb{f_}")
                    for s in range(nseq):
                        e_ = k_ % 2
                        k_ += 1
                        pi = psn()
                        for half, im in ((0, 0), (1, 1)):
                            for mc in range(n):
                                mm(PS[pi][:, half * 256:(half + 1) * 256], F2[f_][:, im, mc, :], zt[:, s * n + mc, :],
                                   mc == 0, mc == n - 1, [BF2[f_], Bzt], [PSB[pi]])
                        act(zcs[e_], PS[pi][:, :], AF.Copy, [PSB[pi]], [Bzcs[e_]])
                        tt("dve", m12[e_], zcs[e_], HH[f_], ALU.mult, [Bzcs[e_], BHH[f_]], [Bm12[e_]])
                        tt(PL, m34[e_], zcs[e_], HH2[f_], ALU.mult, [Bzcs[e_], BHH2[f_]], [Bm34[e_]])
                        tt("dve", Yr[:, s * n + kf, :], m12[e_][:, 0:256], m12[e_][:, 256:512], ALU.add, [Bm12[e_]], [BYr])
                        tt(PL, Wv[:, s * n + kf, :], m34[e_][:, 256:512], m34[e_][:, 0:256], ALU.subtract, [Bm34[e_]], [BW])
                acc = [psn() for _ in range(4)]
                if g == 1:
                    for kf in range(n):
                        f_ = kf % 2
                        dma("sp", I2[f_], I2_d[Ls][kf], [], [BI2[f_]], f"I2{f_}")
                        for cc in range(2):
                            for j in range(2):
                                pi = acc[cc * 2 + j]
                                for im, src_, Bs in ((0, Yr, BYr), (1, Wv, BW)):
                                    mm(PS[pi][:, :], src_[:, kf, cc * 128:(cc + 1) * 128], I2[f_][:, im, j * 512:(j + 1) * 512],
                                       kf == 0 and im == 0, kf == n - 1 and im == 1, [Bs, BI2[f_]], [PSB[pi]])
                else:
                    for kf in range(n):
                        dma("sp", I2[kf], I2_d[Ls][kf], [], [BI2[kf]], f"I2{kf}")
                    for cc in range(2):
                        for j in range(2):
                            pi = acc[cc * 2 + j]
                            for s2 in range(2):
                                s = j * 2 + s2
                                for kf in range(n):
                                    for im, src_, Bs in ((0, Yr, BYr), (1, Wv, BW)):
                                        mm(PS[pi][:, s2 * 256:(s2 + 1) * 256], src_[:, s * n + kf, cc * 128:(cc + 1) * 128],
                                           I2[kf][:, im, :], kf == 0 and im == 0, kf == n - 1 and im == 1,
                                           [Bs, BI2[kf]], [PSB[pi]])
                k_ = 0
                for cc in range(2):
                    for j in range(2):
                        pi = acc[cc * 2 + j]
                        e_ = k_ % 2
                        k_ += 1
                        sl = slice(j * 512, (j + 1) * 512)
                        sk = vecs[:, l, V_HSKIP + o * 4 + 2 * hf + cc:V_HSKIP + o * 4 + 2 * hf + cc + 1]
                        stt("dve", tmpo[e_], z[:, cc, sl], sk, PS[pi][:, :], ALU.mult, ALU.add, [Bz, Bvec, PSB[pi]], [Btmpo[e_]])
                        if o == 0:
                            tt(PL, z[:, cc, sl], tmpo[e_], x1[:, cc, sl], ALU.mult, [Btmpo[e_], Bx1], [Bz])
                        else:
                            tt(PL, ymix[:, 8 + 2 * hf + cc, sl], tmpo[e_], x2[:, cc, sl], ALU.mult, [Btmpo[e_], Bx2], [By])
        S.barrier()

    def phase_gdn(l, g):
        RB.reset()
        RX.reset()

        def A(shape, dtype=F32, parts=128):
            n = 1
            for s_ in shape:
                n *= s_
            nb = (n * (4 if dtype == F32 else 2) + 31) // 32 * 32
            reg = RB if RB.off + nb <= RB.ncols * 4 else RX
            return reg.alloc(shape, dtype, parts)

        nseq, Ls = (4, 256) if g == 0 else (1, 1024)
        cps = Ls // 64
        upad = A([nseq * (Ls + 2)])
        u3 = upad.rearrange("p (s l) -> p s l", s=nseq)
        Bu = Buf("gupad")
        memset(PLG, upad, 0.0, [Bu])
        t1f = A([T])
        t2f = A([T])
        t1 = t1f.rearrange("p (s l) -> p s l", s=nseq)
        t2 = t2f.rearrange("p (s l) -> p s l", s=nseq)
        Bt1, Bt2 = Buf("gt1"), Buf("gt2")
        bg = A([T], parts=40)
        gcl = A([2], parts=40)
        nA = A([1], parts=40)
        bgt = A([16, 40], parts=64)
        gc = A([16, 8], parts=64)
        egc = A([16, 8], parts=64)
        kdf = A([16, 8], parts=64)
        begc = A([16, 8], parts=64)
        gtot = A([16, 8])
        Bbg, Bgs = Buf("bg"), Buf("gsmall")
        memset(PLG, bg, 0.0, [Bbg])
        dma("sp", gcl, gcol_d[l], [], [Bgs], "g0")
        act(nA[32:40, :], gcl[32:40, 0:1], AF.Exp, [Bgs], [Bgs])
        ts("dve", nA[32:40, :], nA[32:40, :], -1.0, None, ALU.mult, None, [Bgs], [Bgs])
        wt, wb = wget(("win_t", l, TBA))
        pis = proj(wt, wb, hT, Bh)
        e1 = t1f
        for tb in range(2):
            sl = slice(tb * 512, (tb + 1) * 512)
            pi = pis[tb]
            act(bg[0:8, sl], PS[pi][0:8, :], AF.Sigmoid, [PSB[pi]], [Bbg])
            act(e1[32:40, sl], PS[pi][32:40, :], AF.Exp, [PSB[pi], Bgs], [Bt1], bias=gcl[32:40, 1:2])
            act(e1[32:40, sl], e1[32:40, sl], AF.Ln, [Bt1], [Bt1], bias=1.0)
            ts("dve", bg[32:40, sl], e1[32:40, sl], nA[32:40, 0:1], None, ALU.mult, None, [Bt1, Bgs], [Bbg])
        for half in range(2):
            pi = psn()
            for j in range(8):
                ci = half * 8 + j
                tr(PS[pi][0:64, j * 40:(j + 1) * 40], bg[0:40, ci * 64:(ci + 1) * 64], ident[0:40, 0:40], [Bbg], [PSB[pi]])
            cp("dve", bgt[:, half * 8:(half + 1) * 8, :], PS[pi][0:64, 0:320].rearrange("p (a b) -> p a b", a=8),
               [PSB[pi]], [Bgs])
        g8 = A([16, 8], parts=64)
        gF4 = A([16, 4], parts=64)
        gB4 = A([16, 4], parts=64)
        cp("dve", g8, bgt[:, :, 32:40], [Bgs], [Bgs])
        cp("dve", gF4, bgt[:, :, 32:36], [Bgs], [Bgs])
        cp("dve", gB4, bgt[:, :, 36:40], [Bgs], [Bgs])
        pi = psn()
        mm(PS[pi][0:64, 0:64], gmask[:, 0, 0, :], gF4.rearrange("p a b -> p (a b)"), True, True, [Bconst, Bgs], [PSB[pi]])
        mm(PS[pi][0:64, 64:128], gmask[:, 1, 0, :], gB4.rearrange("p a b -> p (a b)"), True, True, [Bconst, Bgs], [PSB[pi]])
        cp("dve", gc[:, :, 0:4], PS[pi][0:64, 0:64].rearrange("p (a b) -> p a b", b=4), [PSB[pi]], [Bgs])
        cp("dve", gc[:, :, 4:8], PS[pi][0:64, 64:128].rearrange("p (a b) -> p a b", b=4), [PSB[pi]], [Bgs])
        pi2 = psn()
        ptot = PS[pi2][:, 0:128].rearrange("p (a b) -> p a b", b=8)
        mm(PS[pi2][:, 0:128], ones32[0:64, :], g8.rearrange("p a b -> p (a b)"), True, True, [Bconst, Bgs], [PSB[pi2]])
        act(egc, gc, AF.Exp, [Bgs], [Bgs])
        cp("dve", gtot, ptot, [PSB[pi2]], [Bgs])
        tt("dve", kdf, gtot[0:64], gc, ALU.subtract, [Bgs], [Bgs])
        act(kdf, kdf, AF.Exp, [Bgs], [Bgs])
        act(gtot, gtot, AF.Exp, [Bgs], [Bgs])
        tt("dve", begc, bgt[:, :, 0:8], egc, ALU.mult, [Bgs], [Bgs])

        QT = A([T])
        KT = A([T])
        VT = A([T])
        szT = A([T], BF16)
        sqb = A([512], BF16)
        raw_tx = A([512])
        rsb = raw_tx
        Ktok = A([16, 128], BF16, parts=64)
        Vtok = A([16, 128], BF16, parts=64)
        oacc = A([16, 128], parts=64)
        rso = A([16], parts=64)
        nst = 2 * nseq
        Sst = [A([128]) for _ in range(nst)]
        BS = [Buf(f"S{i}") for i in range(nst)]
        BQT, BKT, BVT, Bsz, Bsq, Brs, BKt, BVt, Bo, Brso = (Buf(x) for x in (
            "QT", "KT", "VT", "szT", "gsq", "grs", "Ktok", "Vtok", "oacc", "rso"))
        tx = raw_tx[0:64, :].rearrange("p (a b) -> p a b", a=8)
        PTb = A([8, 64], parts=64)
        BPT = Buf("PTb")
        Nn = [A([8, 64], parts=64) for _ in range(2)]
        NTt = [A([8, 64], parts=64) for _ in range(2)]
        Btx = Brs
        BN = [Buf("N0"), Buf("N1")]
        BNT = [Buf("NT0"), Buf("NT1")]
        Xs = [A([8, 256], parts=64) for _ in range(2)]
        wTs = [A([8, 64]) for _ in range(2)]
        qdTs = [A([8, 64]) for _ in range(2)]
        kds = [A([8, 128], parts=64) for _ in range(2)]
        Ats = [A([8, 64], parts=64) for _ in range(2)]
        BXs = [Buf("X0"), Buf("X1")]
        BwTs = [Buf("wT0"), Buf("wT1")]
        Bqds = [Buf("qd0"), Buf("qd1")]
        Bkds = [Buf("kd0"), Buf("kd1")]
        BAts = [Buf("At0"), Buf("At1")]
        vnew = [A([128], parts=64) for _ in range(4)]
        Bvn = [Buf(f"vn{i}") for i in range(4)]
        i64 = ident[0:64, 0:64]
        vn_i = [0]
        GST = cfg.get("gdn_stop", 9)

        def gen_solve(si_, hd, d, ci0):
            col = d * 4 + hd
            spi = [0]

            def psn():
                i = spi[0] % 6
                spi[0] += 1
                return i

            Mc = gmask[:, d, :, :]
            Ms = gmask[:, 2 + d, :, :]
            X, BX, wT, BwT, qdT, Bqd, kd, Bkd, At, BAt = (Xs[si_], BXs[si_], wTs[si_], BwTs[si_], qdTs[si_], Bqds[si_],
                                                          kds[si_], Bkds[si_], Ats[si_], BAts[si_])
            cs8 = slice(ci0, ci0 + 8)
            tok8 = slice(ci0 * 64, (ci0 + 8) * 64)
            gb = bgt[:, cs8, 32 + col:33 + col].to_broadcast([64, 8, 64])
            tt(PLG, tx, Mc, gb, ALU.mult, [Bconst, Bgs], [Btx])
            pR = psn()
            for e in range(8):
                mm(PS[pR][:, e * 64:(e + 1) * 64], ones32[0:64, :], tx[:, e, :], True, True, [Bconst, Btx], [PSB[pR]])
            pG = psn()
            pQ = psn()
            for e in range(8):
                ck = slice((ci0 + e) * 64, (ci0 + e + 1) * 64)
                mm(PS[pG][0:64, e * 64:(e + 1) * 64], KT[:, ck], KT[:, ck], True, True, [BKT], [PSB[pG]])
                mm(PS[pQ][0:64, e * 64:(e + 1) * 64], KT[:, ck], QT[:, ck], True, True, [BKT, BQT], [PSB[pQ]])
            tt("dve", X[:, :, 0:128], Vtok[:, cs8, :], bgt[:, cs8, col:col + 1].to_broadcast([64, 8, 128]),
               ALU.mult, [BVt, Bgs], [BX])
            tt("dve", X[:, :, 128:256], Ktok[:, cs8, :], begc[:, cs8, col:col + 1].to_broadcast([64, 8, 128]),
               ALU.mult, [BKt, Bgs], [BX])
            tt("dve", kd, Ktok[:, cs8, :], kdf[:, cs8, col:col + 1].to_broadcast([64, 8, 128]), ALU.mult,
               [BKt, Bgs], [Bkd])
            yield
            pRv = PS[pR][:, :].rearrange("p (a b) -> p a b", a=8)
            tt("dve", tx, pRv[0:64], gc[:, cs8, col:col + 1].to_broadcast([64, 8, 64]), ALU.subtract,
               [PSB[pR], Bgs], [Btx])
            cp("dve", qdT, pRv, [PSB[pR]], [Bqd])
            ts("dve", Nn[0], tx, 0.0, None, ALU.max, None, [Btx], [BN[0]])
            ts("dve", At, tx, 0.0, None, ALU.min, None, [Btx], [BAt])
            act(Nn[0], Nn[0], AF.Exp, [BN[0]], [BN[0]], scale=-1.0)
            act(At, At, AF.Exp, [BAt], [BAt])
            act(qdT, qdT, AF.Exp, [Bqd], [Bqd])
            yield
            pGv = PS[pG][0:64, :].rearrange("p (a b) -> p a b", a=8)
            pQv = PS[pQ][0:64, :].rearrange("p (a b) -> p a b", a=8)
            tt("dve", Nn[0], Nn[0], pGv, ALU.mult, [BN[0], PSB[pG]], [BN[0]])
            tt("dve", Nn[0], Nn[0], Ms, ALU.mult, [BN[0], Bconst], [BN[0]])
            tt("dve", Nn[0], Nn[0], bgt[:, cs8, col:col + 1].to_broadcast([64, 8, 64]), ALU.mult,
               [BN[0], Bgs], [BN[0]])
            tt("dve", At, At, pQv, ALU.mult, [BAt, PSB[pQ]], [BAt])
            tt("dve", At, At, Mc, ALU.mult, [BAt, Bconst], [BAt])
            tt("dve", qdT, QT[:, tok8].rearrange("p (a b) -> p a b", a=8), qdT, ALU.mult, [BQT, Bqd], [Bqd])
            pT = psn()
            for e in range(8):
                tr(PS[pT][0:64, e * 64:(e + 1) * 64], Nn[0][:, e, :], i64, [BN[0]], [PSB[pT]])
            yield
            cp("act", NTt[0], PS[pT][0:64, :].rearrange("p (a b) -> p a b", a=8), [PSB[pT]], [BNT[0]])

            stt("dve", PTb, NTt[0], -1.0, i64.unsqueeze(1).to_broadcast([64, 8, 64]), ALU.mult, ALU.add,
                [BNT[0], Bconst], [BPT])
            cur = 0
            for lev in range(5):
                nxt = 1 - cur
                pN1 = psn()
                for e in range(8):
                    mm(PS[pN1][0:64, e * 64:(e + 1) * 64], NTt[cur][:, e, :], Nn[cur][:, e, :], True, True,
                       [BN[cur], BNT[cur]], [PSB[pN1]])
                pN2 = None
                if lev < 4:
                    pN2 = psn()
                    for e in range(8):
                        mm(PS[pN2][0:64, e * 64:(e + 1) * 64], Nn[cur][:, e, :], NTt[cur][:, e, :], True, True,
                           [BN[cur], BNT[cur]], [PSB[pN2]])
                yield
                cp("act", Nn[nxt], PS[pN1][0:64, :].rearrange("p (a b) -> p a b", a=8), [PSB[pN1]], [BN[nxt]])
                if pN2 is not None:
                    cp("act", NTt[nxt], PS[pN2][0:64, :].rearrange("p (a b) -> p a b", a=8), [PSB[pN2]], [BNT[nxt]])
                pP = psn()
                for e in range(8):
                    mm(PS[pP][0:64, e * 64:(e + 1) * 64], Nn[nxt][:, e, :], PTb[:, e, :], True, True,
                       [BN[nxt], BPT], [PSB[pP]])
                yield
                tt("dve", PTb, PTb, PS[pP][0:64, :].rearrange("p (a b) -> p a b", a=8), ALU.add, [BPT, PSB[pP]], [BPT])
                cur = nxt
            banks = [psn() for _ in range(4)]
            for e in range(8):
                pb = banks[e // 2]
                mm(PS[pb][0:64, (e % 2) * 256:(e % 2 + 1) * 256], PTb[:, e, :], X[:, e, :], True, True,
                   [BPT, BX], [PSB[pb]])
            yield
            for b_ in range(4):
                pb = banks[b_]
                cp("act" if b_ % 2 == 0 else "dve", X[:, 2 * b_:2 * b_ + 2, :],
                   PS[pb][0:64, :].rearrange("p (a b) -> p a b", a=2), [PSB[pb]], [BX])
            pW = psn()
            for e in range(8):
                tr(PS[pW][:, e * 64:(e + 1) * 64], X[:, e, 128:256], i64, [BX], [PSB[pW]])
            yield
            cp("act", wT, PS[pW][:, :].rearrange("p (a b) -> p a b", a=8), [PSB[pW]], [BwT])

        def gen_scan(si_, hd, d, ci0, first):
            col = d * 4 + hd
            X, BX, wT, BwT, qdT, Bqd, kd, Bkd, At, BAt = (Xs[si_], BXs[si_], wTs[si_], BwTs[si_], qdTs[si_], Bqds[si_],
                                                          kds[si_], Bkds[si_], Ats[si_], BAts[si_])
            if g == 1:
                chains = [(0, list(range(8)) if d == 0 else list(range(7, -1, -1)))]
            else:
                chains = []
                for s2 in range(2):
                    es = list(range(s2 * 4, s2 * 4 + 4))
                    chains.append((ci0 // 4 + s2, es if d == 0 else es[::-1]))
            nstep = len(chains[0][1])
            pA, pB = 6, 7
            for step in range(nstep):
                vis = []
                for c_, (s, es) in enumerate(chains):
                    e = es[step]
                    si = d * nseq + s
                    mm(PS[pA][0:64, c_ * 256:c_ * 256 + 128], wT[:, e, :], Sst[si], True, True, [BwT, BS[si]], [PSB[pA]])
                yield
                for c_, (s, es) in enumerate(chains):
                    e = es[step]
                    si = d * nseq + s
                    vi = vn_i[0] % 4
                    vn_i[0] += 1
                    vis.append(vi)
                    tt("dve", vnew[vi], X[:, e, 0:128], PS[pA][0:64, c_ * 256:c_ * 256 + 128], ALU.subtract,
                       [BX, PSB[pA]], [Bvn[vi]])
                for c_, (s, es) in enumerate(chains):
                    e = es[step]
                    si = d * nseq + s
                    vi = vis[c_]
                    po = PS[pA][0:64, c_ * 256 + 128:c_ * 256 + 256]
                    mm(po, qdT[:, e, :], Sst[si], True, False, [Bqd, BS[si]], [PSB[pA]])
                    mm(po, At[:, e, :], vnew[vi], False, True, [BAt, Bvn[vi]], [PSB[pA]])
                    mm(PS[pB][:, c_ * 128:(c_ + 1) * 128], kd[:, e, :], vnew[vi], True, True, [Bkd, Bvn[vi]], [PSB[pB]])
                yield
                for c_, (s, es) in enumerate(chains):
                    e = es[step]
                    ci = ci0 + e
                    si = d * nseq + s
                    po = PS[pA][0:64, c_ * 256 + 128:c_ * 256 + 256]
                    if first:
                        cp("dve", oacc[:, ci, :], po, [PSB[pA]], [Bo])
                    else:
                        tt("dve", oacc[:, ci, :], oacc[:, ci, :], po, ALU.add, [Bo, PSB[pA]], [Bo])
                    stt("dve", Sst[si], Sst[si], gtot[:, ci, col:col + 1], PS[pB][:, c_ * 128:(c_ + 1) * 128], ALU.mult, ALU.add,
                        [BS[si], Bgs, PSB[pB]], [BS[si]])

        def drive(*gens):
            gens = [g_ for g_ in gens if g_ is not None]
            while gens:
                for g_ in list(gens):
                    try:
                        next(g_)
                    except StopIteration:
                        gens.remove(g_)

        for hd in range(4):
            if GST <= 1:
                continue
            for part, dst, Bd in ((0, QT, BQT), (1, KT, BKT), (2, VT, BVT)):
                ch = part * 4 + hd
                wt, wb = wget(("win_t", l, TGQ + ch))
                pis = proj(wt, wb, hT, Bh)
                evac_pad(u3, Bu, pis, nseq, Ls)
                wc = [vecs[:, l, V_GCW + tap * 12 + ch:V_GCW + tap * 12 + ch + 1] for tap in range(3)]
                conv3(u3, Bu, nseq, Ls, wc, None, dst.rearrange("p (s l) -> p s l", s=nseq), Bd, t1, t2, Bt1, Bt2,
                      final_func=AF.Silu)
                if part < 2:
                    for tb in range(2):
                        sl = slice(tb * 512, (tb + 1) * 512)
                        act(sqb, dst[:, sl], AF.Square, [Bd], [Bsq])
                        pi = psn()
                        mm(PS[pi][:, :], ones_bf[:, :], sqb, True, True, [Bsq, Bconst], [PSB[pi]])
                        act(rsb, PS[pi][:, :], AF.Sqrt, [PSB[pi]], [Brs], bias=EPS)
                        recip("dve", rsb, rsb, [Brs], [Brs])
                        stt("dve", dst[:, sl], dst[:, sl], (DH ** -0.5 if part == 0 else 1.0), rsb, ALU.mult, ALU.mult,
                            [Bd, Brs], [Bd])
            wt, wb = wget(("win_t", l, TGZ + hd))
            pis = proj(wt, wb, hT, Bh)
            for tb in range(2):
                act(szT[:, tb * 512:(tb + 1) * 512], PS[pis[tb]][:, :], AF.Silu, [PSB[pis[tb]]], [Bsz])
            for src_, Bs_, dst_, Bd_ in ((KT, BKT, Ktok, BKt), (VT, BVT, Vtok, BVt)):
                for q4 in range(4):
                    pi = psn()
                    for j in range(4):
                        ci = q4 * 4 + j
                        tr(PS[pi][0:64, j * 128:(j + 1) * 128], src_[:, ci * 64:(ci + 1) * 64], ident[:, :], [Bs_], [PSB[pi]])
                    cp("act", dst_[:, q4 * 4:(q4 + 1) * 4, :], PS[pi][0:64, :].rearrange("p (a b) -> p a b", a=4),
                       [PSB[pi]], [Bd_])
            for d in range(2):
                for s in range(nseq):
                    si = d * nseq + s
                    if g == 0:
                        memset(PLG, Sst[si], 0.0, [BS[si]])
                    else:
                        dma("sp", Sst[si], (sf0_d if d == 0 else sb0_d)[l, hd], [], [BS[si]], f"st{si}")
            if GST <= 2:
                continue
            subs = []
            for sbi in range(2):
                subs.append((0, sbi * 8))
                subs.append((1, (1 - sbi) * 8))
            prev = None
            for i_, (d, ci0) in enumerate(subs):
                drive(gen_solve(i_ % 2, hd, d, ci0), prev)
                prev = gen_scan(i_ % 2, hd, d, ci0, i_ < 2)
            drive(prev)
            if g == 0:
                for d in range(2):
                    for s in range(nseq):
                        si = d * nseq + s
                        dma("sp", (sf_o if d == 0 else sb_o)[l, s, hd], Sst[si], [BS[si]], [], f"so{si}")
            for half in range(2):
                dst_sq = (t1f if half == 0 else t2f)[0:64, :].rearrange("p (a b) -> p a b", a=8)
                tt(PLG, dst_sq, oacc[:, half * 8:(half + 1) * 8, :], oacc[:, half * 8:(half + 1) * 8, :], ALU.mult,
                   [Bo], [Bt1 if half == 0 else Bt2])
                red("dve", rso[:, half * 8:(half + 1) * 8], dst_sq, ALU.add, [Bt1 if half == 0 else Bt2], [Brso])
            act(rso, rso, AF.Sqrt, [Brso], [Brso], bias=EPS, scale=1.0 / 128)
            recip("dve", rso, rso, [Brso], [Brso])
            tt("dve", oacc, oacc, rso.unsqueeze(2).to_broadcast([64, 16, 128]), ALU.mult, [Bo, Brso], [Bo])
            for half in range(2):
                pi = psn()
                for j in range(8):
                    ci = half * 8 + j
                    tr(PS[pi][:, j * 64:(j + 1) * 64], oacc[:, ci, :], i64, [Bo], [PSB[pi]])
                sl = slice(half * 512, (half + 1) * 512)
                stt("dve", ymix[:, 12 + hd, sl], PS[pi][:, :], vecs[:, l, V_GNG:V_GNG + 1], szT[:, sl], ALU.mult, ALU.mult,
                    [PSB[pi], Bvec, Bsz], [By])
        S.barrier()

    def phase_merge(l, g):
        RX.reset()
        merged = RX.alloc([16, T], BF16)
        Bmg = Buf("merged")
        sig = [RX.alloc([512]) for _ in range(2)]
        Bsig = [Buf("sig0"), Buf("sig1")]
        accm = [RX.alloc([512]) for _ in range(2)]
        Bacc = [Buf("acc0"), Buf("acc1")]
        tmpm = [RX.alloc([512]) for _ in range(2)]
        Btm = [Buf("mt0"), Buf("mt1")]
        kranges = [range(0, 8), range(8, 12), range(12, 16)]
        for n_ in range(16):
            wtp, wbp = wget(("wp_t", l, n_))
            pbs = [proj(wtp, wbp, ymix, By, kranges[i]) for i in range(3)]
            for i in range(3):
                wtg, wbg = wget(("win_t", l, TGATE + i * 16 + n_))
                pg = proj(wtg, wbg, hT, Bh)
                pb = pbs[i]
                bcol = vecs[:, l, V_BGATE + i * 16 + n_:V_BGATE + i * 16 + n_ + 1]
                for tb in range(2):
                    sl = slice(tb * 512, (tb + 1) * 512)
                    act(sig[tb], PS[pg[tb]][:, :], AF.Sigmoid, [PSB[pg[tb]], Bvec], [Bsig[tb]], bias=bcol)
                    if i == 0:
                        tt("dve", accm[tb], PS[pb[tb]][:, :], sig[tb], ALU.mult, [PSB[pb[tb]], Bsig[tb]], [Bacc[tb]])
                    else:
                        tt("dve", tmpm[tb], PS[pb[tb]][:, :], sig[tb], ALU.mult, [PSB[pb[tb]], Bsig[tb]], [Btm[tb]])
                        if i == 1:
                            tt(PL, accm[tb], accm[tb], tmpm[tb], ALU.add, [Bacc[tb], Btm[tb]], [Bacc[tb]])
                        else:
                            tt(PL, merged[:, n_, sl], accm[tb], tmpm[tb], ALU.add, [Bacc[tb], Btm[tb]], [Bmg])
        tap(f"merged_{l}_{g}", merged, [128, 16, T], BF16, [Bmg])
        dma("sp", xT, xscr.rearrange("(k p) t -> p k t", p=128), [Bxscr], [Bx], "xld")
        for m in range(16):
            wt, wb = wget(("wout_t", l, m))
            pis = proj(wt, wb, merged, Bmg)
            for tb in range(2):
                sl = slice(tb * 512, (tb + 1) * 512)
                stt("dve", xT[:, m, sl], PS[pis[tb]][:, :], modv[:, l, 32 + m:33 + m, g], xT[:, m, sl], ALU.mult, ALU.add,
                    [PSB[pis[tb]], Bmod, Bx], [Bx])
        S.barrier()

    def phase_ffn(l, g):
        RX.reset()
        nseq, Ls = (4, 256) if g == 0 else (1, 1024)
        upad = [RX.alloc([nseq * (Ls + 2)]) for _ in range(2)]
        u3 = [u.rearrange("p (s l) -> p s l", s=nseq) for u in upad]
        Bu = [Buf("fu0"), Buf("fu1")]
        for i in range(2):
            memset(PL, upad[i], 0.0, [Bu[i]])
        t1 = [RX.alloc([nseq, Ls]) for _ in range(2)]
        t2 = [RX.alloc([nseq, Ls]) for _ in range(2)]
        Bt1 = [Buf("ft10"), Buf("ft11")]
        Bt2 = [Buf("ft20"), Buf("ft21")]
        ca = RX.alloc([T])
        cb = RX.alloc([T])
        Bca, Bcb = Buf("ca"), Buf("cb")
        gbuf = ymix
        for qd in range(4):
            for jj in range(11):
                j = qd * 11 + jj
                for ab in range(2):
                    ch = ab * 44 + j
                    wt, wb = wget(("wup_t", l, ch))
                    pis = proj(wt, wb, hT, Bh)
                    evac_pad(u3[ab], Bu[ab], pis, nseq, Ls)
                    wc = [vecs[:, l, V_FCW + tap_ * 88 + ch:V_FCW + tap_ * 88 + ch + 1] for tap_ in range(3)]
                    bc = vecs[:, l, V_FCB + ch:V_FCB + ch + 1]
                    if ab == 0:
                        conv3(u3[0], Bu[0], nseq, Ls, wc, bc, ca.rearrange("p (s l) -> p s l", s=nseq), Bca,
                              t1[0], t2[0], Bt1[0], Bt2[0], final_func=AF.Silu)
                    else:
                        conv3(u3[1], Bu[1], nseq, Ls, wc, bc, cb.rearrange("p (s l) -> p s l", s=nseq), Bcb,
                              t1[1], t2[1], Bt1[1], Bt2[1])
                tt(PL, gbuf[:, jj, :], ca, cb, ALU.mult, [Bca, Bcb], [By])
            for m in range(16):
                wt, wb = wget(("wdn_t", l, qd, m), kct=11)
                pis = proj(wt, wb, gbuf, By, range(11))
                for tb in range(2):
                    sl = slice(tb * 512, (tb + 1) * 512)
                    stt("dve", xT[:, m, sl], PS[pis[tb]][:, :], modv[:, l, 80 + m:81 + m, g], xT[:, m, sl], ALU.mult, ALU.add,
                        [PSB[pis[tb]], Bmod, Bx], [Bx])
        S.barrier()

    for g in groups:
        dma("sp", xT, xT_d[g].rearrange("(k p) t -> p k t", p=128), [], [Bx], "xin")
        for l in layers:
            phase_norm(l, g, 0)
            tap(f"h_{l}_{g}", hT[:, :, :], [128, 16, T], BF16, [Bh])
            dma("sp", xscr.rearrange("(k p) t -> p k t", p=128), xT, [Bx], [Bxscr], "xsp")
            S.barrier()
            if "hy" in phases:
                phase_hyprep(l, 256 if g == 0 else 1024)
            if "attn" in phases:
                phase_attn(l, g)
            if "hy" in phases:
                phase_hyena(l, g)
            if "gdn" in phases:
                phase_gdn(l, g)
            tap(f"ymix_{l}_{g}", ymix[:, :, :], [128, 16, T], BF16, [By])
            if "merge" in phases:
                phase_merge(l, g)
                tap(f"x1_{l}_{g}", xT, [128, 16, T], F32, [Bx])
            if "ffn" in phases:
                phase_norm(l, g, 1)
                phase_ffn(l, g)
                tap(f"x2_{l}_{g}", xT, [128, 16, T], F32, [Bx])
        if cfg.get("final", True):
            phase_final(g)
    if cfg.get("_collect"):
        return wkeys
    S.final_wait("sp")
    S.emit()
    return nc


def build2(cfg=None):
    cfg = dict(cfg or {})
    c1 = dict(cfg)
    c1["_collect"] = True
    plan = build(c1)
    cfg["_wplan"] = plan
    return build(cfg)


_SHARED_KEYS = None


def kernel(**inputs):
    inp = {k: np.asarray(v) for k, v in inputs.items()}
    sh = prep_shared(inp)
    nc = build2()
    in_maps = []
    for core in range(8):
        m = dict(sh)
        m.update(prep_core(inp, core))
        in_maps.append(m)
    res = run_bass_kernel_spmd(nc, in_maps, core_ids=list(range(8))).results
    B, SEQ = 32, 256
    y_p = np.zeros((B, SEQ, D), np.float32)
    y_s = np.zeros((4, 1024, D), np.float32)
    nk = np.zeros((B, NLAYER, SEQ, 8, 128), np.float32)
    nv = np.zeros((B, NLAYER, SEQ, 8, 128), np.float32)
    nsf = np.zeros((B, NLAYER, 4, 128, 128), np.float32)
    nsb = np.zeros((B, NLAYER, 4, 128, 128), np.float32)
    for core in range(8):
        r = res[core]
        yT = np.asarray(r["yT"])
        y_p[4 * core:4 * core + 4] = yT[0].T.reshape(4, SEQ, D)
        if core < 4:
            y_s[core] = yT[1].T
        kT = np.asarray(r["kT_out"])
        vT = np.asarray(r["vT_out"])
        nk[4 * core:4 * core + 4] = kT.transpose(2, 0, 1).reshape(4, SEQ, NLAYER, 8, 128).transpose(0, 2, 1, 3, 4)
        nv[4 * core:4 * core + 4] = vT.transpose(2, 0, 1).reshape(4, SEQ, NLAYER, 8, 128).transpose(0, 2, 1, 3, 4)
        nsf[4 * core:4 * core + 4] = np.asarray(r["sf_out"]).transpose(1, 0, 2, 3, 4)
        nsb[4 * core:4 * core + 4] = np.asarray(r["sb_out"]).transpose(1, 0, 2, 3, 4)
    return (y_p, y_s, nk, nv, nsf, nsb)
```

```python
import math
import numpy as np
import ml_dtypes
import concourse.bass as bass
import concourse.mybir as mybir
from concourse.bass_utils import run_bass_kernel_spmd

F32 = mybir.dt.float32
BF16 = mybir.dt.bfloat16
AF = mybir.ActivationFunctionType
ALU = mybir.AluOpType
AX = mybir.AxisListType


class Buf:
    __slots__ = ("name", "w", "r")

    def __init__(self, name):
        self.name = name
        self.w = {}
        self.r = {}


class _Eng:
    def __init__(self, key, sem):
        self.key = key
        self.sem = sem
        self.cnt = 0
        self.ops = []
        self.waited = {}


class Sched:
    ENG = ("pe", "act", "dve", "pool", "sp")

    def __init__(self, nc):
        self.nc = nc
        self.eng = {k: _Eng(k, nc.alloc_semaphore(name="s_" + k)) for k in self.ENG}
        self.dsem = {}

    def dma_sem(self, name):
        if name not in self.dsem:
            self.dsem[name] = [self.nc.alloc_semaphore(name="d_" + name), 0]
        return name

    def _wait(self, e, sid, sem, val):
        if e.waited.get(sid, 0) >= val:
            return
        e.waited[sid] = val
        e.ops.append(lambda h, sem=sem, val=val: h.wait_ge(sem, val))

    def op(self, ekey, fn, r=(), w=(), dma=None):
        e = self.eng[ekey]
        deps = {}

        def add(d, raw):
            for sid, (sem, val) in d.items():
                if sid == e.key and dma is None:
                    if e.key == "pe" or not raw:
                        continue
                if val > deps.get(sid, (None, 0))[1]:
                    deps[sid] = (sem, val)

        for b in r:
            add(b.w, True)
        for b in w:
            add(b.w, False)
            add(b.r, False)
        for sid, (sem, val) in deps.items():
            self._wait(e, sid, sem, val)
        if dma is None:
            e.cnt += 1
            tok = (e.key, e.sem, e.cnt)
            sem, inc = e.sem, 1
        else:
            ds = self.dsem[dma]
            ds[1] += 16
            tok = ("d_" + dma, ds[0], ds[1])
            sem, inc = ds[0], 16
        e.ops.append(lambda h, fn=fn, sem=sem, inc=inc: fn(h).then_inc(sem, inc))
        for b in r:
            b.r[tok[0]] = (tok[1], tok[2])
        for b in w:
            b.w = {tok[0]: (tok[1], tok[2])}
            b.r = {}
        return tok

    def barrier(self):
        for e in self.eng.values():
            for o in self.eng.values():
                if o is not e and o.cnt > 0:
                    self._wait(e, o.key, o.sem, o.cnt)
            for name, (sem, val) in self.dsem.items():
                if val > 0:
                    self._wait(e, "d_" + name, sem, val)

    def final_wait(self, ekey="sp"):
        e = self.eng[ekey]
        for o in self.eng.values():
            if o is not e and o.cnt > 0:
                self._wait(e, o.key, o.sem, o.cnt)
        for name, (sem, val) in self.dsem.items():
            if val > 0:
                self._wait(e, "d_" + name, sem, val)

    def emit(self):
        nc = self.nc
        with nc.Block() as block:
            @block.tensor
            def _(h):
                for f in self.eng["pe"].ops:
                    f(h)

            @block.scalar
            def _(h):
                for f in self.eng["act"].ops:
                    f(h)

            @block.vector
            def _(h):
                for f in self.eng["dve"].ops:
                    f(h)

            @block.gpsimd
            def _(h):
                for f in self.eng["pool"].ops:
                    f(h)

            @block.sync
            def _(h):
                for f in self.eng["sp"].ops:
                    f(h)


D = 2048
T = 1024
KC = 16
NLAYER = 2
DH = 128
D_FF = 5632
N_IN = 12816
EPS = 1e-6
TQ, TK, TV, THY, TGQ, TGZ, TBA, TGATE = 0, 8, 16, 24, 36, 48, 52, 53
NWIN = 101
V_BMOD, V_LN1, V_LN2, V_BGATE, V_HCW, V_HCB, V_HSKIP, V_GCW, V_GNG, V_FCW, V_FCB, V_FING, NV = (
    0, 96, 112, 128, 176, 212, 224, 232, 268, 272, 536, 624, 640)
KS_TOK = [0, 0, 0, 128, 256, 384, 384, 384]
NSLOT = 4
BFNP = ml_dtypes.bfloat16


def _tile_w(w):
    K, N = w.shape
    return np.ascontiguousarray(w.reshape(K // 128, 128, N // 128, 128).transpose(2, 1, 0, 3)).reshape(N // 128, 128, K)


def _fm(v):
    return np.ascontiguousarray(v.reshape(-1, 128).T)


def _dft_consts(L):
    N = 2 * L
    k = np.arange(L, dtype=np.float64)
    m = np.arange(L, dtype=np.float64)
    w = 2 * np.pi * (k + 0.5) / N
    ang = np.outer(m, w)
    Cm, Sm = np.cos(ang), np.sin(ang)
    ang1 = np.outer(m + 1, w)
    Cm1, Sm1 = np.cos(ang1), np.sin(ang1)
    Cm1[L - 1] = 0
    Sm1[L - 1] = 0
    n = L // 128

    def tf(M):
        return M.reshape(n, 128, n, 128).transpose(2, 1, 0, 3)

    F4 = np.stack([tf(Cm), tf(Cm1), tf(-Sm), tf(Sm1)], axis=2)
    F2 = np.stack([tf(Cm), tf(Sm)], axis=2)
    CT, ST = Cm.T / L, Sm.T / L
    I2 = np.stack([CT.reshape(n, 128, L), ST.reshape(n, 128, L)], axis=2)
    c = lambda a: np.ascontiguousarray(a.astype(np.float32).astype(BFNP))
    return c(F4), c(F2), c(I2)


def _zfeat(L):
    f32 = np.float32
    t_norm = np.linspace(0.0, 1.0, L, dtype=f32)
    t_idx = np.arange(L, dtype=f32)
    bands = np.linspace(1e-4, 15.0, 16, dtype=f32)
    ang = (f32(2.0 * math.pi / L) * t_idx[:, None] * bands[None, :]).astype(f32)
    z = np.concatenate([t_norm[:, None], np.cos(ang), np.sin(ang)], axis=-1).astype(f32)
    tn = np.ascontiguousarray((-t_norm).reshape(L // 128, 128).T)
    return np.ascontiguousarray(z.T), tn


def _gdn_masks():
    p = np.arange(64)[:, None]
    f = np.arange(64)[None, :]
    ms = [(p <= f), (p >= f), (p > f), (p < f)]
    out = np.stack([np.tile(m.astype(np.float32)[:, None, :], (1, 8, 1)) for m in ms], axis=0)
    return np.ascontiguousarray(out)


def _bias_mask(rpb):
    out = np.full((8, 8, 128, 640), -1e30, np.float32)
    p = np.arange(128)
    cq = p % 64
    cs = np.clip(cq - 8, 0, 48)
    for qt in range(8):
        r = 2 * qt + p // 64
        st = np.clip(r - 4, 0, 8)
        ktok = KS_TOK[qt] + np.arange(640)
        kr = ktok // 64
        ck = ktok % 64
        ok = ((kr[None, :] >= st[:, None]) & (kr[None, :] < st[:, None] + 8)
              & (ck[None, :] >= cs[:, None]) & (ck[None, :] < cs[:, None] + 16))
        ri = np.clip(kr[None, :] - r[:, None] + 7, 0, 14)
        ci = np.clip(ck[None, :] - cq[:, None] + 15, 0, 30)
        for h in range(8):
            vals = rpb[h][ri, ci]
            out[h, qt] = np.where(ok, vals, np.float32(-1e30))
    return out


def prep_shared(inp):
    sh = {}
    L2 = NLAYER
    w_in = inp["w_in"]
    win_t = np.zeros((L2, NWIN, 128, D), np.float32)
    wmod_t = np.zeros((L2, 96, 128, D), np.float32)
    wp_t = np.zeros((L2, 16, 128, D), np.float32)
    wout_t = np.zeros((L2, 16, 128, D), np.float32)
    wup_t = np.zeros((L2, 88, 128, D), np.float32)
    wdn_t = np.zeros((L2, 4, 16, 128, 11 * 128), np.float32)
    vecs = np.zeros((128, L2, NV), np.float32)
    for l in range(L2):
        w = w_in[l]
        win_t[l, 0:52] = _tile_w(w[:, 0:6656])
        ba = np.zeros((D, 128), np.float32)
        ba[:, 0:8] = w[:, 6656:6664]
        ba[:, 32:40] = w[:, 6664:6672]
        win_t[l, 52] = _tile_w(ba)[0]
        win_t[l, 53:101] = _tile_w(w[:, 6672:12816])
        wmod_t[l] = _tile_w(inp["w_mod"][l])
        wp_t[l] = _tile_w(np.concatenate([inp["w_pa"][l], inp["w_pb"][l], inp["w_pc"][l]], axis=0))
        wout_t[l] = _tile_w(inp["w_out"][l])
        wup_t[l] = _tile_w(inp["ffn_w_up"][l])
        for q in range(4):
            wdn_t[l, q] = _tile_w(inp["ffn_w_down"][l][q * 1408:(q + 1) * 1408])
        v = vecs[:, l]
        v[:, V_BMOD:V_BMOD + 96] = _fm(inp["b_mod"][l])
        v[:, V_LN1:V_LN1 + 16] = _fm(inp["ln1_g"][l])
        v[:, V_LN2:V_LN2 + 16] = _fm(inp["ln2_g"][l])
        v[:, V_BGATE:V_BGATE + 48] = _fm(inp["b_gate"][l])
        for tap in range(3):
            v[:, V_HCW + tap * 12:V_HCW + tap * 12 + 12] = _fm(inp["hy_conv_w"][l][tap])
            v[:, V_GCW + tap * 12:V_GCW + tap * 12 + 12] = _fm(inp["gdn_conv_w"][l][tap])
            v[:, V_FCW + tap * 88:V_FCW + tap * 88 + 88] = _fm(inp["ffn_conv_w"][l][tap])
        v[:, V_HCB:V_HCB + 12] = _fm(inp["hy_conv_b"][l])
        v[:, V_HSKIP:V_HSKIP + 8] = _fm(inp["hy_skip"][l].reshape(-1))
        v[:, V_GNG:V_GNG + 1] = _fm(inp["gdn_norm_g"][l])
        v[:, V_FCB:V_FCB + 88] = _fm(inp["ffn_conv_b"][l])
        v[:, V_FING:V_FING + 16] = _fm(inp["final_g"])
    sh.update(win_t=win_t, wmod_t=wmod_t, wp_t=wp_t, wout_t=wout_t, wup_t=wup_t, wdn_t=wdn_t, vecs=vecs)
    sh["bm"] = np.stack([_bias_mask(inp["na_rpb"][l]) for l in range(L2)], axis=0)
    sh["hw1"] = np.ascontiguousarray(inp["hy_w1"])
    sh["hw2"] = np.ascontiguousarray(inp["hy_w2"])
    sh["hw3"] = np.ascontiguousarray(inp["hy_w3"])
    hcol = np.zeros((L2, 64, 4), np.float32)
    hcol[:, :, 0] = inp["hy_b1"]
    hcol[:, :, 1] = inp["hy_b2"]
    hcol[:, :, 2] = inp["hy_freq"][:, 0]
    hcol[:, :, 3] = inp["hy_freq"][:, 1]
    sh["hcol"] = hcol
    sh["hdec"] = np.ascontiguousarray(inp["hy_decay"].reshape(L2, 2048))
    gcol = np.zeros((L2, 40, 2), np.float32)
    gcol[:, 32:40, 0] = inp["gdn_a_log"].reshape(L2, 8)
    gcol[:, 32:40, 1] = inp["gdn_dt_bias"].reshape(L2, 8)
    sh["gcol"] = gcol
    for L in (256, 1024):
        F4, F2, I2 = _dft_consts(L)
        zfT, tn = _zfeat(L)
        sh[f"F4_{L}"], sh[f"F2_{L}"], sh[f"I2_{L}"], sh[f"zfT_{L}"], sh[f"tn_{L}"] = F4, F2, I2, zfT, tn
    sh["ident"] = np.eye(128, dtype=np.float32)
    sh["gmask"] = _gdn_masks()
    return sh


def prep_core(inp, core):
    pc = {}
    xp = inp["x_prompt"][4 * core:4 * core + 4].reshape(T, D)
    sb = core % 4
    xs = inp["x_sample"][sb]
    pc["xT"] = np.ascontiguousarray(np.stack([xp.T, xs.T], axis=0))
    cv = np.stack([inp["c_ctx"], inp["c"][sb]], axis=-1)
    pc["cvec"] = np.ascontiguousarray(cv.reshape(16, 128, 2).transpose(1, 0, 2))
    pc["ckT"] = np.ascontiguousarray(inp["cache_k"][sb].transpose(0, 2, 3, 1))
    pc["cv"] = np.ascontiguousarray(inp["cache_v"][sb].reshape(NLAYER, 256, 1024))
    pc["sf0"] = np.ascontiguousarray(inp["state_fwd"][sb])
    pc["sb0"] = np.ascontiguousarray(inp["state_bwd"][sb])
    return pc


class Region:
    def __init__(self, tensor, ncols):
        self.t = tensor
        self.ncols = ncols
        self.off = 0

    def reset(self):
        self.off = 0

    def alloc(self, shape, dtype=F32, parts=128):
        n = 1
        for s in shape:
            n *= s
        nb = n * (4 if dtype == F32 else 2)
        nb = (nb + 31) // 32 * 32
        c0 = self.off // 4
        c1 = (self.off + nb) // 4
        assert c1 <= self.ncols, ("region overflow", self.off, nb, self.ncols * 4)
        self.off += nb
        ap = self.t[0:parts, c0:c1]
        if dtype != F32:
            ap = ap.bitcast(dtype)
        ap = ap[:, 0:n]
        if len(shape) == 2:
            ap = ap.rearrange("p (a b) -> p a b", a=shape[0])
        elif len(shape) == 3:
            ap = ap.rearrange("p (a b c) -> p a b c", a=shape[0], b=shape[1])
        return ap


def build(cfg=None):
    cfg = cfg or {}
    layers = cfg.get("layers", [0, 1])
    groups = cfg.get("groups", [0, 1])
    phases = cfg.get("phases", ["attn", "hy", "gdn", "merge", "ffn"])
    taps = cfg.get("taps", [])
    PL = cfg.get("pl", "dve")
    PLG = cfg.get("pl_gdn", "dve")
    nc = bass.Bass("TRN2", target_bir_lowering=False)
    S = Sched(nc)
    Dd = {}

    def din(name, shape, dt=F32):
        Dd[name] = nc.dram_tensor(name, list(shape), dt, kind="ExternalInput").ap()
        return Dd[name]

    def dout(name, shape, dt=F32):
        Dd[name] = nc.dram_tensor(name, list(shape), dt, kind="ExternalOutput").ap()
        return Dd[name]

    def dscr(name, shape, dt=F32):
        Dd[name] = nc.dram_tensor(name, list(shape), dt, kind="Internal").ap()
        return Dd[name]

    xT_d = din("xT", [2, D, T])
    cvec_d = din("cvec", [128, 16, 2])
    ckT_d = din("ckT", [NLAYER, 8, 128, 256])
    cv_d = din("cv", [NLAYER, 256, 1024])
    sf0_d = din("sf0", [NLAYER, 4, 128, 128])
    sb0_d = din("sb0", [NLAYER, 4, 128, 128])
    win_d = din("win_t", [NLAYER, NWIN, 128, D])
    wmod_d = din("wmod_t", [NLAYER, 96, 128, D])
    wp_d = din("wp_t", [NLAYER, 16, 128, D])
    wout_d = din("wout_t", [NLAYER, 16, 128, D])
    wup_d = din("wup_t", [NLAYER, 88, 128, D])
    wdn_d = din("wdn_t", [NLAYER, 4, 16, 128, 11 * 128])
    vecs_d = din("vecs", [128, NLAYER, NV])
    bm_d = din("bm", [NLAYER, 8, 8, 128, 640])
    hw1_d = din("hw1", [NLAYER, 33, 64])
    hw2_d = din("hw2", [NLAYER, 64, 64])
    hw3_d = din("hw3", [NLAYER, 64, 2048])
    hcol_d = din("hcol", [NLAYER, 64, 4])
    hdec_d = din("hdec", [NLAYER, 2048])
    gcol_d = din("gcol", [NLAYER, 40, 2])
    F4_d, F2_d, I2_d, zfT_d, tn_d = {}, {}, {}, {}, {}
    for L in (256, 1024):
        n = L // 128
        F4_d[L] = din(f"F4_{L}", [n, 128, 4, n, 128], BF16)
        F2_d[L] = din(f"F2_{L}", [n, 128, 2, n, 128], BF16)
        I2_d[L] = din(f"I2_{L}", [n, 128, 2, L], BF16)
        zfT_d[L] = din(f"zfT_{L}", [33, L])
        tn_d[L] = din(f"tn_{L}", [128, n])
    ident_d = din("ident", [128, 128])
    gmask_d = din("gmask", [4, 64, 8, 64])
    yT_d = dout("yT", [2, D, T])
    kT_o = dout("kT_out", [NLAYER, 1024, T])
    vT_o = dout("vT_out", [NLAYER, 1024, T])
    sf_o = dout("sf_out", [NLAYER, 4, 4, 128, 128])
    sb_o = dout("sb_out", [NLAYER, 4, 4, 128, 128])
    xscr = dscr("xscr", [D, T])
    hspec = dscr("hspec", [8, 128, 2, 1024])

    big = nc.alloc_sbuf_tensor("big", [128, 16384], F32)
    xT = big[:, :].rearrange("p (k t) -> p k t", k=16)
    hT = nc.alloc_sbuf_tensor("hT", [128, 16, T], BF16)
    ymix = nc.alloc_sbuf_tensor("ymix", [128, 16, T], BF16)
    wring = [nc.alloc_sbuf_tensor(f"wr{i}", [128, 16, 128], BF16) for i in range(NSLOT)]
    vecs = nc.alloc_sbuf_tensor("vecs_s", [128, NLAYER, NV], F32)
    modv = nc.alloc_sbuf_tensor("modv", [128, NLAYER, 96, 2], F32)
    modA = nc.alloc_sbuf_tensor("modA", [128, NLAYER, 2, 2, 16], F32)
    cvec = nc.alloc_sbuf_tensor("cvec_s", [128, 16, 2], F32)
    csil = nc.alloc_sbuf_tensor("csil", [128, 16, 2], BF16)
    ident = nc.alloc_sbuf_tensor("ident_s", [128, 128], F32)
    ones_bf = nc.alloc_sbuf_tensor("ones_bf", [128, 128], BF16)
    ones32 = nc.alloc_sbuf_tensor("ones32", [128, 128], F32)
    gmask = nc.alloc_sbuf_tensor("gmask_s", [64, 4, 8, 64], F32)
    EXC = 11264
    ex_t = nc.alloc_sbuf_tensor("ex", [128, EXC], F32)
    RB = Region(big, 16384)
    RX = Region(ex_t, EXC)

    def AA(shape, dtype=F32, parts=128):
        n = 1
        for s_ in shape:
            n *= s_
        nb = (n * (4 if dtype == F32 else 2) + 31) // 32 * 32
        reg = RB if RB.off + nb <= RB.ncols * 4 else RX
        return reg.alloc(shape, dtype, parts)

    PS = [nc.alloc_psum_tensor(f"ps{i}", [128, 512], F32) for i in range(8)]
    PSB = [Buf(f"ps{i}") for i in range(8)]
    st = {"ps": 0, "w": 0, "wi": 0}

    Bx, Bh, By, Bvec, Bmod, Bcs, Bconst, Bxscr, Bhspec = (Buf(n) for n in
                                                           ("x", "h", "ymix", "vecs", "mod", "csil", "const", "xscr", "hspec"))
    wB = [Buf(f"w{i}") for i in range(NSLOT)]
    for i in range(NSLOT):
        S.dma_sem(f"w{i}")

    def psn():
        i = st["ps"] % 8
        st["ps"] += 1
        return i

    def mm(out, lhsT, rhs, start, stop, r, w):
        S.op("pe", lambda h: h.matmul(out, lhsT, rhs, start=start, stop=stop), r=r, w=w)

    def tr(out, in_, idn, r, w):
        S.op("pe", lambda h: h.transpose(out, in_, idn), r=r + [Bconst], w=w)

    def act(out, in_, func, r, w, bias=None, scale=None):
        kw = {}
        if bias is not None:
            kw["bias"] = bias
        if scale is not None:
            kw["scale"] = scale
        S.op("act", lambda h: h.activation(out=out, in_=in_, func=func, **kw), r=r, w=w)

    def tt(eng, out, in0, in1, op, r, w):
        S.op(eng, lambda h: h.tensor_tensor(out=out, in0=in0, in1=in1, op=op), r=r, w=w)

    def ts(eng, out, in0, s1, s2, op0, op1, r, w):
        if op1 is None:
            S.op(eng, lambda h: h.tensor_scalar(out=out, in0=in0, scalar1=s1, scalar2=None, op0=op0), r=r, w=w)
        else:
            S.op(eng, lambda h: h.tensor_scalar(out=out, in0=in0, scalar1=s1, scalar2=s2, op0=op0, op1=op1), r=r, w=w)

    def stt(eng, out, in0, scalar, in1, op0, op1, r, w):
        S.op(eng, lambda h: h.scalar_tensor_tensor(out=out, in0=in0, scalar=scalar, in1=in1, op0=op0, op1=op1), r=r, w=w)

    def cp(eng, out, in_, r, w):
        if eng == "act":
            act(out, in_, AF.Copy, r, w)
        else:
            S.op(eng, lambda h: h.tensor_copy(out=out, in_=in_), r=r, w=w)

    def red(eng, out, in_, op, r, w):
        S.op(eng, lambda h: h.tensor_reduce(out=out, in_=in_, axis=AX.X, op=op), r=r, w=w)

    def recip(eng, out, in_, r, w):
        S.op(eng, lambda h: h.reciprocal(out=out, in_=in_), r=r, w=w)

    def memset(eng, ap, val, w):
        S.op(eng, lambda h: h.memset(ap, val), w=w)

    def dma(eng, out, in_, r, w, sem):
        S.dma_sem(sem)
        S.op(eng, lambda h: h.dma_start(out=out, in_=in_), r=r, w=w, dma=sem)

    wplan = cfg.get("_wplan")
    wkeys = []
    LOOKAHEAD = NSLOT - 1

    def _wsrc(key):
        ap = Dd[key[0]]
        for ix in key[1:]:
            ap = ap[ix]
        return ap

    def _wissue(i, key, kct):
        t = wring[i % NSLOT]
        dma("pool", t[:, 0:kct, :], _wsrc(key).rearrange("p (k n) -> p k n", n=128), [], [wB[i % NSLOT]], f"w{i % NSLOT}")

    def wget(key, kct=16):
        i = st["w"]
        st["w"] += 1
        wkeys.append((key, kct))
        if wplan is None:
            _wissue(i, key, kct)
        else:
            assert wplan[i] == (key, kct), (i, wplan[i], key, kct)
            while st["wi"] <= min(i + LOOKAHEAD, len(wplan) - 1):
                k2, c2 = wplan[st["wi"]]
                _wissue(st["wi"], k2, c2)
                st["wi"] += 1
        return wring[i % NSLOT], wB[i % NSLOT]

    def proj(wt, wb, rhs_t, rhs_b, kcs=range(16)):
        kcs = list(kcs)
        pis = []
        for tb in range(2):
            pi = psn()
            for n_, kc in enumerate(kcs):
                mm(PS[pi][:, :], wt[:, kc, :], rhs_t[:, kc, tb * 512:(tb + 1) * 512], n_ == 0, n_ == len(kcs) - 1,
                   [wb, rhs_b], [PSB[pi]])
            pis.append(pi)
        return pis

    def tap(name, src_ap, shape, dt, rb):
        if name in taps:
            d = dout("tap_" + name, shape, dt)
            dma("sp", d, src_ap, rb, [], "tap")

    dma("sp", vecs[:, :, :], vecs_d, [], [Bvec], "c0")
    dma("sp", cvec[:, :, :], cvec_d, [], [Bcs], "c1")
    dma("sp", ident[:, :], ident_d, [], [Bconst], "c2")
    dma("sp", gmask[:, :, :, :], gmask_d.rearrange("m p e f -> p m e f"), [], [Bconst], "c2")
    memset("dve", ones_bf[:, :], 1.0, [Bconst])
    memset("dve", ones32[:, :], 1.0, [Bconst])
    act(csil[:, :, :], cvec[:, :, :], AF.Silu, [Bcs], [Bcs])
    if cfg.get("zero_ymix"):
        memset(PL, ymix[:, :, :], 0.0, [By])

    for l in layers:
        pi = psn()
        for j in range(96):
            wt, wb = wget(("wmod_t", l, j))
            for kc in range(16):
                mm(PS[pi][:, 2 * j:2 * j + 2], wt[:, kc, :], csil[:, kc, :], kc == 0, kc == 15, [wb, Bcs], [PSB[pi]])
        psv = PS[pi][:, 0:192].rearrange("p (j g) -> p j g", g=2)
        for g in range(2):
            tt("dve", modv[:, l, :, g], psv[:, :, g], vecs[:, l, V_BMOD:V_BMOD + 96], ALU.add, [PSB[pi], Bvec], [Bmod])
        for g in range(2):
            stt("dve", modA[:, l, 0, g, :], modv[:, l, 16:32, g], 1.0, vecs[:, l, V_LN1:V_LN1 + 16], ALU.add, ALU.mult,
                [Bmod, Bvec], [Bmod])
            stt("dve", modA[:, l, 1, g, :], modv[:, l, 64:80, g], 1.0, vecs[:, l, V_LN2:V_LN2 + 16], ALU.add, ALU.mult,
                [Bmod, Bvec], [Bmod])

    def phase_norm(l, g, which):
        RX.reset()
        sq = [RX.alloc([512], BF16) for _ in range(2)]
        Bsq = [Buf("sq0"), Buf("sq1")]
        tmp = [RX.alloc([512]) for _ in range(2)]
        Btmp = [Buf("tmp0"), Buf("tmp1")]
        rs = RX.alloc([512])
        Brs = Buf("rs")
        shift0 = 0 if which == 0 else 48
        for tb in range(2):
            sl = slice(tb * 512, (tb + 1) * 512)
            pi = psn()
            for kc in range(16):
                act(sq[kc % 2], xT[:, kc, sl], AF.Square, [Bx], [Bsq[kc % 2]])
                mm(PS[pi][:, :], ones_bf[:, :], sq[kc % 2], kc == 0, kc == 15, [Bsq[kc % 2], Bconst], [PSB[pi]])
            act(rs, PS[pi][:, :], AF.Sqrt, [PSB[pi]], [Brs], bias=EPS, scale=1.0 / D)
            recip("dve", rs, rs, [Brs], [Brs])
            for kc in range(16):
                tt("dve", tmp[kc % 2], xT[:, kc, sl], rs, ALU.mult, [Bx, Brs], [Btmp[kc % 2]])
                act(hT[:, kc, sl], tmp[kc % 2], AF.Identity, [Btmp[kc % 2], Bmod], [Bh],
                    bias=modv[:, l, shift0 + kc:shift0 + kc + 1, g], scale=modA[:, l, which, g, kc:kc + 1])
        S.barrier()

    def phase_final(g):
        RX.reset()
        sq = [RX.alloc([512], BF16) for _ in range(2)]
        Bsq = [Buf("fsq0"), Buf("fsq1")]
        tmp = [RX.alloc([512]) for _ in range(2)]
        Btmp = [Buf("ftmp0"), Buf("ftmp1")]
        o32 = [RX.alloc([512]) for _ in range(2)]
        Bo = [Buf("fo0"), Buf("fo1")]
        rs = RX.alloc([512])
        Brs = Buf("frs")
        for tb in range(2):
            sl = slice(tb * 512, (tb + 1) * 512)
            pi = psn()
            for kc in range(16):
                act(sq[kc % 2], xT[:, kc, sl], AF.Square, [Bx], [Bsq[kc % 2]])
                mm(PS[pi][:, :], ones_bf[:, :], sq[kc % 2], kc == 0, kc == 15, [Bsq[kc % 2], Bconst], [PSB[pi]])
            act(rs, PS[pi][:, :], AF.Sqrt, [PSB[pi]], [Brs], bias=EPS, scale=1.0 / D)
            recip("dve", rs, rs, [Brs], [Brs])
            for kc in range(16):
                tt("dve", tmp[kc % 2], xT[:, kc, sl], rs, ALU.mult, [Bx, Brs], [Btmp[kc % 2]])
                act(o32[kc % 2], tmp[kc % 2], AF.Identity, [Btmp[kc % 2], Bvec], [Bo[kc % 2]],
                    scale=vecs[:, 0, V_FING + kc:V_FING + kc + 1])
                dma("sp", yT_d[g, kc * 128:(kc + 1) * 128, sl], o32[kc % 2], [Bo[kc % 2]], [], f"yo{kc % 2}")
        S.barrier()

    def phase_attn(l, g):
        RB.reset()
        RX.reset()
        qT = RB.alloc([T], BF16)
        kTb = RB.alloc([T], BF16)
        k32 = RB.alloc([T])
        v32 = RB.alloc([T])
        vtok = RB.alloc([8, 128], BF16)
        BqT, BkTb, Bk32, Bv32, Bvtok = (Buf(n) for n in ("qT", "kTb", "k32", "v32", "vtok"))
        NW = 3
        if g == 0:
            P32 = [RB.alloc([256]) for _ in range(NW)]
            PT = [RB.alloc([2, 128], BF16) for _ in range(NW)]
        else:
            P32 = [RB.alloc([896]) for _ in range(NW)]
            PT = [RB.alloc([7, 128], BF16) for _ in range(NW)]
            bmt = [RB.alloc([640]) for _ in range(2)]
            Bbm = [Buf("bm0"), Buf("bm1")]
            ckTb = RB.alloc([256], BF16)
            cvb = RB.alloc([2, 128], BF16)
            Bck, Bcvb = Buf("ckTb"), Buf("cvb")
        sm = [RB.alloc([4]) for _ in range(NW)]
        BP = [Buf(f"P{i}") for i in range(NW)]
        BPT = [Buf(f"PT{i}") for i in range(NW)]
        Bsm = [Buf(f"sm{i}") for i in range(NW)]
        inst = 0
        for h in range(cfg.get("nheads", 8)):
            if cfg.get("attn_stop", 9) <= 0:
                continue
            wt, wb = wget(("win_t", l, TQ + h))
            pis = proj(wt, wb, hT, Bh)
            for tb in range(2):
                act(qT[:, tb * 512:(tb + 1) * 512], PS[pis[tb]][:, :], AF.Identity, [PSB[pis[tb]]], [BqT], scale=DH ** -0.5)
            if cfg.get("parts", 9) <= 1:
                continue
            wt, wb = wget(("win_t", l, TK + h))
            pis = proj(wt, wb, hT, Bh)
            for tb in range(2):
                sl = slice(tb * 512, (tb + 1) * 512)
                if g == 0:
                    act(k32[:, sl], PS[pis[tb]][:, :], AF.Copy, [PSB[pis[tb]]], [Bk32])
                    cp("dve", kTb[:, sl], k32[:, sl], [Bk32], [BkTb])
                else:
                    cp("dve", kTb[:, sl], PS[pis[tb]][:, :], [PSB[pis[tb]]], [BkTb])
            if g == 0 and not cfg.get("no_kvout"):
                dma("sp", kT_o[l, h * 128:(h + 1) * 128, :], k32, [Bk32], [], "ko")
            if cfg.get("parts", 9) <= 2:
                continue
            wt, wb = wget(("win_t", l, TV + h))
            pis = proj(wt, wb, hT, Bh)
            for tb in range(2):
                act(v32[:, tb * 512:(tb + 1) * 512], PS[pis[tb]][:, :], AF.Copy, [PSB[pis[tb]]], [Bv32])
            if g == 0 and not cfg.get("no_kvout"):
                dma("sp", vT_o[l, h * 128:(h + 1) * 128, :], v32, [Bv32], [], "vo")
            if cfg.get("attn_stop", 9) <= 1:
                continue
            for half in range(2):
                pi = psn()
                for j in range(4):
                    t_ = half * 4 + j
                    tr(PS[pi][:, j * 128:(j + 1) * 128], v32[:, t_ * 128:(t_ + 1) * 128], ident[:, :], [Bv32], [PSB[pi]])
                cp("dve", vtok[:, half * 4:(half + 1) * 4, :], PS[pi][:, :].rearrange("p (a b) -> p a b", a=4),
                   [PSB[pi]], [Bvtok])
            if g == 1:
                dma("pool", ckTb, ckT_d[l, h], [], [Bck], "ck")
                dma("pool", cvb, cv_d[l, :, h * 128:(h + 1) * 128].rearrange("(c p) d -> p c d", p=128), [], [Bcvb], "cvb")
            if cfg.get("attn_stop", 9) <= 2:
                continue
            nq = 8
            for qi in range(nq):
                w_ = inst % NW
                inst += 1
                q0 = qi * 128
                if g == 0:
                    b = qi // 2
                    nk = 256
                    pi = psn()
                    mm(PS[pi][:, 0:256], qT[:, q0:q0 + 128], kTb[:, b * 256:(b + 1) * 256], True, True,
                       [BqT, BkTb], [PSB[pi]])
                    src = PS[pi][:, 0:256]
                    srcb = [PSB[pi]]
                else:
                    nk = 896
                    ks = KS_TOK[qi]
                    bi = (h * 8 + qi) % 2
                    dma("sp", bmt[bi], bm_d[l, h, qi], [], [Bbm[bi]], f"bm{bi}")
                    pa = psn()
                    mm(PS[pa][:, :], qT[:, q0:q0 + 128], kTb[:, ks:ks + 512], True, True, [BqT, BkTb], [PSB[pa]])
                    pb = psn()
                    mm(PS[pb][:, 0:128], qT[:, q0:q0 + 128], kTb[:, ks + 512:ks + 640], True, True, [BqT, BkTb], [PSB[pb]])
                    mm(PS[pb][:, 128:384], qT[:, q0:q0 + 128], ckTb, True, True, [BqT, Bck], [PSB[pb]])
                    tt("dve", P32[w_][:, 0:512], PS[pa][:, :], bmt[bi][:, 0:512], ALU.add, [PSB[pa], Bbm[bi]], [BP[w_]])
                    tt("dve", P32[w_][:, 512:640], PS[pb][:, 0:128], bmt[bi][:, 512:640], ALU.add, [PSB[pb], Bbm[bi]], [BP[w_]])
                    cp("dve", P32[w_][:, 640:896], PS[pb][:, 128:384], [PSB[pb]], [BP[w_]])
                    src = P32[w_]
                    srcb = [BP[w_]]
                red("dve", sm[w_][:, 0:1], src, ALU.max, srcb, [Bsm[w_]])
                ts("dve", sm[w_][:, 1:2], sm[w_][:, 0:1], -1.0, None, ALU.mult, None, [Bsm[w_]], [Bsm[w_]])
                act(P32[w_], src, AF.Exp, srcb + [Bsm[w_]], [BP[w_]], bias=sm[w_][:, 1:2])
                if cfg.get("attn_stop", 9) <= 3:
                    continue
                red("dve", sm[w_][:, 2:3], P32[w_], ALU.add, [BP[w_]], [Bsm[w_]])
                recip("dve", sm[w_][:, 3:4], sm[w_][:, 2:3], [Bsm[w_]], [Bsm[w_]])
                ts(PL, P32[w_], P32[w_], sm[w_][:, 3:4], None, ALU.mult, None, [BP[w_], Bsm[w_]], [BP[w_]])
                if cfg.get("attn_stop", 9) <= 4:
                    continue
                nchunk = nk // 128
                c = 0
                while c < nchunk:
                    n_ = min(4, nchunk - c)
                    pi2 = psn()
                    for j in range(n_):
                        tr(PS[pi2][:, j * 128:(j + 1) * 128], P32[w_][:, (c + j) * 128:(c + j + 1) * 128], ident[:, :],
                           [BP[w_]], [PSB[pi2]])
                    act(PT[w_][:, c:c + n_, :], PS[pi2][:, 0:n_ * 128].rearrange("p (a b) -> p a b", a=n_), AF.Copy,
                        [PSB[pi2]], [BPT[w_]])
                    c += n_
                pi3 = psn()
                if g == 0:
                    for c in range(2):
                        mm(PS[pi3][:, 0:128], vtok[:, b * 2 + c, :], PT[w_][:, c, :], c == 0, c == 1,
                           [Bvtok, BPT[w_]], [PSB[pi3]])
                else:
                    for c in range(5):
                        mm(PS[pi3][:, 0:128], vtok[:, ks // 128 + c, :], PT[w_][:, c, :], c == 0, False,
                           [Bvtok, BPT[w_]], [PSB[pi3]])
                    for c in range(2):
                        mm(PS[pi3][:, 0:128], cvb[:, c, :], PT[w_][:, 5 + c, :], False, c == 1,
                           [Bcvb, BPT[w_]], [PSB[pi3]])
                cp("dve", ymix[:, h, q0:q0 + 128], PS[pi3][:, 0:128], [PSB[pi3]], [By])
        S.barrier()

    PI = math.pi

    def phase_hyprep(l, L):
        RB.reset()
        RX.reset()
        n = L // 128
        nblk = max(1, L // 512)
        bw = min(L, 512)
        zfT = RB.alloc([L], parts=33)
        w1 = RB.alloc([64], parts=33)
        w2 = RB.alloc([64], parts=64)
        w3 = RB.alloc([2048], parts=64)
        hcol = RB.alloc([4], parts=64)
        fb = RB.alloc([2], parts=64)
        tn = RB.alloc([n])
        h1T = RB.alloc([L], parts=64)
        h2T = RB.alloc([L], parts=64)
        a_ = RB.alloc([bw], parts=64)
        m1 = RB.alloc([bw], parts=64)
        m2 = RB.alloc([bw], parts=64)
        absdec = RB.alloc([2048])
        env = [RB.alloc([512]) for _ in range(2)]
        F4 = [RB.alloc([4, n, 128], BF16) for _ in range(2)]
        hsp = [RX.alloc([2, 2, 256]) for _ in range(2)]
        filt = ymix[:, :, :].rearrange("p a b -> p (a b)")[:, 0:n * 2048].rearrange("p (m c) -> p m c", m=n)
        Bc, Bh1, Bh2, Ba, Bm1, Bm2, Bad = (Buf(x) for x in ("hyc", "h1T", "h2T", "a_", "m1", "m2", "absdec"))
        Benv = [Buf("env0"), Buf("env1")]
        BF4 = [Buf("F40"), Buf("F41")]
        Bhsp = [Buf("hsp0"), Buf("hsp1")]
        dma("sp", zfT, zfT_d[L], [], [Bc], "hy0")
        dma("sp", w1, hw1_d[l], [], [Bc], "hy0")
        dma("sp", w2, hw2_d[l], [], [Bc], "hy0")
        dma("sp", w3, hw3_d[l], [], [Bc], "hy0")
        dma("sp", hcol, hcol_d[l], [], [Bc], "hy0")
        dma("sp", tn, tn_d[L], [], [Bc], "hy0")
        dma("sp", absdec, hdec_d[l].partition_broadcast(128), [], [Bad], "hy1")
        act(absdec, absdec, AF.Abs, [Bad], [Bad])
        tt("dve", fb[:, 0:1], hcol[:, 0:1], hcol[:, 2:3], ALU.mult, [Bc], [Bc])
        tt("dve", fb[:, 1:2], hcol[:, 1:2], hcol[:, 3:4], ALU.mult, [Bc], [Bc])

        def sin_layer(dst, Bdst, lhsT, K, src, Bsrc, fcol, bcol):
            for blk in range(nblk):
                sl = slice(blk * bw, (blk + 1) * bw)
                pi = psn()
                mm(PS[pi][0:64, 0:bw], lhsT, src[0:K, sl], True, True, [Bc] + Bsrc, [PSB[pi]])
                ts("dve", a_, PS[pi][0:64, 0:bw], hcol[:, fcol:fcol + 1], fb[:, bcol:bcol + 1], ALU.mult, ALU.add,
                   [PSB[pi], Bc], [Ba])
                ts("dve", m1, a_, -PI, 2 * PI, ALU.is_lt, ALU.mult, [Ba], [Bm1])
                ts(PL, m2, a_, PI, -2 * PI, ALU.is_gt, ALU.mult, [Ba], [Bm2])
                tt("dve", a_, a_, m1, ALU.add, [Ba, Bm1], [Ba])
                tt("dve", a_, a_, m2, ALU.add, [Ba, Bm2], [Ba])
                act(dst[:, sl], a_, AF.Sin, [Ba], [Bdst])

        sin_layer(h1T, Bh1, w1[0:33, :], 33, zfT, [], 2, 0)
        sin_layer(h2T, Bh2, w2[0:64, :], 64, h1T, [Bh1], 3, 1)
        k_ = 0
        for mc in range(n):
            for nb in range(4):
                e_ = k_ % 2
                k_ += 1
                pi = psn()
                mm(PS[pi][:, :], h2T[:, mc * 128:(mc + 1) * 128], w3[:, nb * 512:(nb + 1) * 512], True, True,
                   [Bh2, Bc], [PSB[pi]])
                act(env[e_], absdec[:, nb * 512:(nb + 1) * 512], AF.Exp, [Bad, Bc], [Benv[e_]], scale=tn[:, mc:mc + 1])
                tt("dve", filt[:, mc, nb * 512:(nb + 1) * 512], PS[pi][:, :], env[e_], ALU.mult,
                   [PSB[pi], Benv[e_]], [By])
        for kf in range(n):
            f_ = kf % 2
            dma("sp", F4[f_], F4_d[L][kf], [], [BF4[f_]], f"F4{f_}")
            for o in range(2):
                e_ = (kf * 2 + o) % 2
                pr = psn()
                pim = psn()
                for (pp, ia, ib) in ((pr, 0, 1), (pim, 2, 3)):
                    for mc in range(n):
                        mm(PS[pp][:, :], F4[f_][:, ia, mc, :], filt[:, mc, o * 1024:o * 1024 + 512], mc == 0, False,
                           [BF4[f_], By], [PSB[pp]])
                    for mc in range(n):
                        mm(PS[pp][:, :], F4[f_][:, ib, mc, :], filt[:, mc, o * 1024 + 512:o * 1024 + 1024], False, mc == n - 1,
                           [BF4[f_], By], [PSB[pp]])
                act(hsp[e_][:, :, 0, :], PS[pr][:, :].rearrange("p (a b) -> p a b", a=2), AF.Copy, [PSB[pr]], [Bhsp[e_]])
                cp("dve", hsp[e_][:, :, 1, :], PS[pim][:, :].rearrange("p (a b) -> p a b", a=2), [PSB[pim]], [Bhsp[e_]])
                dma("sp", hspec[kf, :, o, :], hsp[e_].rearrange("p a b c -> p (a b c)"), [Bhsp[e_]], [Bhspec], f"hso{e_}")
        S.barrier()

    def conv3(u3, Bu, nseq, Ls, wcols, bias_col, dst, Bdst, t1, t2, Bt1, Bt2, final_eng="dve", final_func=None):
        w0, w1_, w2_ = wcols
        if bias_col is not None:
            act(t1, u3[:, :, 0:Ls], AF.Identity, [Bu, Bvec], [Bt1], bias=bias_col, scale=w0)
        else:
            act(t1, u3[:, :, 0:Ls], AF.Identity, [Bu, Bvec], [Bt1], scale=w0)
        stt("dve", t2, u3[:, :, 1:Ls + 1], w1_, t1, ALU.mult, ALU.add, [Bu, Bvec, Bt1], [Bt2])
        if final_func is None:
            stt(final_eng, dst, u3[:, :, 2:Ls + 2], w2_, t2, ALU.mult, ALU.add, [Bu, Bvec, Bt2], [Bdst])
        else:
            stt(final_eng, t1, u3[:, :, 2:Ls + 2], w2_, t2, ALU.mult, ALU.add, [Bu, Bvec, Bt2], [Bt1])
            act(dst, t1, final_func, [Bt1], [Bdst])

    def evac_pad(u3, Bu, pis, nseq, Ls):
        for tb in range(2):
            if nseq == 4:
                act(u3[:, 2 * tb:2 * tb + 2, 1:Ls + 1], PS[pis[tb]][:, :].rearrange("p (a b) -> p a b", a=2), AF.Copy,
                    [PSB[pis[tb]]], [Bu])
            else:
                act(u3[:, 0:1, 1 + tb * 512:1 + (tb + 1) * 512], PS[pis[tb]][:, :].rearrange("p (a b) -> p a b", a=1),
                    AF.Copy, [PSB[pis[tb]]], [Bu])

    def phase_hyena(l, g):
        RB.reset()
        RX.reset()
        nseq, Ls = (4, 256) if g == 0 else (1, 1024)
        n = Ls // 128
        upad = AA([nseq * (Ls + 2)])
        u3 = upad.rearrange("p (s l) -> p s l", s=nseq)
        Bu = Buf("upad")
        memset(PL, upad, 0.0, [Bu])
        t1 = AA([nseq, Ls])
        t2 = AA([nseq, Ls])
        Bt1, Bt2 = Buf("t1"), Buf("t2")
        z = AA([2, T])
        x1 = AA([2, T], BF16)
        x2 = AA([2, T], BF16)
        zt = AA([8, 256], BF16)
        Yr = AA([8, 256], BF16)
        Wv = AA([8, 256], BF16)
        zcs = [AA([512]) for _ in range(2)]
        HH = [AA([512]) for _ in range(2)]
        HH2 = [AA([512]) for _ in range(2)]
        m12 = [AA([512]) for _ in range(2)]
        m34 = [AA([512]) for _ in range(2)]
        F2 = [AA([2, n, 128], BF16) for _ in range(2)]
        I2 = [AA([2, Ls], BF16) for _ in range(2)]
        tmpo = [AA([512]) for _ in range(2)]
        Bz, Bx1, Bx2, Bzt, BYr, BW = (Buf(x) for x in ("z", "x1", "x2", "zt", "Yr", "W"))
        Bzcs = [Buf("zcs0"), Buf("zcs1")]
        BHH = [Buf("HH0"), Buf("HH1")]
        BHH2 = [Buf("HH20"), Buf("HH21")]
        Bm12 = [Buf("m120"), Buf("m121")]
        Bm34 = [Buf("m340"), Buf("m341")]
        BF2 = [Buf("F20"), Buf("F21")]
        BI2 = [Buf("I20"), Buf("I21")]
        Btmpo = [Buf("tmpo0"), Buf("tmpo1")]
        for hf in range(2):
            for part in range(3):
                for cc in range(2):
                    ch = part * 4 + 2 * hf + cc
                    wt, wb = wget(("win_t", l, THY + ch))
                    pis = proj(wt, wb, hT, Bh)
                    evac_pad(u3, Bu, pis, nseq, Ls)
                    wc = [vecs[:, l, V_HCW + tap * 12 + ch:V_HCW + tap * 12 + ch + 1] for tap in range(3)]
                    bc = vecs[:, l, V_HCB + ch:V_HCB + ch + 1]
                    if part == 0:
                        dst, Bd = z[:, cc, :], Bz
                    elif part == 1:
                        dst, Bd = x1[:, cc, :], Bx1
                    else:
                        dst, Bd = x2[:, cc, :], Bx2
                    conv3(u3, Bu, nseq, Ls, wc, bc, dst.rearrange("p (s l) -> p s l", s=nseq), Bd, t1, t2, Bt1, Bt2)
            for o in range(2):
                for t2_ in range(0, 8, 2):
                    pi = psn()
                    for dt_ in range(2):
                        for cc in range(2):
                            tr(PS[pi][:, dt_ * 256 + cc * 128:dt_ * 256 + (cc + 1) * 128],
                               z[:, cc, (t2_ + dt_) * 128:(t2_ + dt_ + 1) * 128], ident[:, :], [Bz], [PSB[pi]])
                    act(zt[:, t2_:t2_ + 2, :], PS[pi][:, :].rearrange("p (a b) -> p a b", a=2), AF.Copy, [PSB[pi]], [Bzt])
                k_ = 0
                for kf in range(n):
                    f_ = kf % 2
                    dma("sp", F2[f_], F2_d[Ls][kf], [], [BF2[f_]], f"F2{f_}")
                    hsl = hspec[kf, :, o, :]
                    dma("sp", HH[f_], hsl[:, hf * 512:(hf + 1) * 512], [Bhspec], [BHH[f_]], f"HH{f_}")
                    dma("sp", HH2[f_][:, 0:256], hsl[:, hf * 512 + 256:hf * 512 + 512], [Bhspec], [BHH2[f_]], f"HHb{f_}")
                    dma("sp", HH2[f_][:, 256:512], hsl[:, hf * 512:hf * 512 + 256], [Bhspec], [BHH2[f_]], f"HHb{f_}")
                    for s in range(nseq):
                        e_ = k_ % 2
                        k_ += 1
                        pi = psn()
                        for half, im in ((0, 0), (1, 1)):
                            for mc in range(n):
                                mm(PS[pi][:, half * 256:(half + 1) * 256], F2[f_][:, im, mc, :], zt[:, s * n + mc, :],
                                   mc == 0, mc == n - 1, [BF2[f_], Bzt], [PSB[pi]])
                        act(zcs[e_], PS[pi][:, :], AF.Copy, [PSB[pi]], [Bzcs[e_]])
                        tt("dve", m12[e_], zcs[e_], HH[f_], ALU.mult, [Bzcs[e_], BHH[f_]], [Bm12[e_]])
                        tt(PL, m34[e_], zcs[e_], HH2[f_], ALU.mult, [Bzcs[e_], BHH2[f_]], [Bm34[e_]])
                        tt("dve", Yr[:, s * n + kf, :], m12[e_][:, 0:256], m12[e_][:, 256:512], ALU.add, [Bm12[e_]], [BYr])
                        tt(PL, Wv[:, s * n + kf, :], m34[e_][:, 256:512], m34[e_][:, 0:256], ALU.subtract, [Bm34[e_]], [BW])
                acc = [psn() for _ in range(4)]
                if g == 1:
                    for kf in range(n):
                        f_ = kf % 2
                        dma("sp", I2[f_], I2_d[Ls][kf], [], [BI2[f_]], f"I2{f_}")
                        for cc in range(2):
                            for j in range(2):
                                pi = acc[cc * 2 + j]
                                for im, src_, Bs in ((0, Yr, BYr), (1, Wv, BW)):
                                    mm(PS[pi][:, :], src_[:, kf, cc * 128:(cc + 1) * 128], I2[f_][:, im, j * 512:(j + 1) * 512],
                                       kf == 0 and im == 0, kf == n - 1 and im == 1, [Bs, BI2[f_]], [PSB[pi]])
                else:
                    for kf in range(n):
                        dma("sp", I2[kf], I2_d[Ls][kf], [], [BI2[kf]], f"I2{kf}")
                    for cc in range(2):
                        for j in range(2):
                            pi = acc[cc * 2 + j]
                            for s2 in range(2):
                                s = j * 2 + s2
                                for kf in range(n):
                                    for im, src_, Bs in ((0, Yr, BYr), (1, Wv, BW)):
                                        mm(PS[pi][:, s2 * 256:(s2 + 1) * 256], src_[:, s * n + kf, cc * 128:(cc + 1) * 128],
                                           I2[kf][:, im, :], kf == 0 and im == 0, kf == n - 1 and im == 1,
                                           [Bs, BI2[kf]], [PSB[pi]])
                k_ = 0
                for cc in range(2):
                    for j in range(2):
                        pi = acc[cc * 2 + j]
                        e_ = k_ % 2
                        k_ += 1
                        sl = slice(j * 512, (j + 1) * 512)
                        sk = vecs[:, l, V_HSKIP + o * 4 + 2 * hf + cc:V_HSKIP + o * 4 + 2 * hf + cc + 1]
                        stt("dve", tmpo[e_], z[:, cc, sl], sk, PS[pi][:, :], ALU.mult, ALU.add, [Bz, Bvec, PSB[pi]], [Btmpo[e_]])
                        if o == 0:
                            tt(PL, z[:, cc, sl], tmpo[e_], x1[:, cc, sl], ALU.mult, [Btmpo[e_], Bx1], [Bz])
                        else:
                            tt(PL, ymix[:, 8 + 2 * hf + cc, sl], tmpo[e_], x2[:, cc, sl], ALU.mult, [Btmpo[e_], Bx2], [By])
        S.barrier()

    def phase_gdn(l, g):
        RB.reset()
        RX.reset()

        def A(shape, dtype=F32, parts=128):
            n = 1
            for s_ in shape:
                n *= s_
            nb = (n * (4 if dtype == F32 else 2) + 31) // 32 * 32
            reg = RB if RB.off + nb <= RB.ncols * 4 else RX
            return reg.alloc(shape, dtype, parts)

        nseq, Ls = (4, 256) if g == 0 else (1, 1024)
        cps = Ls // 64
        upad = A([nseq * (Ls + 2)])
        u3 = upad.rearrange("p (s l) -> p s l", s=nseq)
        Bu = Buf("gupad")
        memset(PLG, upad, 0.0, [Bu])
        t1f = A([T])
        t2f = A([T])
        t1 = t1f.rearrange("p (s l) -> p s l", s=nseq)
        t2 = t2f.rearrange("p (s l) -> p s l", s=nseq)
        Bt1, Bt2 = Buf("gt1"), Buf("gt2")
        bg = A([T], parts=40)
        gcl = A([2], parts=40)
        nA = A([1], parts=40)
        bgt = A([16, 40], parts=64)
        gc = A([16, 8], parts=64)
        egc = A([16, 8], parts=64)
        kdf = A([16, 8], parts=64)
        begc = A([16, 8], parts=64)
        gtot = A([16, 8])
        Bbg, Bgs = Buf("bg"), Buf("gsmall")
        memset(PLG, bg, 0.0, [Bbg])
        dma("sp", gcl, gcol_d[l], [], [Bgs], "g0")
        act(nA[32:40, :], gcl[32:40, 0:1], AF.Exp, [Bgs], [Bgs])
        ts("dve", nA[32:40, :], nA[32:40, :], -1.0, None, ALU.mult, None, [Bgs], [Bgs])
        wt, wb = wget(("win_t", l, TBA))
        pis = proj(wt, wb, hT, Bh)
        e1 = t1f
        for tb in range(2):
            sl = slice(tb * 512, (tb + 1) * 512)
            pi = pis[tb]
            act(bg[0:8, sl], PS[pi][0:8, :], AF.Sigmoid, [PSB[pi]], [Bbg])
            act(e1[32:40, sl], PS[pi][32:40, :], AF.Exp, [PSB[pi], Bgs], [Bt1], bias=gcl[32:40, 1:2])
            act(e1[32:40, sl], e1[32:40, sl], AF.Ln, [Bt1], [Bt1], bias=1.0)
            ts("dve", bg[32:40, sl], e1[32:40, sl], nA[32:40, 0:1], None, ALU.mult, None, [Bt1, Bgs], [Bbg])
        for half in range(2):
            pi = psn()
            for j in range(8):
                ci = half * 8 + j
                tr(PS[pi][0:64, j * 40:(j + 1) * 40], bg[0:40, ci * 64:(ci + 1) * 64], ident[0:40, 0:40], [Bbg], [PSB[pi]])
            cp("dve", bgt[:, half * 8:(half + 1) * 8, :], PS[pi][0:64, 0:320].rearrange("p (a b) -> p a b", a=8),
               [PSB[pi]], [Bgs])
        g8 = A([16, 8], parts=64)
        gF4 = A([16, 4], parts=64)
        gB4 = A([16, 4], parts=64)
        cp("dve", g8, bgt[:, :, 32:40], [Bgs], [Bgs])
        cp("dve", gF4, bgt[:, :, 32:36], [Bgs], [Bgs])
        cp("dve", gB4, bgt[:, :, 36:40], [Bgs], [Bgs])
        pi = psn()
        mm(PS[pi][0:64, 0:64], gmask[:, 0, 0, :], gF4.rearrange("p a b -> p (a b)"), True, True, [Bconst, Bgs], [PSB[pi]])
        mm(PS[pi][0:64, 64:128], gmask[:, 1, 0, :], gB4.rearrange("p a b -> p (a b)"), True, True, [Bconst, Bgs], [PSB[pi]])
        cp("dve", gc[:, :, 0:4], PS[pi][0:64, 0:64].rearrange("p (a b) -> p a b", b=4), [PSB[pi]], [Bgs])
        cp("dve", gc[:, :, 4:8], PS[pi][0:64, 64:128].rearrange("p (a b) -> p a b", b=4), [PSB[pi]], [Bgs])
        pi2 = psn()
        ptot = PS[pi2][:, 0:128].rearrange("p (a b) -> p a b", b=8)
        mm(PS[pi2][:, 0:128], ones32[0:64, :], g8.rearrange("p a b -> p (a b)"), True, True, [Bconst, Bgs], [PSB[pi2]])
        act(egc, gc, AF.Exp, [Bgs], [Bgs])
        cp("dve", gtot, ptot, [PSB[pi2]], [Bgs])
        tt("dve", kdf, gtot[0:64], gc, ALU.subtract, [Bgs], [Bgs])
        act(kdf, kdf, AF.Exp, [Bgs], [Bgs])
        act(gtot, gtot, AF.Exp, [Bgs], [Bgs])
        tt("dve", begc, bgt[:, :, 0:8], egc, ALU.mult, [Bgs], [Bgs])

        QT = A([T])
        KT = A([T])
        VT = A([T])
        szT = A([T], BF16)
        sqb = A([512], BF16)
        raw_tx = A([512])
        rsb = raw_tx
        Ktok = A([16, 128], BF16, parts=64)
        Vtok = A([16, 128], BF16, parts=64)
        oacc = A([16, 128], parts=64)
        rso = A([16], parts=64)
        nst = 2 * nseq
        Sst = [A([128]) for _ in range(nst)]
        BS = [Buf(f"S{i}") for i in range(nst)]
        BQT, BKT, BVT, Bsz, Bsq, Brs, BKt, BVt, Bo, Brso = (Buf(x) for x in (
            "QT", "KT", "VT", "szT", "gsq", "grs", "Ktok", "Vtok", "oacc", "rso"))
        tx = raw_tx[0:64, :].rearrange("p (a b) -> p a b", a=8)
        PTb = A([8, 64], parts=64)
        BPT = Buf("PTb")
        Nn = [A([8, 64], parts=64) for _ in range(2)]
        NTt = [A([8, 64], parts=64) for _ in range(2)]
        Btx = Brs
        BN = [Buf("N0"), Buf("N1")]
        BNT = [Buf("NT0"), Buf("NT1")]
        Xs = [A([8, 256], parts=64) for _ in range(2)]
        wTs = [A([8, 64]) for _ in range(2)]
        qdTs = [A([8, 64]) for _ in range(2)]
        kds = [A([8, 128], parts=64) for _ in range(2)]
        Ats = [A([8, 64], parts=64) for _ in range(2)]
        BXs = [Buf("X0"), Buf("X1")]
        BwTs = [Buf("wT0"), Buf("wT1")]
        Bqds = [Buf("qd0"), Buf("qd1")]
        Bkds = [Buf("kd0"), Buf("kd1")]
        BAts = [Buf("At0"), Buf("At1")]
        vnew = [A([128], parts=64) for _ in range(4)]
        Bvn = [Buf(f"vn{i}") for i in range(4)]
        i64 = ident[0:64, 0:64]
        vn_i = [0]
        GST = cfg.get("gdn_stop", 9)

        def gen_solve(si_, hd, d, ci0):
            col = d * 4 + hd
            spi = [0]

            def psn():
                i = spi[0] % 6
                spi[0] += 1
                return i

            Mc = gmask[:, d, :, :]
            Ms = gmask[:, 2 + d, :, :]
            X, BX, wT, BwT, qdT, Bqd, kd, Bkd, At, BAt = (Xs[si_], BXs[si_], wTs[si_], BwTs[si_], qdTs[si_], Bqds[si_],
                                                          kds[si_], Bkds[si_], Ats[si_], BAts[si_])
            cs8 = slice(ci0, ci0 + 8)
            tok8 = slice(ci0 * 64, (ci0 + 8) * 64)
            gb = bgt[:, cs8, 32 + col:33 + col].to_broadcast([64, 8, 64])
            tt(PLG, tx, Mc, gb, ALU.mult, [Bconst, Bgs], [Btx])
            pR = psn()
            for e in range(8):
                mm(PS[pR][:, e * 64:(e + 1) * 64], ones32[0:64, :], tx[:, e, :], True, True, [Bconst, Btx], [PSB[pR]])
            pG = psn()
            pQ = psn()
            for e in range(8):
                ck = slice((ci0 + e) * 64, (ci0 + e + 1) * 64)
                mm(PS[pG][0:64, e * 64:(e + 1) * 64], KT[:, ck], KT[:, ck], True, True, [BKT], [PSB[pG]])
                mm(PS[pQ][0:64, e * 64:(e + 1) * 64], KT[:, ck], QT[:, ck], True, True, [BKT, BQT], [PSB[pQ]])
            tt("dve", X[:, :, 0:128], Vtok[:, cs8, :], bgt[:, cs8, col:col + 1].to_broadcast([64, 8, 128]),
               ALU.mult, [BVt, Bgs], [BX])
            tt("dve", X[:, :, 128:256], Ktok[:, cs8, :], begc[:, cs8, col:col + 1].to_broadcast([64, 8, 128]),
               ALU.mult, [BKt, Bgs], [BX])
            tt("dve", kd, Ktok[:, cs8, :], kdf[:, cs8, col:col + 1].to_broadcast([64, 8, 128]), ALU.mult,
               [BKt, Bgs], [Bkd])
            yield
            pRv = PS[pR][:, :].rearrange("p (a b) -> p a b", a=8)
            tt("dve", tx, pRv[0:64], gc[:, cs8, col:col + 1].to_broadcast([64, 8, 64]), ALU.subtract,
               [PSB[pR], Bgs], [Btx])
            cp("dve", qdT, pRv, [PSB[pR]], [Bqd])
            ts("dve", Nn[0], tx, 0.0, None, ALU.max, None, [Btx], [BN[0]])
            ts("dve", At, tx, 0.0, None, ALU.min, None, [Btx], [BAt])
            act(Nn[0], Nn[0], AF.Exp, [BN[0]], [BN[0]], scale=-1.0)
            act(At, At, AF.Exp, [BAt], [BAt])
            act(qdT, qdT, AF.Exp, [Bqd], [Bqd])
            yield
            pGv = PS[pG][0:64, :].rearrange("p (a b) -> p a b", a=8)
            pQv = PS[pQ][0:64, :].rearrange("p (a b) -> p a b", a=8)
            tt("dve", Nn[0], Nn[0], pGv, ALU.mult, [BN[0], PSB[pG]], [BN[0]])
            tt("dve", Nn[0], Nn[0], Ms, ALU.mult, [BN[0], Bconst], [BN[0]])
            tt("dve", Nn[0], Nn[0], bgt[:, cs8, col:col + 1].to_broadcast([64, 8, 64]), ALU.mult,
               [BN[0], Bgs], [BN[0]])
            tt("dve", At, At, pQv, ALU.mult, [BAt, PSB[pQ]], [BAt])
            tt("dve", At, At, Mc, ALU.mult, [BAt, Bconst], [BAt])
            tt("dve", qdT, QT[:, tok8].rearrange("p (a b) -> p a b", a=8), qdT, ALU.mult, [BQT, Bqd], [Bqd])
            pT = psn()
            for e in range(8):
                tr(PS[pT][0:64, e * 64:(e + 1) * 64], Nn[0][:, e, :], i64, [BN[0]], [PSB[pT]])
            yield
            cp("act", NTt[0], PS[pT][0:64, :].rearrange("p (a b) -> p a b", a=8), [PSB[pT]], [BNT[0]])

            stt("dve", PTb, NTt[0], -1.0, i64.unsqueeze(1).to_broadcast([64, 8, 64]), ALU.mult, ALU.add,
                [BNT[0], Bconst], [BPT])
            cur = 0
            for lev in range(5):
                nxt = 1 - cur
                pN1 = psn()
                for e in range(8):
                    mm(PS[pN1][0:64, e * 64:(e + 1) * 64], NTt[cur][:, e, :], Nn[cur][:, e, :], True, True,
                       [BN[cur], BNT[cur]], [PSB[pN1]])
                pN2 = None
                if lev < 4:
                    pN2 = psn()
                    for e in range(8):
                        mm(PS[pN2][0:64, e * 64:(e + 1) * 64], Nn[cur][:, e, :], NTt[cur][:, e, :], True, True,
                           [BN[cur], BNT[cur]], [PSB[pN2]])
                yield
                cp("act", Nn[nxt], PS[pN1][0:64, :].rearrange("p (a b) -> p a b", a=8), [PSB[pN1]], [BN[nxt]])
                if pN2 is not None:
                    cp("act", NTt[nxt], PS[pN2][0:64, :].rearrange("p (a b) -> p a b", a=8), [PSB[pN2]], [BNT[nxt]])
                pP = psn()
                for e in range(8):
                    mm(PS[pP][0:64, e * 64:(e + 1) * 64], Nn[nxt][:, e, :], PTb[:, e, :], True, True,
                       [BN[nxt], BPT], [PSB[pP]])
                yield
                tt("dve", PTb, PTb, PS[pP][0:64, :].rearrange("p (a b) -> p a b", a=8), ALU.add, [BPT, PSB[pP]], [BPT])
                cur = nxt
            banks = [psn() for _ in range(4)]
            for e in range(8):
                pb = banks[e // 2]
                mm(PS[pb][0:64, (e % 2) * 256:(e % 2 + 1) * 256], PTb[:, e, :], X[:, e, :], True, True,
                   [BPT, BX], [PSB[pb]])
            yield
            for b_ in range(4):
                pb = banks[b_]
                cp("act" if b_ % 2 == 0 else "dve", X[:, 2 * b_:2 * b_ + 2, :],
                   PS[pb][0:64, :].rearrange("p (a b) -> p a b", a=2), [PSB[pb]], [BX])
            pW = psn()
            for e in range(8):
                tr(PS[pW][:, e * 64:(e + 1) * 64], X[:, e, 128:256], i64, [BX], [PSB[pW]])
            yield
            cp("act", wT, PS[pW][:, :].rearrange("p (a b) -> p a b", a=8), [PSB[pW]], [BwT])

        def gen_scan(si_, hd, d, ci0, first):
            col = d * 4 + hd
            X, BX, wT, BwT, qdT, Bqd, kd, Bkd, At, BAt = (Xs[si_], BXs[si_], wTs[si_], BwTs[si_], qdTs[si_], Bqds[si_],
                                                          kds[si_], Bkds[si_], Ats[si_], BAts[si_])
            if g == 1:
                chains = [(0, list(range(8)) if d == 0 else list(range(7, -1, -1)))]
            else:
                chains = []
                for s2 in range(2):
                    es = list(range(s2 * 4, s2 * 4 + 4))
                    chains.append((ci0 // 4 + s2, es if d == 0 else es[::-1]))
            nstep = len(chains[0][1])
            pA, pB = 6, 7
            for step in range(nstep):
                vis = []
                for c_, (s, es) in enumerate(chains):
                    e = es[step]
                    si = d * nseq + s
                    mm(PS[pA][0:64, c_ * 256:c_ * 256 + 128], wT[:, e, :], Sst[si], True, True, [BwT, BS[si]], [PSB[pA]])
                yield
                for c_, (s, es) in enumerate(chains):
                    e = es[step]
                    si = d * nseq + s
                    vi = vn_i[0] % 4
                    vn_i[0] += 1
                    vis.append(vi)
                    tt("dve", vnew[vi], X[:, e, 0:128], PS[pA][0:64, c_ * 256:c_ * 256 + 128], ALU.subtract,
                       [BX, PSB[pA]], [Bvn[vi]])
                for c_, (s, es) in enumerate(chains):
                    e = es[step]
                    si = d * nseq + s
                    vi = vis[c_]
                    po = PS[pA][0:64, c_ * 256 + 128:c_ * 256 + 256]
                    mm(po, qdT[:, e, :], Sst[si], True, False, [Bqd, BS[si]], [PSB[pA]])
                    mm(po, At[:, e, :], vnew[vi], False, True, [BAt, Bvn[vi]], [PSB[pA]])
                    mm(PS[pB][:, c_ * 128:(c_ + 1) * 128], kd[:, e, :], vnew[vi], True, True, [Bkd, Bvn[vi]], [PSB[pB]])
                yield
                for c_, (s, es) in enumerate(chains):
                    e = es[step]
                    ci = ci0 + e
                    si = d * nseq + s
                    po = PS[pA][0:64, c_ * 256 + 128:c_ * 256 + 256]
                    if first:
                        cp("dve", oacc[:, ci, :], po, [PSB[pA]], [Bo])
                    else:
                        tt("dve", oacc[:, ci, :], oacc[:, ci, :], po, ALU.add, [Bo, PSB[pA]], [Bo])
                    stt("dve", Sst[si], Sst[si], gtot[:, ci, col:col + 1], PS[pB][:, c_ * 128:(c_ + 1) * 128], ALU.mult, ALU.add,
                        [BS[si], Bgs, PSB[pB]], [BS[si]])

        def drive(*gens):
            gens = [g_ for g_ in gens if g_ is not None]
            while gens:
                for g_ in list(gens):
                    try:
                        next(g_)
                    except StopIteration:
                        gens.remove(g_)

        for hd in range(4):
            if GST <= 1:
                continue
            for part, dst, Bd in ((0, QT, BQT), (1, KT, BKT), (2, VT, BVT)):
                ch = part * 4 + hd
                wt, wb = wget(("win_t", l, TGQ + ch))
                pis = proj(wt, wb, hT, Bh)
                evac_pad(u3, Bu, pis, nseq, Ls)
                wc = [vecs[:, l, V_GCW + tap * 12 + ch:V_GCW + tap * 12 + ch + 1] for tap in range(3)]
                conv3(u3, Bu, nseq, Ls, wc, None, dst.rearrange("p (s l) -> p s l", s=nseq), Bd, t1, t2, Bt1, Bt2,
                      final_func=AF.Silu)
                if part < 2:
                    for tb in range(2):
                        sl = slice(tb * 512, (tb + 1) * 512)
                        act(sqb, dst[:, sl], AF.Square, [Bd], [Bsq])
                        pi = psn()
                        mm(PS[pi][:, :], ones_bf[:, :], sqb, True, True, [Bsq, Bconst], [PSB[pi]])
                        act(rsb, PS[pi][:, :], AF.Sqrt, [PSB[pi]], [Brs], bias=EPS)
                        recip("dve", rsb, rsb, [Brs], [Brs])
                        stt("dve", dst[:, sl], dst[:, sl], (DH ** -0.5 if part == 0 else 1.0), rsb, ALU.mult, ALU.mult,
                            [Bd, Brs], [Bd])
            wt, wb = wget(("win_t", l, TGZ + hd))
            pis = proj(wt, wb, hT, Bh)
            for tb in range(2):
                act(szT[:, tb * 512:(tb + 1) * 512], PS[pis[tb]][:, :], AF.Silu, [PSB[pis[tb]]], [Bsz])
            for src_, Bs_, dst_, Bd_ in ((KT, BKT, Ktok, BKt), (VT, BVT, Vtok, BVt)):
                for q4 in range(4):
                    pi = psn()
                    for j in range(4):
                        ci = q4 * 4 + j
                        tr(PS[pi][0:64, j * 128:(j + 1) * 128], src_[:, ci * 64:(ci + 1) * 64], ident[:, :], [Bs_], [PSB[pi]])
                    cp("act", dst_[:, q4 * 4:(q4 + 1) * 4, :], PS[pi][0:64, :].rearrange("p (a b) -> p a b", a=4),
                       [PSB[pi]], [Bd_])
            for d in range(2):
                for s in range(nseq):
                    si = d * nseq + s
                    if g == 0:
                        memset(PLG, Sst[si], 0.0, [BS[si]])
                    else:
                        dma("sp", Sst[si], (sf0_d if d == 0 else sb0_d)[l, hd], [], [BS[si]], f"st{si}")
            if GST <= 2:
                continue
            subs = []
            for sbi in range(2):
                subs.append((0, sbi * 8))
                subs.append((1, (1 - sbi) * 8))
            prev = None
            for i_, (d, ci0) in enumerate(subs):
                drive(gen_solve(i_ % 2, hd, d, ci0), prev)
                prev = gen_scan(i_ % 2, hd, d, ci0, i_ < 2)
            drive(prev)
            if g == 0:
                for d in range(2):
                    for s in range(nseq):
                        si = d * nseq + s
                        dma("sp", (sf_o if d == 0 else sb_o)[l, s, hd], Sst[si], [BS[si]], [], f"so{si}")
            for half in range(2):
                dst_sq = (t1f if half == 0 else t2f)[0:64, :].rearrange("p (a b) -> p a b", a=8)
                tt(PLG, dst_sq, oacc[:, half * 8:(half + 1) * 8, :], oacc[:, half * 8:(half + 1) * 8, :], ALU.mult,
                   [Bo], [Bt1 if half == 0 else Bt2])
                red("dve", rso[:, half * 8:(half + 1) * 8], dst_sq, ALU.add, [Bt1 if half == 0 else Bt2], [Brso])
            act(rso, rso, AF.Sqrt, [Brso], [Brso], bias=EPS, scale=1.0 / 128)
            recip("dve", rso, rso, [Brso], [Brso])
            tt("dve", oacc, oacc, rso.unsqueeze(2).to_broadcast([64, 16, 128]), ALU.mult, [Bo, Brso], [Bo])
            for half in range(2):
                pi = psn()
                for j in range(8):
                    ci = half * 8 + j
                    tr(PS[pi][:, j * 64:(j + 1) * 64], oacc[:, ci, :], i64, [Bo], [PSB[pi]])
                sl = slice(half * 512, (half + 1) * 512)
                stt("dve", ymix[:, 12 + hd, sl], PS[pi][:, :], vecs[:, l, V_GNG:V_GNG + 1], szT[:, sl], ALU.mult, ALU.mult,
                    [PSB[pi], Bvec, Bsz], [By])
        S.barrier()

    def phase_merge(l, g):
        RX.reset()
        merged = RX.alloc([16, T], BF16)
        Bmg = Buf("merged")
        sig = [RX.alloc([512]) for _ in range(2)]
        Bsig = [Buf("sig0"), Buf("sig1")]
        accm = [RX.alloc([512]) for _ in range(2)]
        Bacc = [Buf("acc0"), Buf("acc1")]
        tmpm = [RX.alloc([512]) for _ in range(2)]
        Btm = [Buf("mt0"), Buf("mt1")]
        kranges = [range(0, 8), range(8, 12), range(12, 16)]
        for n_ in range(16):
            wtp, wbp = wget(("wp_t", l, n_))
            pbs = [proj(wtp, wbp, ymix, By, kranges[i]) for i in range(3)]
            for i in range(3):
                wtg, wbg = wget(("win_t", l, TGATE + i * 16 + n_))
                pg = proj(wtg, wbg, hT, Bh)
                pb = pbs[i]
                bcol = vecs[:, l, V_BGATE + i * 16 + n_:V_BGATE + i * 16 + n_ + 1]
                for tb in range(2):
                    sl = slice(tb * 512, (tb + 1) * 512)
                    act(sig[tb], PS[pg[tb]][:, :], AF.Sigmoid, [PSB[pg[tb]], Bvec], [Bsig[tb]], bias=bcol)
                    if i == 0:
                        tt("dve", accm[tb], PS[pb[tb]][:, :], sig[tb], ALU.mult, [PSB[pb[tb]], Bsig[tb]], [Bacc[tb]])
                    else:
                        tt("dve", tmpm[tb], PS[pb[tb]][:, :], sig[tb], ALU.mult, [PSB[pb[tb]], Bsig[tb]], [Btm[tb]])
                        if i == 1:
                            tt(PL, accm[tb], accm[tb], tmpm[tb], ALU.add, [Bacc[tb], Btm[tb]], [Bacc[tb]])
                        else:
                            tt(PL, merged[:, n_, sl], accm[tb], tmpm[tb], ALU.add, [Bacc[tb], Btm[tb]], [Bmg])
        tap(f"merged_{l}_{g}", merged, [128, 16, T], BF16, [Bmg])
        dma("sp", xT, xscr.rearrange("(k p) t -> p k t", p=128), [Bxscr], [Bx], "xld")
        for m in range(16):
            wt, wb = wget(("wout_t", l, m))
            pis = proj(wt, wb, merged, Bmg)
            for tb in range(2):
                sl = slice(tb * 512, (tb + 1) * 512)
                stt("dve", xT[:, m, sl], PS[pis[tb]][:, :], modv[:, l, 32 + m:33 + m, g], xT[:, m, sl], ALU.mult, ALU.add,
                    [PSB[pis[tb]], Bmod, Bx], [Bx])
        S.barrier()

    def phase_ffn(l, g):
        RX.reset()
        nseq, Ls = (4, 256) if g == 0 else (1, 1024)
        upad = [RX.alloc([nseq * (Ls + 2)]) for _ in range(2)]
        u3 = [u.rearrange("p (s l) -> p s l", s=nseq) for u in upad]
        Bu = [Buf("fu0"), Buf("fu1")]
        for i in range(2):
            memset(PL, upad[i], 0.0, [Bu[i]])
        t1 = [RX.alloc([nseq, Ls]) for _ in range(2)]
        t2 = [RX.alloc([nseq, Ls]) for _ in range(2)]
        Bt1 = [Buf("ft10"), Buf("ft11")]
        Bt2 = [Buf("ft20"), Buf("ft21")]
        ca = RX.alloc([T])
        cb = RX.alloc([T])
        Bca, Bcb = Buf("ca"), Buf("cb")
        gbuf = ymix
        for qd in range(4):
            for jj in range(11):
                j = qd * 11 + jj
                for ab in range(2):
                    ch = ab * 44 + j
                    wt, wb = wget(("wup_t", l, ch))
                    pis = proj(wt, wb, hT, Bh)
                    evac_pad(u3[ab], Bu[ab], pis, nseq, Ls)
                    wc = [vecs[:, l, V_FCW + tap_ * 88 + ch:V_FCW + tap_ * 88 + ch + 1] for tap_ in range(3)]
                    bc = vecs[:, l, V_FCB + ch:V_FCB + ch + 1]
                    if ab == 0:
                        conv3(u3[0], Bu[0], nseq, Ls, wc, bc, ca.rearrange("p (s l) -> p s l", s=nseq), Bca,
                              t1[0], t2[0], Bt1[0], Bt2[0], final_func=AF.Silu)
                    else:
                        conv3(u3[1], Bu[1], nseq, Ls, wc, bc, cb.rearrange("p (s l) -> p s l", s=nseq), Bcb,
                              t1[1], t2[1], Bt1[1], Bt2[1])
                tt(PL, gbuf[:, jj, :], ca, cb, ALU.mult, [Bca, Bcb], [By])
            for m in range(16):
                wt, wb = wget(("wdn_t", l, qd, m), kct=11)
                pis = proj(wt, wb, gbuf, By, range(11))
                for tb in range(2):
                    sl = slice(tb * 512, (tb + 1) * 512)
                    stt("dve", xT[:, m, sl], PS[pis[tb]][:, :], modv[:, l, 80 + m:81 + m, g], xT[:, m, sl], ALU.mult, ALU.add,
                        [PSB[pis[tb]], Bmod, Bx], [Bx])
        S.barrier()

    for g in groups:
        dma("sp", xT, xT_d[g].rearrange("(k p) t -> p k t", p=128), [], [Bx], "xin")
        for l in layers:
            phase_norm(l, g, 0)
            tap(f"h_{l}_{g}", hT[:, :, :], [128, 16, T], BF16, [Bh])
            dma("sp", xscr.rearrange("(k p) t -> p k t", p=128), xT, [Bx], [Bxscr], "xsp")
            S.barrier()
            if "hy" in phases:
                phase_hyprep(l, 256 if g == 0 else 1024)
            if "attn" in phases:
                phase_attn(l, g)
            if "hy" in phases:
                phase_hyena(l, g)
            if "gdn" in phases:
                phase_gdn(l, g)
            tap(f"ymix_{l}_{g}", ymix[:, :, :], [128, 16, T], BF16, [By])
            if "merge" in phases:
                phase_merge(l, g)
                tap(f"x1_{l}_{g}", xT, [128, 16, T], F32, [Bx])
            if "ffn" in phases:
                phase_norm(l, g, 1)
                phase_ffn(l, g)
                tap(f"x2_{l}_{g}", xT, [128, 16, T], F32, [Bx])
        if cfg.get("final", True):
            phase_final(g)
    if cfg.get("_collect"):
        return wkeys
    S.final_wait("sp")
    S.emit()
    return nc


def build2(cfg=None):
    cfg = dict(cfg or {})
    c1 = dict(cfg)
    c1["_collect"] = True
    plan = build(c1)
    cfg["_wplan"] = plan
    return build(cfg)


_SHARED_KEYS = None


def kernel(**inputs):
    inp = {k: np.asarray(v) for k, v in inputs.items()}
    sh = prep_shared(inp)
    nc = build2()
    in_maps = []
    for core in range(8):
        m = dict(sh)
        m.update(prep_core(inp, core))
        in_maps.append(m)
    res = run_bass_kernel_spmd(nc, in_maps, core_ids=list(range(8))).results
    B, SEQ = 32, 256
    y_p = np.zeros((B, SEQ, D), np.float32)
    y_s = np.zeros((4, 1024, D), np.float32)
    nk = np.zeros((B, NLAYER, SEQ, 8, 128), np.float32)
    nv = np.zeros((B, NLAYER, SEQ, 8, 128), np.float32)
    nsf = np.zeros((B, NLAYER, 4, 128, 128), np.float32)
    nsb = np.zeros((B, NLAYER, 4, 128, 128), np.float32)
    for core in range(8):
        r = res[core]
        yT = np.asarray(r["yT"])
        y_p[4 * core:4 * core + 4] = yT[0].T.reshape(4, SEQ, D)
        if core < 4:
            y_s[core] = yT[1].T
        kT = np.asarray(r["kT_out"])
        vT = np.asarray(r["vT_out"])
        nk[4 * core:4 * core + 4] = kT.transpose(2, 0, 1).reshape(4, SEQ, NLAYER, 8, 128).transpose(0, 2, 1, 3, 4)
        nv[4 * core:4 * core + 4] = vT.transpose(2, 0, 1).reshape(4, SEQ, NLAYER, 8, 128).transpose(0, 2, 1, 3, 4)
        nsf[4 * core:4 * core + 4] = np.asarray(r["sf_out"]).transpose(1, 0, 2, 3, 4)
        nsb[4 * core:4 * core + 4] = np.asarray(r["sb_out"]).transpose(1, 0, 2, 3, 4)
    return (y_p, y_s, nk, nv, nsf, nsb)
```
